# Optimizing a Trainium2 kernel written in Bass

```python
import jax, jax.numpy as jnp
from jax import lax
import numpy as np

D_MODEL = 1024
BATCH = 4
SEQ = 4096
DEPTH = 4

CHUNK = 64
Q_BLOCK = 128
N_MIXERS = 3
EPS = 1e-6

RET_HEADS = 4
RET_DK = D_MODEL
RET_DV = 2 * D_MODEL
RET_HK = RET_DK // RET_HEADS
RET_HV = RET_DV // RET_HEADS
RET_IN = 2 * RET_DK + 2 * RET_DV
ROPE_BASE = 10000.0

FOX_HEADS = 16
FOX_HD = 64
FOX_W = FOX_HEADS * FOX_HD
FOX_IN = 4 * FOX_W + FOX_HEADS
FOX_FORGET_BIAS = 2.0

GLA_HEADS = 4
GLA_DK = D_MODEL // 2
GLA_DV = D_MODEL
GLA_HK = GLA_DK // GLA_HEADS
GLA_HV = GLA_DV // GLA_HEADS
GLA_RANK = 16
GLA_TAU = 16.0
GLA_IN = 2 * GLA_DK + 2 * GLA_DV + GLA_RANK

MAX_POS_OFFSET = 100000

kernel_name = "hybrid_retention_fox_gla_adaln_trunk"


def _rms_norm(x, gain=None):
    xf = x.astype(jnp.float32)
    y = xf * lax.rsqrt(jnp.mean(xf * xf, axis=-1, keepdims=True) + EPS)
    if gain is not None:
        y = y * gain.astype(jnp.float32)
    return y.astype(x.dtype)


def _layer_norm(x):
    xf = x.astype(jnp.float32)
    mu = jnp.mean(xf, axis=-1, keepdims=True)
    var = jnp.mean(jnp.square(xf - mu), axis=-1, keepdims=True)
    return ((xf - mu) * lax.rsqrt(var + EPS)).astype(x.dtype)


def _rope(x, positions):
    half = x.shape[-1] // 2
    inv = ROPE_BASE ** (-jnp.arange(half, dtype=jnp.float32) / half)
    ang = positions.astype(jnp.float32)[..., None, None] * inv
    cos, sin = jnp.cos(ang), jnp.sin(ang)
    x1, x2 = x[..., :half], x[..., half:]
    return jnp.concatenate([x1 * cos - x2 * sin, x1 * sin + x2 * cos], axis=-1)


def _to_chunks(t):
    b, s, h, d = t.shape
    return t.reshape(b, s // CHUNK, CHUNK, h, d).transpose(0, 3, 1, 2, 4)


def _from_chunks(t):
    b, h, n, c, d = t.shape
    return t.transpose(0, 2, 3, 1, 4).reshape(b, n * c, h, d)


def _chunk_state_scan(q_in, k_in, v, state_decay):
    b, h, _, _, dk = q_in.shape
    dv = v.shape[-1]

    def step(state, inp):
        qi, ki, vi, di = inp
        out = jnp.einsum('bhcd,bhde->bhce', qi, state)
        state = di[..., None] * state + jnp.einsum('bhcd,bhce->bhde', ki, vi)
        return state, out

    xs = (jnp.moveaxis(q_in, 2, 0), jnp.moveaxis(k_in, 2, 0), jnp.moveaxis(v, 2, 0),
          jnp.moveaxis(state_decay, 2, 0))
    init = jnp.zeros((b, h, dk, dv), q_in.dtype)
    _, outs = lax.scan(step, init, xs)
    return jnp.moveaxis(outs, 0, 2)


def _retention(h, positions, w_in, w_out):
    b, s, _ = h.shape
    n = s // CHUNK
    q, k, v, g = jnp.split(h @ w_in, [RET_DK, 2 * RET_DK, 2 * RET_DK + RET_DV], axis=-1)
    q = _rope(q.astype(jnp.float32).reshape(b, s, RET_HEADS, RET_HK), positions) * (RET_HK ** -0.5)
    k = _rope(k.astype(jnp.float32).reshape(b, s, RET_HEADS, RET_HK), positions)
    qc, kc = _to_chunks(q), _to_chunks(k)
    vc = _to_chunks(v.astype(jnp.float32).reshape(b, s, RET_HEADS, RET_HV))
    log_gamma = jnp.log1p(-jnp.exp2(-5.0 - jnp.arange(RET_HEADS, dtype=jnp.float32)))
    pos = jnp.arange(CHUNK, dtype=jnp.float32)
    dist = jnp.abs(pos[:, None] - pos[None, :])
    intra_decay = jnp.exp(log_gamma[:, None, None] * dist)
    scores = jnp.einsum('bhncd,bhnmd->bhncm', qc, kc) * intra_decay[:, None]
    o = jnp.einsum('bhncm,bhnme->bhnce', scores, vc)
    q_in = qc * jnp.exp(log_gamma[:, None] * (pos + 1.0))[:, None, :, None]
    k_in = kc * jnp.exp(log_gamma[:, None] * (CHUNK - 1.0 - pos))[:, None, :, None]
    state_decay = jnp.broadcast_to(jnp.exp(log_gamma * CHUNK)[None, :, None, None], (b, RET_HEADS, n, RET_HK))
    o = o + _chunk_state_scan(q_in, k_in, vc, state_decay)
    o = _layer_norm(_from_chunks(o)).reshape(b, s, RET_DV)
    return (o.astype(h.dtype) * jax.nn.silu(g)) @ w_out


def _forgetting_attention(h, w_in, b_f, q_gain, k_gain, w_out):
    b, s, _ = h.shape
    q, k, v, g, f = jnp.split(h @ w_in, [FOX_W, 2 * FOX_W, 3 * FOX_W, 4 * FOX_W], axis=-1)
    q = _rms_norm(q.astype(jnp.float32).reshape(b, s, FOX_HEADS, FOX_HD), q_gain).transpose(0, 2, 1, 3) * (FOX_HD ** -0.5)
    k = _rms_norm(k.astype(jnp.float32).reshape(b, s, FOX_HEADS, FOX_HD), k_gain).transpose(0, 2, 1, 3)
    v = v.astype(jnp.float32).reshape(b, s, FOX_HEADS, FOX_HD).transpose(0, 2, 1, 3)
    log_f = jax.nn.log_sigmoid(f.astype(jnp.float32) + b_f.astype(jnp.float32))
    cum = jnp.cumsum(log_f, axis=1).transpose(0, 2, 1)
    outs = []
    for qs in range(0, s, Q_BLOCK):
        qe = qs + Q_BLOCK
        logits = (jnp.einsum('bhqd,bhkd->bhqk', q[:, :, qs:qe], k[:, :, :qe])
                  + cum[:, :, qs:qe, None] - cum[:, :, None, :qe])
        causal = jnp.arange(qs, qe)[:, None] >= jnp.arange(qe)[None, :]
        p = jax.nn.softmax(jnp.where(causal, logits, -jnp.inf), axis=-1)
        outs.append(jnp.einsum('bhqk,bhkd->bhqd', p, v[:, :, :qe]))
    o = jnp.concatenate(outs, axis=2).transpose(0, 2, 1, 3).reshape(b, s, FOX_W)
    return (o.astype(h.dtype) * jax.nn.silu(g)) @ w_out


def _gated_linear_attention(h, w_in, w_gate2, b_gate, w_out):
    b, s, _ = h.shape
    q, k, v, g, r = jnp.split(h @ w_in, [GLA_DK, 2 * GLA_DK, 2 * GLA_DK + GLA_DV, 2 * GLA_DK + 2 * GLA_DV], axis=-1)
    log_a = jax.nn.log_sigmoid((r @ w_gate2 + b_gate).astype(jnp.float32)) / GLA_TAU
    qc = _to_chunks(q.astype(jnp.float32).reshape(b, s, GLA_HEADS, GLA_HK)) * (GLA_HK ** -0.5)
    kc = _to_chunks(k.astype(jnp.float32).reshape(b, s, GLA_HEADS, GLA_HK))
    vc = _to_chunks(v.astype(jnp.float32).reshape(b, s, GLA_HEADS, GLA_HV))
    cb = jnp.cumsum(_to_chunks(log_a.reshape(b, s, GLA_HEADS, GLA_HK)), axis=3)
    cb_last = cb[:, :, :, -1:, :]
    eb, enb = jnp.exp(cb), jnp.exp(-cb)
    a_causal = jnp.einsum('bhncd,bhnmd->bhncm', qc * eb, kc * enb)
    a_anti = jnp.einsum('bhncd,bhnmd->bhncm', qc * enb, kc * eb)
    idx = jnp.arange(CHUNK)
    attn = jnp.where(idx[:, None] >= idx[None, :], a_causal, a_anti)
    o = jnp.einsum('bhncm,bhnme->bhnce', attn, vc)
    o = o + _chunk_state_scan(qc * eb, kc * jnp.exp(cb_last - cb), vc, jnp.exp(cb_last[:, :, :, 0, :]))
    o = _rms_norm(_from_chunks(o)).reshape(b, s, GLA_DV)
    return (o.astype(h.dtype) * jax.nn.silu(g)) @ w_out


def setup_inputs(seed: int = 0) -> dict:
    key = jax.random.key(seed)
    ks = jax.random.split(key, 20)
    f32 = jnp.float32
    n_a = len(range(0, DEPTH, N_MIXERS))
    n_b = len(range(1, DEPTH, N_MIXERS))
    n_c = len(range(2, DEPTH, N_MIXERS))

    def w(k, shape, fan_in):
        return jax.random.normal(k, shape, f32) * (fan_in ** -0.5)

    x = jax.random.normal(ks[0], (BATCH, SEQ, D_MODEL), f32)
    c = jax.random.normal(ks[1], (BATCH, D_MODEL), f32)
    offs = jax.random.randint(ks[2], (BATCH, 1), 0, MAX_POS_OFFSET, dtype=jnp.int32)
    positions = (offs + jnp.arange(SEQ, dtype=jnp.int32)[None, :]).astype(jnp.int32)
    return {
        "x": x,
        "c": c,
        "positions": positions,
        "mod_w": w(ks[3], (DEPTH, D_MODEL, 3 * D_MODEL), D_MODEL),
        "mod_b": 0.01 * jax.random.normal(ks[4], (DEPTH, 3 * D_MODEL), f32),
        "norm_g": 1.0 + 0.01 * jax.random.normal(ks[5], (DEPTH, D_MODEL), f32),
        "ret_w_in": w(ks[6], (n_a, D_MODEL, RET_IN), D_MODEL),
        "ret_w_out": w(ks[7], (n_a, RET_DV, D_MODEL), RET_DV),
        "fox_w_in": w(ks[8], (n_b, D_MODEL, FOX_IN), D_MODEL),
        "fox_b_f": FOX_FORGET_BIAS + 0.1 * jax.random.normal(ks[9], (n_b, FOX_HEADS), f32),
        "fox_q_gain": 1.0 + 0.01 * jax.random.normal(ks[10], (n_b, FOX_HD), f32),
        "fox_k_gain": 1.0 + 0.01 * jax.random.normal(ks[11], (n_b, FOX_HD), f32),
        "fox_w_out": w(ks[12], (n_b, FOX_W, D_MODEL), FOX_W),
        "gla_w_in": w(ks[13], (n_c, D_MODEL, GLA_IN), D_MODEL),
        "gla_w_gate2": w(ks[14], (n_c, GLA_RANK, GLA_DK), GLA_RANK),
        "gla_b_gate": 0.01 * jax.random.normal(ks[15], (n_c, GLA_DK), f32),
        "gla_w_out": w(ks[16], (n_c, GLA_DV, D_MODEL), GLA_DV),
        "final_g": 1.0 + 0.01 * jax.random.normal(ks[17], (D_MODEL,), f32),
    }


def reference(x, c, positions, mod_w, mod_b, norm_g, ret_w_in, ret_w_out, fox_w_in, fox_b_f,
              fox_q_gain, fox_k_gain, fox_w_out, gla_w_in, gla_w_gate2, gla_b_gate, gla_w_out, final_g):
    c_act = jax.nn.silu(c)
    for i in range(DEPTH):
        shift, scale, gate = jnp.split(c_act @ mod_w[i] + mod_b[i], 3, axis=-1)
        h = _rms_norm(x, norm_g[i]) * (1.0 + scale[:, None, :]) + shift[:, None, :]
        j = i // N_MIXERS
        kind = i % N_MIXERS
        if kind == 0:
            y = _retention(h, positions, ret_w_in[j], ret_w_out[j])
        elif kind == 1:
            y = _forgetting_attention(h, fox_w_in[j], fox_b_f[j], fox_q_gain[j], fox_k_gain[j], fox_w_out[j])
        else:
            y = _gated_linear_attention(h, gla_w_in[j], gla_w_gate2[j], gla_b_gate[j], gla_w_out[j])
        x = x + gate[:, None, :] * y
    return _rms_norm(x, final_g)
```

```python
import contextlib
import numpy as np
import concourse.bass as bass
import concourse.mybir as mybir
from concourse.bass_utils import run_bass_kernel_spmd

F32 = mybir.dt.float32
BF16 = mybir.dt.bfloat16
I32 = mybir.dt.int32
AF = mybir.ActivationFunctionType
ALU = mybir.AluOpType
AX = mybir.AxisListType

D = 1024
EPS = 1e-6
N_CORES = 8


class Buf:
    __slots__ = ("name", "w", "r", "dsem", "dcnt", "osem", "ocnt", "psum")

    def __init__(self, name):
        self.name = name
        self.psum = False
        self.w = None
        self.r = {}
        self.dsem = None
        self.dcnt = 0
        self.osem = None
        self.ocnt = 0


class Ctx:
    def __init__(self, nc):
        self.nc = nc
        self.eng = {"pe": nc.tensor, "dve": nc.vector, "act": nc.scalar,
                    "pool": nc.gpsimd, "sp": nc.sync}
        self.sem = {k: nc.alloc_semaphore("sem_" + k) for k in self.eng}
        self.cnt = {k: 0 for k in self.eng}
        self.seen = {k: {} for k in self.eng}
        self.nsem = 0
        self.ninst = 0
        self.dma_ev = {}
        self.log = {k: [] for k in self.eng}
        self.semname = {id(v): "sem_" + k for k, v in self.sem.items()}

    def newsem(self, name):
        self.nsem += 1
        sm = self.nc.alloc_semaphore(f"{name}_{self.nsem}")
        self.semname[id(sm)] = f"{name}_{self.nsem}"
        self._keep = getattr(self, "_keep", []) + [sm]
        return sm

    def _wait(self, e, ev):
        sem, val = ev
        key = id(sem)
        if e == "pe" and sem is self.sem["pe"]:
            return
        if self.seen[e].get(key, 0) >= val:
            return
        self.eng[e].wait_ge(sem, val)
        self.log[e].append(("W", self.semname[key], val))
        self.seen[e][key] = val
        self.ninst += 1

    def _deps(self, e, r, w):
        for b in r:
            if b.w is not None:
                self._wait(e, b.w)
            if b.psum:
                for k, ev in b.r.items():
                    if k != e:
                        self._wait(e, ev)
        for b in w:
            if b.w is not None:
                self._wait(e, b.w)
            for ev in b.r.values():
                self._wait(e, ev)

    def op(self, e, fn, r=(), w=()):
        self._deps(e, r, w)
        ins = fn(self.eng[e])
        self.cnt[e] += 1
        ins.then_inc(self.sem[e], 1)
        self.log[e].append(("I", "sem_" + e, 1))
        ev = (self.sem[e], self.cnt[e])
        for b in r:
            b.r[e] = ev
        for b in w:
            b.w = ev
            b.r = {}
        self.ninst += 1
        return ins

    def mm(self, fns, r=(), w=()):
        self._deps("pe", r, w)
        ins = None
        for fn in fns:
            ins = fn(self.eng["pe"])
            self.ninst += 1
        self.cnt["pe"] += 1
        ins.then_inc(self.sem["pe"], 1)
        self.log["pe"].append(("I", "sem_pe", 1))
        ev = (self.sem["pe"], self.cnt["pe"])
        for b in r:
            b.r["pe"] = ev
        for b in w:
            b.w = ev
            b.r = {}

    def dma_in(self, q, out_ap, in_ap, w, cont=False):
        if not cont:
            self._deps(q, (), (w,))
        if w.dsem is None:
            w.dsem = self.newsem("d")
        self.eng[q].dma_start(out=out_ap, in_=in_ap).then_inc(w.dsem, 16)
        self.log[q].append(("I", self.semname[id(w.dsem)], 16))
        w.dcnt += 16
        self.dma_ev[id(w.dsem)] = (w.dsem, w.dcnt)
        w.w = (w.dsem, w.dcnt)
        w.r = {}
        self.ninst += 1

    def dma_out(self, q, out_ap, in_ap, r):
        self._deps(q, (r,), ())
        if r.osem is None:
            r.osem = self.newsem("o")
        self.eng[q].dma_start(out=out_ap, in_=in_ap).then_inc(r.osem, 16)
        self.log[q].append(("I", self.semname[id(r.osem)], 16))
        r.ocnt += 16
        self.dma_ev[id(r.osem)] = (r.osem, r.ocnt)
        r.r["dmaout"] = (r.osem, r.ocnt)
        self.ninst += 1

    def barrier(self):
        evs = [(self.sem[k], self.cnt[k]) for k in self.eng if self.cnt[k] > 0]
        evs += list(self.dma_ev.values())
        for e in self.eng:
            for ev in evs:
                self._wait(e, ev)

    def flush_out(self, q, bufs):
        for b in bufs:
            if b.osem is not None and b.ocnt > 0:
                self._wait(q, (b.osem, b.ocnt))


class T:
    def __init__(self, nc, name, shape, dt, psum=False, stack=None):
        if psum:
            self.t = nc.alloc_psum_tensor(name, list(shape), dt)
        elif stack is not None:
            self.t = stack.enter_context(nc.sbuf_tensor(name, list(shape), dt))
        else:
            self.t = nc.alloc_sbuf_tensor(name, list(shape), dt)
        self.b = Buf(name)
        self.b.psum = psum
        self.shape = shape

    def __getitem__(self, idx):
        return self.t[idx]


def _cw_consts():
    two_pi = 2.0 * np.pi
    c1 = 6.28125
    r = two_pi - c1
    c2 = float(np.float32(np.round(r * 2 ** 19) / 2 ** 19))
    c3 = float(np.float32(two_pi - c1 - c2))
    return c1, c2, c3


def _consts():
    cst = {}
    cst["ident"] = np.eye(128, dtype=np.float32)
    half = 128
    inv = (np.float32(10000.0) ** (-(np.arange(half, dtype=np.float32) / np.float32(half)))).astype(np.float32)
    cst["inv"] = inv.reshape(128, 1)
    lg = np.log1p(-np.exp2(-5.0 - np.arange(4, dtype=np.float64)))
    t = np.arange(128)
    m_ = t[:, None]
    c_ = t[None, :]
    same = (m_ // 64) == (c_ // 64)
    maskT = np.zeros((4, 128, 128), np.float64)
    for h in range(4):
        d1 = np.exp(lg[h] * np.abs(c_ - m_))
        d2 = np.exp(lg[h] * (c_ - m_))
        maskT[h] = np.where(same, d1, np.where(c_ > m_, d2, 0.0)) * (256.0 ** -0.5)
    cst["ret_maskT"] = np.ascontiguousarray(maskT.transpose(1, 0, 2)).astype(np.float32)
    gq = np.stack([np.exp(lg[h] * (t + 1.0)) * (256.0 ** -0.5) for h in range(4)], 0)
    cst["ret_gq"] = np.broadcast_to(gq[None], (128, 4, 128)).astype(np.float32).copy()
    gk = np.stack([np.exp(lg[h] * (127.0 - t)) for h in range(4)], 1)
    cst["ret_gk"] = gk.astype(np.float32)
    cst["ret_sdec"] = np.exp(lg * 128.0).astype(np.float64)
    cst["gla_tri"] = (np.where(m_ <= c_, 1.0, 0.0) * (-1.0 / 16.0)).astype(np.float32)
    gm = np.zeros((128, 2, 128), np.float32)
    gm[:, 0, :] = np.where(c_ >= m_, 1.0, 0.0)
    gm[:, 1, :] = np.where((m_ > c_) & same, 1.0, 0.0)
    cst["gla_mask"] = gm
    same32 = (m_ // 32) == (c_ // 32)
    cst["fox_sfx"] = np.where((m_ > c_) & same32, 1.0, 0.0).astype(np.float32)
    ci = np.zeros((128, 8, 70), np.float32)
    for c in range(4):
        ci[:, c, :] = (t < (c + 1) * 32)[:, None]
        ci[:, 4 + c, :] = (t < c * 32)[:, None]
    cst["fox_cumind"] = ci.reshape(128, 560)
    cst["fox_negmask"] = np.where(m_ > c_, -30000.0, 0.0).astype(np.float32)
    sel = np.zeros((70, 8), np.float32)
    sel[64, 0] = 1; sel[65, 1] = 1; sel[66, 2] = 1; sel[67:70, 3] = 1
    sel[67, 4] = -1; sel[68, 5] = -1; sel[69, 6] = -1; sel[64:67, 7] = 1
    cst["fox_sel"] = sel
    return cst


class Prog:
    def __init__(self, S, layers, final=True):
        self.S = S
        self.NB = S // 128
        self.layers = layers
        self.final = final
        self.cst = _consts()
        import os
        self.dbg_stage = int(os.environ.get('DBG_STAGE', '99'))

    def build(self):
        S, NB = self.S, self.NB
        nc = bass.Bass("TRN2", target_bir_lowering=False)
        self.nc = nc
        cx = Ctx(nc)
        self.cx = cx

        def din(name, shape, dt=F32):
            return nc.dram_tensor(name, list(shape), dt, kind="ExternalInput").ap()

        self.x_in = din("x", [S, D])
        self.c_col = din("c_col", [128, 8])
        self.posf = din("posf", [1, S])
        self.mod_w = din("mod_w", [4, D, 3 * D])
        self.mod_b_col = din("mod_b_col", [128, 4, 24])
        self.norm_g_col = din("norm_g_col", [128, 4, 8])
        self.final_g_col = din("final_g_col", [128, 8])
        self.ret_w_in = din("ret_w_in", [2, D, 6144])
        self.ret_w_out = din("ret_w_out", [2, 2048, D])
        self.fox_w_in = din("fox_w_in", [1, D, 4112])
        self.fox_b_f = din("fox_b_f", [1, 16])
        self.fox_qg_col = din("fox_qg_col", [64, 1])
        self.fox_kg_col = din("fox_kg_col", [64, 1])
        self.fox_w_out = din("fox_w_out", [1, D, D])
        self.gla_w_in = din("gla_w_in", [1, D, 3088])
        self.gla_w_gate2 = din("gla_w_gate2", [1, 16, 512])
        self.gla_b_gate = din("gla_b_gate", [1, 512])
        self.gla_w_out = din("gla_w_out", [1, D, D])
        self.dconst = {k: din("k_" + k, v.shape) for k, v in self.cst.items()
                       if isinstance(v, np.ndarray) and v.dtype == np.float32}
        self.out = nc.dram_tensor("out", [S, D], F32, kind="ExternalOutput").ap()
        import os
        self.dbg_on = os.environ.get("DBG_DUMP", "") == "1"
        self.dbg_map = {}
        self.dbg_off = 0
        if self.dbg_on:
            self.dbg = nc.dram_tensor("dbg", [128, 8192], F32, kind="ExternalOutput").ap()
        self.xs = [nc.dram_tensor(f"xs{i}", [S, D], F32, kind="Internal").ap() for i in range(3)]
        self.rope_d = nc.dram_tensor("rope_d", [2, 128, S], F32, kind="Internal").ap()

        self._alloc_common()
        with contextlib.ExitStack() as stk:
            self.pstack = stk
            self._startup()
            cx.barrier()
        src = self.x_in
        prev = None
        NP = {0: 2, 1: 4, 2: 2}
        total = sum(NP[l % 3] for l in self.layers)
        pi = 0
        for l in self.layers:
            kind = l % 3
            self._layer_mod(l)
            for p in range(NP[kind]):
                pi += 1
                is_final = (pi == total) and self.final
                if is_final:
                    dst = self.out
                elif pi == total:
                    dst = self.out
                else:
                    dst = [b for b in self.xs if b is not src and b is not prev][0]
                with contextlib.ExitStack() as stk:
                    self.pstack = stk
                    cx.flush_out("sp", self.all_out_bufs)
                    cx.flush_out("pool", self.all_out_bufs)
                    fn = (self._ret_pass, self._fox_pass, self._gla_pass)[kind]
                    fn(l // 3, p, src, prev, dst, is_final)
                    cx.barrier()
                prev = dst
            src = prev
            prev = None
        cx.flush_out("sp", self.all_out_bufs)
        return nc

    def dump(self, name, ap, buf, P, n):
        if not self.dbg_on or name in self.dbg_map:
            return
        cx = self.cx
        st = self.dbg_st[len(self.dbg_map)]
        cx.op("dve", lambda e: e.tensor_copy(st[0:P, 0:n], ap), r=[buf], w=[st.b])
        cx.dma_out("sp", self.dbg[0:P, self.dbg_off:self.dbg_off + n], st[0:P, 0:n], st.b)
        self.all_out_bufs.append(st.b)
        self.dbg_map[name] = (self.dbg_off, P, n)
        self.dbg_off += n

    def mk(self, name, shape, dt=F32):
        self._uid = getattr(self, "_uid", 0) + 1
        return T(self.nc, f"{name}_{self._uid}", shape, dt, stack=self.pstack)

    def _alloc_common(self):
        nc, cx = self.nc, self.cx
        S = self.S
        mk = lambda name, shape, dt=F32: T(nc, name, shape, dt)
        self.ident_f = mk("ident_f", [128, 128])
        self.ident_b = mk("ident_b", [128, 128], BF16)
        self.ones_f = mk("ones_f", [128, 128])
        self.ones_b = mk("ones_b", [128, 128], BF16)
        self.inv = mk("inv", [128, 1])
        self.modcol = mk("modcol", [128, 4, 24])
        self.ngcol = mk("ngcol", [128, 4, 8])
        self.fgcol = mk("fgcol", [128, 8])
        self.G_bc = mk("G_bc", [128, D])
        self.shift_bc = mk("shift_bc", [128, D])
        self.gate_bc = mk("gate_bc", [128, D])
        self.fg_bc = mk("fg_bc", [128, D])
        self.xt = [mk(f"xt{i}", [128, D]) for i in range(2)]
        self.xp = [mk(f"xp{i}", [128, D]) for i in range(2)]
        self.xo = [mk(f"xo{i}", [128, D]) for i in range(2)]
        self.tmpA = mk("tmpA", [128, D])
        self.hb = mk("hb", [128, D], BF16)
        self.hT = mk("hT", [128, 8, 128], BF16)
        self.st = mk("stat", [128, 8])
        self.rs_i = mk("rs_i", [128, 16], I32)
        self.rs_u = mk("rs_u", [128, 16])
        self.halfpi = mk("halfpi", [128, 1])
        self.junk = mk("junk", [128, D], BF16)
        self.all_out_bufs = [t.b for t in self.xo]
        if self.dbg_on:
            self.dbg_st = [mk(f"dbgst{i}", [128, 512]) for i in range(8)]
        self.pT = [T(nc, f"pT{i}", [128, 1024], BF16, psum=True) for i in range(2)]
        self.pF = [T(nc, f"pF{i}", [128, 512], F32, psum=True) for i in range(4)]
        self.pacc = [T(nc, f"pacc{i}", [128, 512], F32, psum=True) for i in range(2)]
        self._pti = 0
        self.one_c = mk("one_c", [128, 1])
        self.eps_c = mk("eps_c", [128, 1])
        self._pf_i = 0
        self._pt_i = 0

    def pf(self):
        t = self.pF[self._pf_i % len(self.pF)]
        self._pf_i += 1
        return t

    def pt(self):
        t = self.pT[self._pt_i % 2]
        self._pt_i += 1
        return t

    def _startup(self):
        nc, cx = self.nc, self.cx
        S = self.S
        dc = self.dconst
        cx.dma_in("sp", self.ident_f[:], dc["ident"][:, :], self.ident_f.b)
        cx.dma_in("sp", self.inv[:], dc["inv"][:, :], self.inv.b)
        cx.dma_in("sp", self.modcol[:], self.mod_b_col[:, :, :], self.modcol.b)
        cx.dma_in("sp", self.ngcol[:], self.norm_g_col[:, :, :], self.ngcol.b)
        cx.dma_in("sp", self.fgcol[:], self.final_g_col[:, :], self.fgcol.b)
        cx.op("dve", lambda e: e.tensor_copy(self.ident_b[:], self.ident_f[:]),
              r=[self.ident_f.b], w=[self.ident_b.b])
        cx.op("pool", lambda e: e.memset(self.ones_f[:], 1.0), w=[self.ones_f.b])
        cx.op("pool", lambda e: e.memset(self.ones_b[:], 1.0), w=[self.ones_b.b])
        cx.op("pool", lambda e: e.memset(self.halfpi[:], float(np.pi / 2)), w=[self.halfpi.b])
        cx.op("pool", lambda e: e.memset(self.one_c[:], 1.0), w=[self.one_c.b])
        cx.op("pool", lambda e: e.memset(self.eps_c[:], EPS), w=[self.eps_c.b])
        ccol = self.mk("ccol", [128, 8])
        cacol = self.mk("cacol", [128, 8])
        cx.dma_in("sp", ccol[:], self.c_col[:, :], ccol.b)
        cx.op("act", lambda e: e.activation(out=cacol[:], in_=ccol[:], func=AF.Silu),
              r=[ccol.b], w=[cacol.b])
        mw = [self.mk(f"mw{i}", [128, 8, 512]) for i in range(2)]
        k = 0
        for l in self.layers:
            pm = self.pf()
            first = True
            for p in range(6):
                buf = mw[k % 2]
                k += 1
                src = self.mod_w[l].rearrange("(kc p) n -> p kc n", p=128)[:, :, p * 512:(p + 1) * 512]
                cx.dma_in("sp", buf[:], src, buf.b)
                fns = []
                for j in range(4):
                    col = p * 4 + j
                    for kc in range(8):
                        fns.append(lambda e, buf=buf, j=j, kc=kc, col=col: e.matmul(
                            pm[:, col:col + 1], lhsT=buf[:, kc, j * 128:(j + 1) * 128],
                            rhs=cacol[:, kc:kc + 1], start=(kc == 0), stop=(kc == 7)))
                cx.mm(fns, r=[buf.b, cacol.b], w=[pm.b])
            cx.op("dve", lambda e, pm=pm, l=l: e.tensor_tensor(
                out=self.modcol[:, l, :], in0=pm[:, 0:24], in1=self.modcol[:, l, :], op=ALU.add),
                r=[pm.b, self.modcol.b], w=[self.modcol.b])
            cx.op("dve", lambda e, l=l: e.scalar_tensor_tensor(
                out=self.modcol[:, l, 8:16], in0=self.modcol[:, l, 8:16], scalar=1.0,
                in1=self.ngcol[:, l, :], op0=ALU.add, op1=ALU.mult),
                r=[self.modcol.b, self.ngcol.b], w=[self.modcol.b])
        if any(l % 3 == 0 for l in self.layers):
            self._rope_tables()
        if self.final:
            self._bcast_cols(self.fgcol, None, [(self.fg_bc, 0)])

    def _rope_tables(self):
        nc, cx = self.nc, self.cx
        S = self.S
        c1, c2, c3 = _cw_consts()
        ang = self.mk("ang", [128, S])
        kf = self.mk("kf", [128, S])
        ki = self.mk("ki", [128, S], I32)
        sinT = self.mk("sinT", [128, S])
        cosT = self.mk("cosT", [128, S])
        cx.dma_in("sp", ang[:], self.posf[0:1, :].partition_broadcast(128), ang.b)
        cx.op("dve", lambda e: e.tensor_scalar(ang[:], ang[:], self.inv[:, 0:1], None, ALU.mult),
              r=[ang.b, self.inv.b], w=[ang.b])
        cx.op("dve", lambda e: e.tensor_scalar(kf[:], ang[:], float(1.0 / (2 * np.pi)), None, ALU.mult),
              r=[ang.b], w=[kf.b])
        cx.op("dve", lambda e: e.tensor_copy(ki[:], kf[:]), r=[kf.b], w=[ki.b])
        cx.op("dve", lambda e: e.tensor_copy(kf[:], ki[:]), r=[ki.b], w=[kf.b])
        for cc in (c1, c2, c3):
            cx.op("dve", lambda e, cc=cc: e.scalar_tensor_tensor(
                out=ang[:], in0=kf[:], scalar=-float(cc), in1=ang[:], op0=ALU.mult, op1=ALU.add),
                r=[kf.b, ang.b], w=[ang.b])
        pi = float(np.pi)
        cx.op("dve", lambda e: e.tensor_scalar(ang[:], ang[:], pi, -pi, ALU.min, ALU.max),
              r=[ang.b], w=[ang.b])
        cx.op("act", lambda e: e.activation(out=sinT[:], in_=ang[:], func=AF.Sin),
              r=[ang.b], w=[sinT.b])
        cx.op("act", lambda e: e.activation(out=kf[:], in_=ang[:], func=AF.Abs),
              r=[ang.b], w=[kf.b])
        cx.op("act", lambda e: e.activation(out=cosT[:], in_=kf[:], func=AF.Sin, scale=-1.0,
                                            bias=self.halfpi[:, 0:1]),
              r=[kf.b, self.halfpi.b], w=[cosT.b])
        cx.dma_out("sp", self.rope_d[0], cosT[:], cosT.b)
        cx.dma_out("sp", self.rope_d[1], sinT[:], sinT.b)
        cx.flush_out("sp", [cosT.b, sinT.b])

    def _wrap(self, a, tmp):
        cx = self.cx
        pi = float(np.pi)
        cx.op("dve", lambda e: e.tensor_scalar(tmp[:], a[:], pi, -2.0 * pi, ALU.is_gt, ALU.mult),
              r=[a.b], w=[tmp.b])
        cx.op("dve", lambda e: e.tensor_tensor(out=a[:], in0=a[:], in1=tmp[:], op=ALU.add),
              r=[a.b, tmp.b], w=[a.b])
        cx.op("dve", lambda e: e.tensor_scalar(tmp[:], a[:], -pi, 2.0 * pi, ALU.is_lt, ALU.mult),
              r=[a.b], w=[tmp.b])
        cx.op("dve", lambda e: e.tensor_tensor(out=a[:], in0=a[:], in1=tmp[:], op=ALU.add),
              r=[a.b, tmp.b], w=[a.b])
        cx.op("dve", lambda e: e.tensor_scalar(a[:], a[:], pi, -pi, ALU.min, ALU.max),
              r=[a.b], w=[a.b])

    def _bcast_cols(self, colT, l, outs):
        cx = self.cx
        for dst, off in outs:
            rep = self.tmpA
            src = colT[:, off:off + 8] if l is None else colT[:, l, off:off + 8]
            cx.op("dve", lambda e, src=src: e.tensor_copy(
                rep[:].rearrange("p (c m) -> p c m", c=8), src.unsqueeze(2).to_broadcast([128, 8, 128])),
                r=[colT.b], w=[rep.b])
            for hf in range(2):
                pm = self.pf()
                fns = []
                for c in range(4):
                    cc = hf * 4 + c
                    fns.append(lambda e, cc=cc, c=c, pm=pm: e.matmul(
                        pm[:, c * 128:(c + 1) * 128], lhsT=rep[:, cc * 128:(cc + 1) * 128],
                        rhs=self.ident_f[:], start=True, stop=True))
                cx.mm(fns, r=[rep.b, self.ident_f.b], w=[pm.b])
                cx.op("act", lambda e, pm=pm, hf=hf, dst=dst: e.copy(dst[:, hf * 512:(hf + 1) * 512], pm[:]),
                      r=[pm.b], w=[dst.b])

    def _layer_mod(self, l):
        self._bcast_cols(self.modcol, l, [(self.shift_bc, 0), (self.G_bc, 8), (self.gate_bc, 16)])

    def _load_x(self, t, src, prev, half):
        cx = self.cx
        xt = self.xt[t % 2]
        cx.dma_in("sp", xt[:], src[t * 128:(t + 1) * 128, :], xt.b)
        if prev is not None:
            xp = self.xp[t % 2]
            cx.dma_in("sp", xp[:], prev[t * 128:(t + 1) * 128, :], xp.b)

    def _split(self, src, sb, hi, hib, lo, lob):
        cx = self.cx
        cx.op("dve", lambda e: e.tensor_copy(hi, src), r=[sb], w=[hib])
        cx.op("dve", lambda e: e.tensor_tensor(out=lo, in0=src, in1=hi, op=ALU.subtract), r=[sb, hib], w=[lob])

    def _rsqrt(self, dst, a, n, rbufs, wbufs):
        cx = self.cx
        yi = self.rs_i[:, 0:n]
        y = yi.bitcast(F32)
        u = self.rs_u[:, 0:n]
        sb = self.rs_i.b
        cx.op("dve", lambda e: e.tensor_scalar(yi, a.bitcast(I32), 1, None, ALU.arith_shift_right),
              r=rbufs, w=[sb])
        cx.op("dve", lambda e: e.tensor_scalar(yi, yi, -1.0, float(0x5f3759df), ALU.mult, ALU.add),
              r=[sb], w=[sb])
        for it in range(3):
            cx.op("dve", lambda e: e.scalar_tensor_tensor(out=u, in0=y, scalar=-0.5, in1=y,
                                                          op0=ALU.mult, op1=ALU.mult), r=[sb], w=[self.rs_u.b])
            cx.op("dve", lambda e: e.tensor_tensor(out=u, in0=u, in1=a, op=ALU.mult),
                  r=[self.rs_u.b] + list(rbufs), w=[self.rs_u.b])
            if it < 2:
                cx.op("dve", lambda e: e.scalar_tensor_tensor(out=y, in0=u, scalar=1.5, in1=y,
                                                              op0=ALU.add, op1=ALU.mult),
                      r=[self.rs_u.b, sb], w=[sb])
            else:
                cx.op("dve", lambda e: e.scalar_tensor_tensor(out=dst, in0=u, scalar=1.5, in1=y,
                                                              op0=ALU.add, op1=ALU.mult),
                      r=[self.rs_u.b, sb], w=wbufs)

    def _rstd(self, xt, col):
        cx = self.cx
        st = self.st
        cx.op("act", lambda e: e.activation(out=self.junk[:], in_=xt[:], func=AF.Square,
                                            accum_out=st[:, col:col + 1]),
              r=[xt.b], w=[self.junk.b, st.b])
        cx.op("dve", lambda e: e.tensor_scalar(st[:, col:col + 1], st[:, col:col + 1], 1.0 / D, EPS,
                                               ALU.mult, ALU.add), r=[st.b], w=[st.b])
        self._rsqrt(st[:, col + 2:col + 3], st[:, col:col + 1], 1, [st.b], [st.b])

    def _prologue(self, t):
        cx = self.cx
        xt = self.xt[t % 2]
        self._rstd(xt, 0)
        cx.op("dve", lambda e: e.scalar_tensor_tensor(
            out=self.tmpA[:], in0=xt[:], scalar=self.st[:, 2:3], in1=self.G_bc[:],
            op0=ALU.mult, op1=ALU.mult), r=[xt.b, self.st.b, self.G_bc.b], w=[self.tmpA.b])
        cx.op("pool", lambda e: e.tensor_tensor(out=self.hb[:], in0=self.tmpA[:], in1=self.shift_bc[:],
                                                op=ALU.add),
              r=[self.tmpA.b, self.shift_bc.b], w=[self.hb.b])
        pt = self.pt()
        cx.mm([lambda e, c=c: e.transpose(pt[:, c * 128:(c + 1) * 128], self.hb[:, c * 128:(c + 1) * 128],
                                          self.ident_b[:]) for c in range(8)],
              r=[self.hb.b, self.ident_b.b], w=[pt.b])
        cx.op("act", lambda e: e.copy(self.hT[:].rearrange("p c t -> p (c t)"), pt[:]),
              r=[pt.b], w=[self.hT.b])

    def _epilogue(self, t, half, dst, is_final, ogT, nchunk, wout, kparts=128, tsl=None, xprev=None):
        cx = self.cx
        if xprev is None:
            xprev = self.xt[t % 2] if half is None else self.xp[t % 2]
        if tsl is None:
            tsl = slice(0, 128)
        xo = self.xo[t % 2]
        for hf in range(2):
            py = self.pf()
            cx.mm([lambda e, c=c, py=py, hf=hf: e.matmul(
                py[:], lhsT=ogT[0:kparts, c, tsl], rhs=wout[0:kparts, c, hf * 512:(hf + 1) * 512],
                start=(c == 0), stop=(c == nchunk - 1)) for c in range(nchunk)],
                r=[ogT.b, wout.b], w=[py.b])
            cx.op("dve", lambda e, py=py, hf=hf: e.tensor_tensor(
                out=self.tmpA[:, hf * 512:(hf + 1) * 512], in0=py[:],
                in1=self.gate_bc[:, hf * 512:(hf + 1) * 512], op=ALU.mult),
                r=[py.b, self.gate_bc.b], w=[self.tmpA.b])
        cx.op("pool", lambda e: e.tensor_tensor(out=xo[:], in0=self.tmpA[:], in1=xprev[:], op=ALU.add),
              r=[self.tmpA.b, xprev.b], w=[xo.b])
        if is_final:
            self._rstd(xo, 1)
            cx.op("dve", lambda e: e.scalar_tensor_tensor(
                out=xo[:], in0=xo[:], scalar=self.st[:, 3:4], in1=self.fg_bc[:],
                op0=ALU.mult, op1=ALU.mult), r=[xo.b, self.st.b, self.fg_bc.b], w=[xo.b])
        cx.dma_out("sp", dst[t * 128:(t + 1) * 128, :], xo[:], xo.b)

    def _ret_alloc(self):
        nc = self.nc
        mk = self.mk
        self.r_cs = [mk(f"r_cs{i}", [128, 2, 128]) for i in range(2)]
        self.rw_in = mk("rw_in", [128, 8, 3072], BF16)
        self.rw_out = mk("rw_out", [128, 8, 1024], BF16)
        self.r_maskT = mk("r_maskT", [128, 4, 128])
        self.r_gq = mk("r_gq", [128, 4, 128])
        self.r_gk = mk("r_gk", [128, 4])
        self.r_S32 = mk("r_S32", [128, 4, 512])
        self.r_Sb = mk("r_Sb", [128, 4, 512], BF16)
        self.r_qkT = mk("r_qkT", [128, 8, 128], BF16)
        self.r_qin = mk("r_qin", [128, 4, 128], BF16)
        self.r_t1 = mk("r_t1", [128, 4, 128])
        self.r_t2 = mk("r_t2", [128, 4, 128])
        self.r_kin = mk("r_kin", [128, 512], BF16)
        self.r_v = mk("r_v", [128, 1024], BF16)
        self.r_sg = mk("r_sg", [128, 1024], BF16)
        self.r_P = mk("r_P", [128, 2, 128], BF16)
        self.r_on = mk("r_on", [128, 1024], BF16)
        self.r_og = mk("r_og", [128, 1024], BF16)
        self.r_ogT = mk("r_ogT", [128, 8, 128], BF16)
        self.r_bn = mk("r_bn", [128, 2, 6])
        self.r_mv = mk("r_mv", [128, 2, 2])
        self.r_nb = mk("r_nb", [128, 2])
        self.r_rs = mk("r_rs", [128, 2])
        cx = self.cx
        dc = self.dconst
        cx.dma_in("sp", self.r_maskT[:], dc["ret_maskT"][:, :, :], self.r_maskT.b)
        cx.dma_in("sp", self.r_gq[:], dc["ret_gq"][:, :, :], self.r_gq.b)
        cx.dma_in("sp", self.r_gk[:], dc["ret_gk"][:, :], self.r_gk.b)

    def _ret_pass(self, j, half, src, prev, dst, is_final):
        nc, cx = self.nc, self.cx
        self._ret_alloc()
        NB = self.NB
        h0 = 2 * half
        win = self.ret_w_in[j].rearrange("(kc p) n -> p kc n", p=128)
        first = True
        for (dst0, src0, n) in ((0, h0 * 256, 512), (512, 1024 + h0 * 256, 512),
                                (1024, 2048 + h0 * 512, 1024), (2048, 4096 + h0 * 512, 1024)):
            for kc in range(8):
                cx.dma_in("pool", self.rw_in[:, kc, dst0:dst0 + n], win[:, kc, src0:src0 + n],
                          self.rw_in.b, cont=not first)
                first = False
        wout = self.ret_w_out[j].rearrange("(ec p) n -> p ec n", p=128)
        for ec in range(8):
            cx.dma_in("pool", self.rw_out[:, ec, :], wout[:, h0 * 4 + ec, :], self.rw_out.b, cont=(ec > 0))
        cx.op("pool", lambda e: e.memset(self.r_S32[:], 0.0), w=[self.r_S32.b])
        cx.op("pool", lambda e: e.memset(self.r_Sb[:], 0.0), w=[self.r_Sb.b])
        sdec = [float(self.cst["ret_sdec"][h0 + i]) for i in range(2)]

        self._load_x(0, src, prev, half)
        for t in range(NB):
            if t + 1 < NB:
                self._load_x(t + 1, src, prev, half)
            self._prologue(t)
            hT = self.hT
            W = self.rw_in
            pq = [self.pf(), self.pf()]
            for qk in range(2):
                fns = []
                for c in range(4):
                    col = qk * 512 + c * 128
                    for kc in range(8):
                        fns.append(lambda e, qk=qk, c=c, col=col, kc=kc: e.matmul(
                            pq[qk][:, c * 128:(c + 1) * 128], lhsT=W[:, kc, col:col + 128],
                            rhs=hT[:, kc, :], start=(kc == 0), stop=(kc == 7)))
                cx.mm(fns, r=[W.b, hT.b], w=[pq[qk].b])
            rcs = self.r_cs[t % 2]
            cx.dma_in("sp", rcs[:], self.rope_d[:, :, t * 128:(t + 1) * 128].rearrange("c p t -> p c t"), rcs.b)
            cs = rcs[:, 0:1, :].to_broadcast([128, 2, 128])
            sn = rcs[:, 1:2, :].to_broadcast([128, 2, 128])
            for qk in range(2):
                pv = pq[qk][:].rearrange("p (h d t) -> p h d t", h=2, d=2)
                x1 = pv[:, :, 0, :]
                x2 = pv[:, :, 1, :]
                t1 = self.r_t1[:, qk * 2:(qk + 1) * 2, :]
                t2 = self.r_t2[:, qk * 2:(qk + 1) * 2, :]
                ov = self.r_qkT[:, qk * 4:(qk + 1) * 4, :].rearrange("p (h d) t -> p h d t", d=2)
                cx.op("dve", lambda e, x1=x1, t1=t1: e.tensor_tensor(out=t1, in0=x1, in1=cs, op=ALU.mult),
                      r=[pq[qk].b, rcs.b], w=[self.r_t1.b])
                cx.op("dve", lambda e, x2=x2, t2=t2: e.tensor_tensor(out=t2, in0=x2, in1=sn, op=ALU.mult),
                      r=[pq[qk].b, rcs.b], w=[self.r_t2.b])
                cx.op("pool", lambda e, t1=t1, t2=t2, ov=ov: e.tensor_tensor(
                    out=ov[:, :, 0, :], in0=t1, in1=t2, op=ALU.subtract),
                    r=[self.r_t1.b, self.r_t2.b], w=[self.r_qkT.b])
                cx.op("dve", lambda e, x1=x1, t1=t1: e.tensor_tensor(out=t1, in0=x1, in1=sn, op=ALU.mult),
                      r=[pq[qk].b, rcs.b], w=[self.r_t1.b])
                cx.op("dve", lambda e, x2=x2, t2=t2: e.tensor_tensor(out=t2, in0=x2, in1=cs, op=ALU.mult),
                      r=[pq[qk].b, rcs.b], w=[self.r_t2.b])
                cx.op("pool", lambda e, t1=t1, t2=t2, ov=ov: e.tensor_tensor(
                    out=ov[:, :, 1, :], in0=t1, in1=t2, op=ALU.add),
                    r=[self.r_t1.b, self.r_t2.b], w=[self.r_qkT.b])
            qkT = self.r_qkT
            gq = self.r_gq[:, h0:h0 + 2, :].unsqueeze(2).to_broadcast([128, 2, 2, 128])
            cx.op("dve", lambda e: e.tensor_tensor(
                out=self.r_qin[:].rearrange("p (h d) t -> p h d t", d=2),
                in0=qkT[:, 0:4, :].rearrange("p (h d) t -> p h d t", d=2), in1=gq, op=ALU.mult),
                r=[qkT.b, self.r_gq.b], w=[self.r_qin.b])
            ps = self.pf()
            fns = []
            for h in range(2):
                for d in range(2):
                    fns.append(lambda e, h=h, d=d: e.matmul(
                        ps[:, h * 128:(h + 1) * 128], lhsT=qkT[:, 4 + h * 2 + d, :], rhs=qkT[:, h * 2 + d, :],
                        start=(d == 0), stop=(d == 1)))
            cx.mm(fns, r=[qkT.b], w=[ps.b])
            cx.op("dve", lambda e: e.tensor_tensor(
                out=self.r_P[:], in0=ps[:, 0:256].rearrange("p (h c) -> p h c", h=2),
                in1=self.r_maskT[:, h0:h0 + 2, :], op=ALU.mult),
                r=[ps.b, self.r_maskT.b], w=[self.r_P.b])
            pk = self.pt()
            cx.mm([lambda e, c=c: e.transpose(pk[:, c * 128:(c + 1) * 128], qkT[:, 4 + c, :], self.ident_b[:])
                   for c in range(4)], r=[qkT.b, self.ident_b.b], w=[pk.b])
            for h in range(2):
                cx.op("act", lambda e, h=h: e.activation(
                    out=self.r_kin[:, h * 256:(h + 1) * 256], in_=pk[:, h * 256:(h + 1) * 256],
                    func=AF.Copy, scale=self.r_gk[:, h0 + h:h0 + h + 1]),
                    r=[pk.b, self.r_gk.b], w=[self.r_kin.b])
            for vi in range(2):
                pvv = self.pf()
                cx.mm([lambda e, kc=kc, vi=vi, pvv=pvv: e.matmul(
                    pvv[:], lhsT=hT[:, kc, :], rhs=W[:, kc, 1024 + vi * 512:1024 + (vi + 1) * 512],
                    start=(kc == 0), stop=(kc == 7)) for kc in range(8)], r=[W.b, hT.b], w=[pvv.b])
                cx.op("act", lambda e, vi=vi, pvv=pvv: e.copy(self.r_v[:, vi * 512:(vi + 1) * 512], pvv[:]),
                      r=[pvv.b], w=[self.r_v.b])
            for gi in range(2):
                pg = self.pf()
                cx.mm([lambda e, kc=kc, gi=gi, pg=pg: e.matmul(
                    pg[:], lhsT=hT[:, kc, :], rhs=W[:, kc, 2048 + gi * 512:2048 + (gi + 1) * 512],
                    start=(kc == 0), stop=(kc == 7)) for kc in range(8)], r=[W.b, hT.b], w=[pg.b])
                cx.op("act", lambda e, gi=gi, pg=pg: e.activation(
                    out=self.r_sg[:, gi * 512:(gi + 1) * 512], in_=pg[:], func=AF.Silu),
                    r=[pg.b], w=[self.r_sg.b])
            pos = []
            for h in range(2):
                po = self.pf()
                pos.append(po)
                fns = [lambda e, h=h, po=po: e.matmul(po[:], lhsT=self.r_P[:, h, :],
                                                      rhs=self.r_v[:, h * 512:(h + 1) * 512],
                                                      start=True, stop=False)]
                for d in range(2):
                    fns.append(lambda e, h=h, d=d, po=po: e.matmul(
                        po[:], lhsT=self.r_qin[:, h * 2 + d, :], rhs=self.r_Sb[:, h * 2 + d, :],
                        start=False, stop=(d == 1)))
                cx.mm(fns, r=[self.r_P.b, self.r_v.b, self.r_qin.b, self.r_Sb.b], w=[po.b])
                cx.op("dve", lambda e, h=h, po=po: e.bn_stats(self.r_bn[:, h, :], po[:]),
                      r=[po.b], w=[self.r_bn.b])
                cx.op("dve", lambda e, h=h: e.bn_aggr(self.r_mv[:, h, :], self.r_bn[:, h, :]),
                      r=[self.r_bn.b], w=[self.r_mv.b])
            cx.op("dve", lambda e: e.tensor_scalar(self.r_mv[:, :, 1], self.r_mv[:, :, 1], EPS, None, ALU.add),
                  r=[self.r_mv.b], w=[self.r_mv.b])
            self._rsqrt(self.r_rs[:, 0:2], self.r_mv[:, :, 1], 2, [self.r_mv.b], [self.r_rs.b])
            cx.op("dve", lambda e: e.scalar_tensor_tensor(
                out=self.r_nb[:, 0:2], in0=self.r_mv[:, :, 0], scalar=-1.0,
                in1=self.r_rs[:, 0:2], op0=ALU.mult, op1=ALU.mult),
                r=[self.r_mv.b, self.r_rs.b], w=[self.r_nb.b])
            for h in range(2):
                po = pos[h]
                cx.op("act", lambda e, h=h, po=po: e.activation(
                    out=self.r_on[:, h * 512:(h + 1) * 512], in_=po[:], func=AF.Identity,
                    bias=self.r_nb[:, h:h + 1], scale=self.r_rs[:, h:h + 1]),
                    r=[po.b, self.r_nb.b, self.r_rs.b], w=[self.r_on.b])
            cx.op("pool", lambda e: e.tensor_tensor(out=self.r_og[:], in0=self.r_on[:], in1=self.r_sg[:],
                                                    op=ALU.mult),
                  r=[self.r_on.b, self.r_sg.b], w=[self.r_og.b])
            for h in range(2):
                for d in range(2):
                    pd = self.pf()
                    cx.mm([lambda e, h=h, d=d, pd=pd: e.matmul(
                        pd[:], lhsT=self.r_kin[:, h * 256 + d * 128:h * 256 + (d + 1) * 128],
                        rhs=self.r_v[:, h * 512:(h + 1) * 512], start=True, stop=True)],
                        r=[self.r_kin.b, self.r_v.b], w=[pd.b])
                    cx.op("dve", lambda e, h=h, d=d, pd=pd: e.scalar_tensor_tensor(
                        out=self.r_S32[:, h * 2 + d, :], in0=self.r_S32[:, h * 2 + d, :], scalar=sdec[h],
                        in1=pd[:], op0=ALU.mult, op1=ALU.add),
                        r=[pd.b, self.r_S32.b], w=[self.r_S32.b])
            cx.op("pool", lambda e: e.tensor_copy(self.r_Sb[:], self.r_S32[:]),
                  r=[self.r_S32.b], w=[self.r_Sb.b])
            pt = self.pt()
            cx.mm([lambda e, c=c: e.transpose(pt[:, c * 128:(c + 1) * 128], self.r_og[:, c * 128:(c + 1) * 128],
                                              self.ident_b[:]) for c in range(8)],
                  r=[self.r_og.b, self.ident_b.b], w=[pt.b])
            cx.op("act", lambda e: e.copy(self.r_ogT[:].rearrange("p c t -> p (c t)"), pt[:]),
                  r=[pt.b], w=[self.r_ogT.b])
            self._epilogue(t, prev, dst, is_final, self.r_ogT, 8, self.rw_out)

    def _fox_pass(self, j, p, src, prev, dst, is_final):
        nc, cx = self.nc, self.cx
        mk = self.mk
        NB, S = self.NB, self.S
        HP = 4
        hb = HP * p
        dc = self.dconst
        NSB = NB // 4
        w_in = mk("fw_in", [128, 8, 4 * HP * 64 + HP], BF16)
        w_out = mk("fw_out", [64, HP, 1024], BF16)
        bfb = mk("fbf", [128, HP])
        qg = mk("fqg", [64, 1])
        kg = mk("fkg", [64, 1])
        sfxm = mk("fsfx", [128, 128])
        negm = mk("fnegm", [128, 128], BF16)
        negm_f = mk("fnegmf", [128, 128])
        sel = mk("fsel", [70, 8])
        kTa = mk("fkTa", [70, HP, S], BF16)
        qTa = mk("fqTa", [70, HP, 512], BF16)
        Vp = mk("fVp", [128, NB, HP, 65], BF16)
        sgT = mk("fsgT", [64, HP, 512], BF16)
        ogT = mk("fogT", [64, HP, 512], BF16)
        sq = mk("fsq", [64, 2, HP * 128])
        sqh = mk("fsqh", [64, 2, HP * 128], BF16)
        sql = mk("fsql", [64, 2, HP * 128], BF16)
        lafh = mk("flafh", [128, HP], BF16)
        lafl = mk("flafl", [128, HP], BF16)
        sfxb = mk("fsfxb", [128, 128], BF16)
        rdh = mk("frdh", [65, 512], BF16)
        rdl = mk("frdl", [65, 512], BF16)
        rs = mk("frs", [64, 2, HP * 128])
        zf = mk("fzf", [128, HP])
        laf = mk("flaf", [128, HP])
        wk = mk("fwk", [128, HP])
        Rp = mk("fRp", [70, HP])
        R2 = mk("fR2", [70, 2, 4, HP])
        pcs = [mk(f"fpc{i}", [70, 2, 4, HP]) for i in range(3)]
        pcb = mk("fpcb", [70, 2, 4, HP], BF16)
        dd = mk("fdd", [70, 2, 4, HP])
        selk = mk("fselk", [70, 4, HP])
        selq = mk("fselq", [70, 4, HP])
        cumind = mk("fcumind", [128, 8, 70], BF16)
        Pt = [mk(f"fPt{i}", [128, 512], BF16) for i in range(3)]
        osb = mk("fosb", [65, 512])
        onr = mk("fonr", [64, 512])
        nq = HP * 64

        win = self.fox_w_in[j].rearrange("(kc p) n -> p kc n", p=128)
        first = True
        for gi in range(4):
            for kc in range(8):
                cx.dma_in("pool", w_in[:, kc, gi * nq:(gi + 1) * nq],
                          win[:, kc, gi * 1024 + hb * 64:gi * 1024 + hb * 64 + nq], w_in.b, cont=not first)
                first = False
        for kc in range(8):
            cx.dma_in("pool", w_in[:, kc, 4 * nq:4 * nq + HP], win[:, kc, 4096 + hb:4096 + hb + HP], w_in.b, cont=True)
        wout = self.fox_w_out[j].rearrange("(h d) n -> d h n", d=64)
        for h in range(HP):
            cx.dma_in("pool", w_out[:, h, :], wout[:, hb + h, :], w_out.b, cont=(h > 0))
        cx.dma_in("sp", bfb[:], self.fox_b_f[j:j + 1, hb:hb + HP].partition_broadcast(128), bfb.b)
        cx.dma_in("sp", qg[:], self.fox_qg_col[:, :], qg.b)
        cx.dma_in("sp", kg[:], self.fox_kg_col[:, :], kg.b)
        cx.dma_in("sp", sfxm[:], dc["fox_sfx"][:, :], sfxm.b)
        cx.dma_in("sp", negm_f[:], dc["fox_negmask"][:, :], negm_f.b)
        cx.dma_in("sp", sel[:], dc["fox_sel"][:, :], sel.b)
        cx.dma_in("pool", cumind[:].rearrange("p c m -> p (c m)"), dc["fox_cumind"][:, :], cumind.b)
        cx.op("dve", lambda e: e.tensor_copy(negm[:], negm_f[:]), r=[negm_f.b], w=[negm.b])
        cx.op("dve", lambda e: e.tensor_copy(sfxb[:], sfxm[:]), r=[sfxm.b], w=[sfxb.b])
        cx.op("dve", lambda e: e.tensor_scalar(qg[:], qg[:], 0.125, None, ALU.mult), r=[qg.b], w=[qg.b])
        cx.op("pool", lambda e: e.memset(Rp[:], 0.0), w=[Rp.b])

        for J in range(NSB):
            self._load_x(4 * J, src, None, None)
            for tb in range(4):
                t = 4 * J + tb
                if tb < 3:
                    self._load_x(t + 1, src, None, None)
                self._prologue(t)
                hT = self.hT
                for qi, (off, gcol, dstT, tok0) in enumerate(((0, qg, qTa, tb * 128), (nq, kg, kTa, t * 128))):
                    pp = self.pf()
                    fns = []
                    for h in range(HP):
                        for kc in range(8):
                            fns.append(lambda e, pp=pp, off=off, h=h, kc=kc: e.matmul(
                                pp[0:64, h * 128:(h + 1) * 128], lhsT=w_in[:, kc, off + h * 64:off + (h + 1) * 64],
                                rhs=hT[:, kc, :], start=(kc == 0), stop=(kc == 7)))
                    cx.mm(fns, r=[w_in.b, hT.b], w=[pp.b])
                    cx.op("act", lambda e, qi=qi, pp=pp: e.activation(out=sq[:, qi, :], in_=pp[0:64, :],
                                                                      func=AF.Square), r=[pp.b], w=[sq.b])
                    self._split(sq[:, qi, :], sq.b, sqh[:, qi, :], sqh.b, sql[:, qi, :], sql.b)
                    ps_ = self.pf()
                    cx.mm([lambda e, qi=qi, ps_=ps_: e.matmul(ps_[0:64, :], lhsT=self.ones_b[0:64, 0:64],
                                                            rhs=sqh[:, qi, :], start=True, stop=False),
                           lambda e, qi=qi, ps_=ps_: e.matmul(ps_[0:64, :], lhsT=self.ones_b[0:64, 0:64],
                                                            rhs=sql[:, qi, :], start=False, stop=True)],
                          r=[sqh.b, sql.b, self.ones_b.b], w=[ps_.b])
                    cx.op("dve", lambda e, qi=qi, ps_=ps_: e.tensor_scalar(
                        rs[:, qi, :], ps_[0:64, :], 1.0 / 64.0, EPS, ALU.mult, ALU.add), r=[ps_.b], w=[rs.b])
                    cx.op("act", lambda e, qi=qi: e.activation(out=rs[:, qi, :], in_=rs[:, qi, :], func=AF.Ln),
                          r=[rs.b], w=[rs.b])
                    cx.op("act", lambda e, qi=qi: e.activation(out=rs[:, qi, :], in_=rs[:, qi, :], func=AF.Exp,
                                                               scale=-0.5), r=[rs.b], w=[rs.b])
                    cx.op("dve", lambda e, qi=qi, pp=pp, gcol=gcol, dstT=dstT, tok0=tok0: e.scalar_tensor_tensor(
                        out=dstT[0:64, :, tok0:tok0 + 128], in0=pp[0:64, :].rearrange("p (h t) -> p h t", h=HP),
                        scalar=gcol[:, 0:1], in1=rs[:, qi, :].rearrange("p (h t) -> p h t", h=HP),
                        op0=ALU.mult, op1=ALU.mult), r=[pp.b, gcol.b, rs.b], w=[dstT.b])
                pg = self.pf()
                fns = []
                for h in range(HP):
                    for kc in range(8):
                        fns.append(lambda e, h=h, kc=kc: e.matmul(
                            pg[0:64, h * 128:(h + 1) * 128], lhsT=w_in[:, kc, 3 * nq + h * 64:3 * nq + (h + 1) * 64],
                            rhs=hT[:, kc, :], start=(kc == 0), stop=(kc == 7)))
                cx.mm(fns, r=[w_in.b, hT.b], w=[pg.b])
                cx.op("act", lambda e: e.activation(out=sgT[:, :, tb * 128:(tb + 1) * 128],
                                                    in_=pg[0:64, :].rearrange("p (h t) -> p h t", h=HP),
                                                    func=AF.Silu), r=[pg.b], w=[sgT.b])
                pv = self.pf()
                cx.mm([lambda e, kc=kc: e.matmul(pv[:, 0:nq], lhsT=hT[:, kc, :], rhs=w_in[:, kc, 2 * nq:3 * nq],
                                                 start=(kc == 0), stop=(kc == 7)) for kc in range(8)],
                      r=[w_in.b, hT.b], w=[pv.b])
                pff = self.pf()
                cx.mm([lambda e, kc=kc: e.matmul(pff[:, 0:HP], lhsT=hT[:, kc, :], rhs=w_in[:, kc, 4 * nq:4 * nq + HP],
                                                 start=(kc == 0), stop=(kc == 7)) for kc in range(8)],
                      r=[w_in.b, hT.b], w=[pff.b])
                cx.op("dve", lambda e: e.tensor_tensor(out=zf[:], in0=pff[:, 0:HP], in1=bfb[:], op=ALU.add),
                      r=[pff.b, bfb.b], w=[zf.b])
                cx.op("act", lambda e: e.activation(out=zf[:], in_=zf[:], func=AF.Exp, scale=-1.0),
                      r=[zf.b], w=[zf.b])
                cx.op("dve", lambda e: e.tensor_scalar(zf[:], zf[:], 1.0, None, ALU.add), r=[zf.b], w=[zf.b])
                cx.op("act", lambda e: e.activation(out=laf[:], in_=zf[:], func=AF.Ln), r=[zf.b], w=[laf.b])
                psx = self.pf()
                self._split(laf[:], laf.b, lafh[:], lafh.b, lafl[:], lafl.b)
                fns = [lambda e: e.matmul(psx[:, 0:HP], lhsT=sfxb[:], rhs=lafh[:], start=True, stop=False),
                       lambda e: e.matmul(psx[:, 0:HP], lhsT=sfxb[:], rhs=lafl[:], start=False, stop=True)]
                for c8 in range(8):
                    o0 = 8 + c8 * HP
                    fns.append(lambda e, c8=c8, o0=o0: e.matmul(psx[0:70, o0:o0 + HP], lhsT=cumind[:, c8, :],
                                                                rhs=lafh[:], start=True, stop=False))
                    fns.append(lambda e, c8=c8, o0=o0: e.matmul(psx[0:70, o0:o0 + HP], lhsT=cumind[:, c8, :],
                                                                rhs=lafl[:], start=False, stop=True))
                cx.mm(fns, r=[sfxb.b, lafh.b, lafl.b, cumind.b], w=[psx.b])
                cx.op("act", lambda e: e.activation(out=wk[:], in_=psx[:, 0:HP], func=AF.Exp, scale=-1.0),
                      r=[psx.b], w=[wk.b])
                cx.op("dve", lambda e: e.tensor_tensor(
                    out=Vp[:, t, :, 0:64], in0=pv[:, 0:nq].rearrange("p (h d) -> p h d", h=HP),
                    in1=wk[:].unsqueeze(2).to_broadcast([128, HP, 64]), op=ALU.mult),
                    r=[pv.b, wk.b], w=[Vp.b])
                cx.op("dve", lambda e: e.tensor_copy(Vp[:, t, :, 64], wk[:]), r=[wk.b], w=[Vp.b])
                cx.op("dve", lambda e: e.tensor_tensor(
                    out=R2[:].rearrange("p s c h -> p (s c) h"),
                    in0=psx[0:70, 8:8 + 8 * HP].rearrange("p (c h) -> p c h", h=HP),
                    in1=Rp[:].unsqueeze(1).to_broadcast([70, 8, HP]), op=ALU.add),
                    r=[psx.b, Rp.b], w=[R2.b])
                cx.op("dve", lambda e: e.tensor_copy(Rp[:], R2[:, 0, 3, :]), r=[R2.b], w=[Rp.b])
                cur = R2
                for i in range(3):
                    cx.op("dve", lambda e, cur=cur: e.tensor_copy(pcb[:], cur[:]), r=[cur.b], w=[pcb.b])
                    cx.op("dve", lambda e, i=i: e.tensor_copy(pcs[i][:], pcb[:]), r=[pcb.b], w=[pcs[i].b])
                    if i < 2:
                        cx.op("dve", lambda e, cur=cur, i=i: e.tensor_tensor(out=dd[:], in0=cur[:], in1=pcs[i][:],
                                                                             op=ALU.subtract),
                              r=[cur.b, pcs[i].b], w=[dd.b])
                        cur = dd
                self._fox_sel(selk, pcs, 0, sel, 0)
                self._fox_sel(selq, pcs, 1, sel, 4)
                cx.op("dve", lambda e: e.tensor_copy(
                    kTa[64:70, :, t * 128:(t + 1) * 128].rearrange("p h (c u) -> p h c u", u=32),
                    selk[64:70, :, :].rearrange("p c h -> p h c").unsqueeze(3).to_broadcast([6, HP, 4, 32])),
                    r=[selk.b], w=[kTa.b])
                cx.op("dve", lambda e: e.tensor_copy(
                    qTa[64:70, :, tb * 128:(tb + 1) * 128].rearrange("p h (c u) -> p h c u", u=32),
                    selq[64:70, :, :].rearrange("p c h -> p h c").unsqueeze(3).to_broadcast([6, HP, 4, 32])),
                    r=[selq.b], w=[qTa.b])
            if J == 0 and p == 0:
                self.dump("qTa", qTa[0:70, 0, :], qTa.b, 70, 512)
                self.dump("kTa", kTa[0:70, 0, 0:512], kTa.b, 70, 512)
                self.dump("Vp", Vp[:, 0:4, 0, :], Vp.b, 128, 260)
                self.dump("sgT", sgT[:, 0, :], sgT.b, 64, 512)
            nkb = 4 * J + 4
            for h in range(HP):
                acc = self.pacc[h % 2]
                for i in range(nkb):
                    a = i - 4 * J
                    c0 = max(a, 0) * 128
                    ncol = 512 - c0
                    ps = self.pf()
                    fns = [lambda e, ps=ps, i=i, c0=c0, ncol=ncol, a=a: e.matmul(
                        ps[:, 0:ncol], lhsT=kTa[0:70, h, i * 128:(i + 1) * 128], rhs=qTa[0:70, h, c0:512],
                        start=True, stop=(a < 0))]
                    if a >= 0:
                        fns.append(lambda e, ps=ps: e.matmul(ps[:, 0:128], lhsT=self.ident_b[:], rhs=negm[:],
                                                             start=False, stop=True))
                    cx.mm(fns, r=[kTa.b, qTa.b, self.ident_b.b, negm.b], w=[ps.b])
                    pt_ = Pt[self._pti % 3]
                    self._pti += 1
                    cx.op("act", lambda e, ps=ps, pt_=pt_, ncol=ncol: e.activation(
                        out=pt_[:, 0:ncol], in_=ps[:, 0:ncol], func=AF.Exp), r=[ps.b], w=[pt_.b])
                    cx.mm([lambda e, pt_=pt_, i=i, c0=c0, ncol=ncol: e.matmul(
                        acc[0:65, c0:512], lhsT=Vp[:, i, h, :], rhs=pt_[:, 0:ncol],
                        start=(i == 0), stop=(i == nkb - 1))], r=[Vp.b, pt_.b], w=[acc.b])
                cx.op("act", lambda e, acc=acc: e.copy(osb[:], acc[0:65, :]), r=[acc.b], w=[osb.b])
                if J == 0 and p == 0 and h == 0:
                    self.dump("osb", osb[:, :], osb.b, 65, 512)
                cx.op("dve", lambda e: e.reciprocal(osb[64:65, :], osb[64:65, :]), r=[osb.b], w=[osb.b])
                pbc = self.pf()
                self._split(osb[64:65, :], osb.b, rdh[64:65, :], rdh.b, rdl[64:65, :], rdl.b)
                cx.mm([lambda e: e.matmul(pbc[0:64, :], lhsT=self.ones_b[64:65, 0:64], rhs=rdh[64:65, :],
                                          start=True, stop=False),
                       lambda e: e.matmul(pbc[0:64, :], lhsT=self.ones_b[64:65, 0:64], rhs=rdl[64:65, :],
                                          start=False, stop=True)], r=[rdh.b, rdl.b, self.ones_b.b], w=[pbc.b])
                cx.op("dve", lambda e: e.tensor_tensor(out=onr[:], in0=osb[0:64, :], in1=pbc[0:64, :], op=ALU.mult),
                      r=[osb.b, pbc.b], w=[onr.b])
                if J == 0 and p == 0 and h == 0:
                    self.dump("onr", onr[:, :], onr.b, 64, 512)
                cx.op("pool", lambda e, h=h: e.tensor_tensor(out=ogT[:, h, :], in0=onr[:], in1=sgT[:, h, :],
                                                             op=ALU.mult), r=[onr.b, sgT.b], w=[ogT.b])
            for tb in range(4):
                t = 4 * J + tb
                xq = self.xp[t % 2]
                cx.dma_in("sp", xq[:], (src if prev is None else prev)[t * 128:(t + 1) * 128, :], xq.b)
                self._epilogue(t, prev, dst, is_final, ogT, HP, w_out, kparts=64,
                               tsl=slice(tb * 128, (tb + 1) * 128), xprev=xq)

    def _fox_sel(self, out, pcs, side, sel, c0):
        cx = self.cx
        cx.op("dve", lambda e: e.tensor_scalar(out[:], pcs[0][:, side, :, :], sel[:, c0:c0 + 1], sel[:, c0 + 3:c0 + 4],
                                               ALU.mult, ALU.add), r=[pcs[0].b, sel.b], w=[out.b])
        for i in (1, 2):
            cx.op("dve", lambda e, i=i: e.scalar_tensor_tensor(
                out=out[:], in0=pcs[i][:, side, :, :], scalar=sel[:, c0 + i:c0 + i + 1],
                in1=out[:], op0=ALU.mult, op1=ALU.add),
                r=[pcs[i].b, sel.b, out.b], w=[out.b])

    def _gla_pass(self, j, half, src, prev, dst, is_final):
        nc, cx = self.nc, self.cx
        mk = self.mk
        NB = self.NB
        h0 = 2 * half
        dc = self.dconst
        w_in = mk("gw_in", [128, 8, 1552], BF16)
        w_out = mk("gw_out", [128, 4, 1024], BF16)
        w2 = mk("gw2", [128, 256], BF16)
        bg = mk("gbg", [128, 256])
        tri = mk("gtri", [128, 128])
        trib = mk("gtrib", [128, 128], BF16)
        lah = mk("glah", [128, 256], BF16)
        lal = mk("glal", [128, 256], BF16)
        gmask = mk("gmask", [128, 2, 128])
        S32 = mk("gS32", [128, 2, 256])
        Sb = mk("gSb", [128, 2, 256], BF16)
        rT = mk("grT", [128, 128], BF16)
        e1 = mk("ge1", [128, 256])
        la = mk("gla", [128, 256])
        ebT = mk("gebT", [128, 2, 128])
        enbT = mk("genbT", [128, 2, 128])
        kuf = mk("gkuf", [128, 2, 128])
        cbl = mk("gcbl", [128, 2])
        dec = mk("gdec", [128, 2])
        qeT = mk("gqeT", [128, 2, 128], BF16)
        qnT = mk("gqnT", [128, 2, 128], BF16)
        keT = mk("gkeT", [128, 2, 128], BF16)
        kpT = mk("gkpT", [128, 2, 128], BF16)
        kuT = mk("gkuT", [128, 2, 128], BF16)
        ku = mk("gku", [128, 256], BF16)
        pm = mk("gpm", [128, 2, 2, 128])
        P = mk("gP", [128, 2, 128], BF16)
        v = mk("gv", [128, 512], BF16)
        sg = mk("gsg", [128, 512], BF16)
        ssq = mk("gssq", [128, 4])
        on = mk("gon", [128, 512], BF16)
        og = mk("gog", [128, 512], BF16)
        ogT = mk("gogT", [128, 4, 128], BF16)
        scale = 128.0 ** -0.5

        win = self.gla_w_in[j].rearrange("(kc p) n -> p kc n", p=128)
        first = True
        for (d0, s0, n) in ((0, h0 * 128, 256), (256, 512 + h0 * 128, 256), (512, 1024 + h0 * 256, 512),
                            (1024, 2048 + h0 * 256, 512), (1536, 3072, 16)):
            for kc in range(8):
                cx.dma_in("pool", w_in[:, kc, d0:d0 + n], win[:, kc, s0:s0 + n], w_in.b, cont=not first)
                first = False
        wout = self.gla_w_out[j].rearrange("(ec p) n -> p ec n", p=128)
        for ec in range(4):
            cx.dma_in("pool", w_out[:, ec, :], wout[:, h0 * 2 + ec, :], w_out.b, cont=(ec > 0))
        cx.op("pool", lambda e: e.memset(w2[:], 0.0), w=[w2.b])
        cx.op("pool", lambda e: e.memset(rT[:], 0.0), w=[rT.b])
        cx.dma_in("pool", w2[0:16, :], self.gla_w_gate2[j][:, h0 * 128:h0 * 128 + 256], w2.b)
        cx.dma_in("sp", bg[:], self.gla_b_gate[j:j + 1, h0 * 128:h0 * 128 + 256].partition_broadcast(128), bg.b)
        cx.dma_in("sp", tri[:], dc["gla_tri"][:, :], tri.b)
        cx.op("dve", lambda e: e.tensor_copy(trib[:], tri[:]), r=[tri.b], w=[trib.b])
        cx.dma_in("sp", gmask[:], dc["gla_mask"][:, :, :], gmask.b)
        cx.op("pool", lambda e: e.memset(S32[:], 0.0), w=[S32.b])
        cx.op("pool", lambda e: e.memset(Sb[:], 0.0), w=[Sb.b])

        self._load_x(0, src, prev, half)
        for t in range(NB):
            if t + 1 < NB:
                self._load_x(t + 1, src, prev, half)
            self._prologue(t)
            hT = self.hT
            for _once in (0,):
                pqk = self.pf()
                fns = []
                for c in range(4):
                    for kc in range(8):
                        fns.append(lambda e, c=c, kc=kc: e.matmul(
                            pqk[:, c * 128:(c + 1) * 128], lhsT=w_in[:, kc, c * 128:(c + 1) * 128],
                            rhs=hT[:, kc, :], start=(kc == 0), stop=(kc == 7)))
                cx.mm(fns, r=[w_in.b, hT.b], w=[pqk.b])
                if self.dbg_stage < 1:
                    break
                pr = self.pf()
                cx.mm([lambda e, kc=kc: e.matmul(pr[0:16, 0:128], lhsT=w_in[:, kc, 1536:1552], rhs=hT[:, kc, :],
                                                 start=(kc == 0), stop=(kc == 7)) for kc in range(8)],
                      r=[w_in.b, hT.b], w=[pr.b])
                cx.op("dve", lambda e: e.tensor_copy(rT[0:16, :], pr[0:16, 0:128]), r=[pr.b], w=[rT.b])
                if self.dbg_stage < 2:
                    break
                pz = self.pf()
                import os as _os
                if _os.environ.get("DBG_VAR", "") == "A":
                    cx.mm([lambda e: e.matmul(pz[:, 0:256], lhsT=hT[:, 0, :], rhs=w_in[:, 0, 0:256], start=True, stop=True)],
                          r=[hT.b, w_in.b], w=[pz.b])
                else:
                    cx.mm([lambda e: e.matmul(pz[:, 0:256], lhsT=rT[:], rhs=w2[:], start=True, stop=True)],
                          r=[rT.b, w2.b], w=[pz.b])
                if _os.environ.get("DBG_VAR", "") == "B":
                    cx.op("dve", lambda e: e.tensor_tensor(out=e1[:], in0=pz[:, 0:256], in1=self.G_bc[:, 0:256], op=ALU.add),
                          r=[pz.b, self.G_bc.b], w=[e1.b])
                else:
                    cx.op("dve", lambda e: e.tensor_tensor(out=e1[:], in0=pz[:, 0:256], in1=bg[:], op=ALU.add),
                          r=[pz.b, bg.b], w=[e1.b])
                cx.op("act", lambda e: e.activation(out=e1[:], in_=e1[:], func=AF.Exp, scale=-1.0),
                      r=[e1.b], w=[e1.b])
                cx.op("dve", lambda e: e.tensor_scalar(e1[:], e1[:], 1.0, None, ALU.add), r=[e1.b], w=[e1.b])
                cx.op("act", lambda e: e.activation(out=la[:], in_=e1[:], func=AF.Ln), r=[e1.b], w=[la.b])
                if self.dbg_stage < 3:
                    break
                pcb = self.pf()
                self._split(la[:], la.b, lah[:], lah.b, lal[:], lal.b)
                fns = []
                for h in range(2):
                    fns.append(lambda e, h=h: e.matmul(pcb[:, h * 128:(h + 1) * 128], lhsT=lah[:, h * 128:(h + 1) * 128],
                                                       rhs=trib[:], start=True, stop=False))
                    fns.append(lambda e, h=h: e.matmul(pcb[:, h * 128:(h + 1) * 128], lhsT=lal[:, h * 128:(h + 1) * 128],
                                                       rhs=trib[:], start=False, stop=True))
                cx.mm(fns, r=[lah.b, lal.b, trib.b], w=[pcb.b])
                pcv = pcb[:, 0:256].rearrange("p (h t) -> p h t", h=2)
                cx.op("act", lambda e: e.activation(out=ebT[:], in_=pcv, func=AF.Exp), r=[pcb.b], w=[ebT.b])
                cx.op("act", lambda e: e.activation(out=enbT[:], in_=pcv, func=AF.Exp, scale=-1.0),
                      r=[pcb.b], w=[enbT.b])
                cx.op("dve", lambda e: e.tensor_copy(cbl[:], pcv[:, :, 127]), r=[pcb.b], w=[cbl.b])
                cx.op("act", lambda e: e.activation(out=dec[:], in_=cbl[:], func=AF.Exp), r=[cbl.b], w=[dec.b])
                for h in range(2):
                    cx.op("act", lambda e, h=h: e.activation(out=kuf[:, h, :], in_=pcv[:, h, :], func=AF.Exp,
                                                             scale=-1.0, bias=cbl[:, h:h + 1]),
                          r=[pcb.b, cbl.b], w=[kuf.b])
                if self.dbg_stage < 4:
                    break
                qv = pqk[:, 0:256].rearrange("p (h t) -> p h t", h=2)
                kv = pqk[:, 256:512].rearrange("p (h t) -> p h t", h=2)
                cx.op("dve", lambda e: e.scalar_tensor_tensor(out=qeT[:], in0=qv, scalar=scale, in1=ebT[:],
                                                              op0=ALU.mult, op1=ALU.mult),
                      r=[pqk.b, ebT.b], w=[qeT.b])
                cx.op("dve", lambda e: e.scalar_tensor_tensor(out=qnT[:], in0=qv, scalar=scale, in1=enbT[:],
                                                              op0=ALU.mult, op1=ALU.mult),
                      r=[pqk.b, enbT.b], w=[qnT.b])
                cx.op("dve", lambda e: e.tensor_tensor(out=keT[:], in0=kv, in1=enbT[:], op=ALU.mult),
                      r=[pqk.b, enbT.b], w=[keT.b])
                cx.op("dve", lambda e: e.tensor_tensor(out=kpT[:], in0=kv, in1=ebT[:], op=ALU.mult),
                      r=[pqk.b, ebT.b], w=[kpT.b])
                cx.op("dve", lambda e: e.tensor_tensor(out=kuT[:], in0=kv, in1=kuf[:], op=ALU.mult),
                      r=[pqk.b, kuf.b], w=[kuT.b])
                if self.dbg_stage < 5:
                    break
                pa = self.pf()
                fns = []
                for h in range(2):
                    fns.append(lambda e, h=h: e.matmul(pa[:, h * 128:(h + 1) * 128], lhsT=keT[:, h, :],
                                                       rhs=qeT[:, h, :], start=True, stop=True))
                    fns.append(lambda e, h=h: e.matmul(pa[:, 256 + h * 128:256 + (h + 1) * 128], lhsT=kpT[:, h, :],
                                                       rhs=qnT[:, h, :], start=True, stop=True))
                cx.mm(fns, r=[keT.b, qeT.b, kpT.b, qnT.b], w=[pa.b])
                cx.op("dve", lambda e: e.tensor_tensor(
                    out=pm[:], in0=pa[:].rearrange("p (y h c) -> p y h c", y=2, h=2),
                    in1=gmask[:].unsqueeze(2).to_broadcast([128, 2, 2, 128]), op=ALU.mult),
                    r=[pa.b, gmask.b], w=[pm.b])
                cx.op("pool", lambda e: e.tensor_tensor(out=P[:], in0=pm[:, 0, :, :], in1=pm[:, 1, :, :], op=ALU.add),
                      r=[pm.b], w=[P.b])
                if self.dbg_stage < 6:
                    break
                pv = self.pf()
                cx.mm([lambda e, kc=kc: e.matmul(pv[:], lhsT=hT[:, kc, :], rhs=w_in[:, kc, 512:1024],
                                                 start=(kc == 0), stop=(kc == 7)) for kc in range(8)],
                      r=[w_in.b, hT.b], w=[pv.b])
                cx.op("act", lambda e: e.copy(v[:], pv[:]), r=[pv.b], w=[v.b])
                pg = self.pf()
                cx.mm([lambda e, kc=kc: e.matmul(pg[:], lhsT=hT[:, kc, :], rhs=w_in[:, kc, 1024:1536],
                                                 start=(kc == 0), stop=(kc == 7)) for kc in range(8)],
                      r=[w_in.b, hT.b], w=[pg.b])
                cx.op("act", lambda e: e.activation(out=sg[:], in_=pg[:], func=AF.Silu), r=[pg.b], w=[sg.b])
                if self.dbg_stage < 7:
                    break
                po = self.pf()
                fns = []
                for h in range(2):
                    fns.append(lambda e, h=h: e.matmul(po[:, h * 256:(h + 1) * 256], lhsT=P[:, h, :],
                                                       rhs=v[:, h * 256:(h + 1) * 256], start=True, stop=False))
                    fns.append(lambda e, h=h: e.matmul(po[:, h * 256:(h + 1) * 256], lhsT=qeT[:, h, :],
                                                       rhs=Sb[:, h, :], start=False, stop=True))
                cx.mm(fns, r=[P.b, v.b, qeT.b, Sb.b], w=[po.b])
                if self.dbg_stage < 8:
                    break
                for h in range(2):
                    cx.op("act", lambda e, h=h: e.activation(out=self.junk[:, 0:256], in_=po[:, h * 256:(h + 1) * 256],
                                                             func=AF.Square, accum_out=ssq[:, h:h + 1]),
                          r=[po.b], w=[self.junk.b, ssq.b])
                cx.op("dve", lambda e: e.tensor_scalar(ssq[:, 0:2], ssq[:, 0:2], 1.0 / 256.0, EPS, ALU.mult, ALU.add),
                      r=[ssq.b], w=[ssq.b])
                self._rsqrt(ssq[:, 2:4], ssq[:, 0:2], 2, [ssq.b], [ssq.b])
                for h in range(2):
                    cx.op("act", lambda e, h=h: e.activation(out=on[:, h * 256:(h + 1) * 256],
                                                             in_=po[:, h * 256:(h + 1) * 256], func=AF.Copy,
                                                             scale=ssq[:, 2 + h:3 + h]),
                          r=[po.b, ssq.b], w=[on.b])
                cx.op("pool", lambda e: e.tensor_tensor(out=og[:], in0=on[:], in1=sg[:], op=ALU.mult),
                      r=[on.b, sg.b], w=[og.b])
                if self.dbg_stage < 9:
                    break
                pk = self.pt()
                cx.mm([lambda e, h=h: e.transpose(pk[:, h * 128:(h + 1) * 128], kuT[:, h, :], self.ident_b[:])
                       for h in range(2)], r=[kuT.b, self.ident_b.b], w=[pk.b])
                cx.op("act", lambda e: e.copy(ku[:], pk[:, 0:256]), r=[pk.b], w=[ku.b])
                pd = self.pf()
                cx.mm([lambda e, h=h: e.matmul(pd[:, h * 256:(h + 1) * 256], lhsT=ku[:, h * 128:(h + 1) * 128],
                                               rhs=v[:, h * 256:(h + 1) * 256], start=True, stop=True)
                       for h in range(2)], r=[ku.b, v.b], w=[pd.b])
                for h in range(2):
                    cx.op("dve", lambda e, h=h: e.scalar_tensor_tensor(
                        out=S32[:, h, :], in0=S32[:, h, :], scalar=dec[:, h:h + 1], in1=pd[:, h * 256:(h + 1) * 256],
                        op0=ALU.mult, op1=ALU.add), r=[pd.b, S32.b, dec.b], w=[S32.b])
                cx.op("pool", lambda e: e.tensor_copy(Sb[:], S32[:]), r=[S32.b], w=[Sb.b])
                if self.dbg_stage < 10:
                    break
                pt = self.pt()
                cx.mm([lambda e, c=c: e.transpose(pt[:, c * 128:(c + 1) * 128], og[:, c * 128:(c + 1) * 128],
                                                  self.ident_b[:]) for c in range(4)],
                      r=[og.b, self.ident_b.b], w=[pt.b])
                cx.op("act", lambda e: e.copy(ogT[:].rearrange("p c t -> p (c t)"), pt[:, 0:512]),
                      r=[pt.b], w=[ogT.b])
            self._epilogue(t, prev, dst, is_final, ogT, 4, w_out)


def _core_inputs(b, inp, cst, S):
    f = lambda a: np.ascontiguousarray(np.asarray(a, dtype=np.float32))
    m = {}
    m["x"] = f(inp["x"][b, :S])
    m["c_col"] = f(np.asarray(inp["c"])[b].reshape(8, 128).T)
    m["posf"] = f(np.asarray(inp["positions"])[b, :S].astype(np.float32).reshape(1, S))
    m["mod_w"] = f(inp["mod_w"])
    m["mod_b_col"] = f(np.asarray(inp["mod_b"]).reshape(4, 24, 128).transpose(2, 0, 1))
    m["norm_g_col"] = f(np.asarray(inp["norm_g"]).reshape(4, 8, 128).transpose(2, 0, 1))
    m["final_g_col"] = f(np.asarray(inp["final_g"]).reshape(8, 128).T)
    for k in ("ret_w_in", "ret_w_out", "fox_w_in", "fox_b_f", "fox_w_out", "gla_w_in",
              "gla_w_gate2", "gla_b_gate", "gla_w_out"):
        m[k] = f(inp[k])
    m["fox_qg_col"] = f(np.asarray(inp["fox_q_gain"]).reshape(64, 1))
    m["fox_kg_col"] = f(np.asarray(inp["fox_k_gain"]).reshape(64, 1))
    for k, v in cst.items():
        if isinstance(v, np.ndarray) and v.dtype == np.float32:
            m["k_" + k] = v
    return m


def run(inputs, S=4096, layers=(0, 1, 2, 3), final=True, trace=False):
    prog = Prog(S, list(layers), final)
    nc = prog.build()
    B = np.asarray(inputs["x"]).shape[0]
    in_maps = [_core_inputs(b % B, inputs, prog.cst, S) for b in range(N_CORES)]
    res = run_bass_kernel_spmd(nc, in_maps, core_ids=list(range(N_CORES)), trace=trace)
    out = np.stack([np.asarray(res.results[b]["out"]) for b in range(B)], 0)
    if prog.dbg_on:
        d = np.asarray(res.results[0]["dbg"])
        prog.dbg_vals = {k: d[0:P, off:off + n] for k, (off, P, n) in prog.dbg_map.items()}
    return out.astype(np.float32), res, prog


def kernel(**inputs):
    out, _, _ = run(inputs)
    return out
```

```python
import contextlib
import numpy as np
import concourse.bass as bass
import concourse.mybir as mybir
from concourse.bass_utils import run_bass_kernel_spmd

F32 = mybir.dt.float32
BF16 = mybir.dt.bfloat16
I32 = mybir.dt.int32
AF = mybir.ActivationFunctionType
ALU = mybir.AluOpType
AX = mybir.AxisListType

D = 1024
EPS = 1e-6
N_CORES = 8


class Buf:
    __slots__ = ("name", "w", "r", "dsem", "dcnt", "osem", "ocnt", "psum")

    def __init__(self, name):
        self.name = name
        self.psum = False
        self.w = None
        self.r = {}
        self.dsem = None
        self.dcnt = 0
        self.osem = None
        self.ocnt = 0


class Ctx:
    def __init__(self, nc):
        self.nc = nc
        self.eng = {"pe": nc.tensor, "dve": nc.vector, "act": nc.scalar,
                    "pool": nc.gpsimd, "sp": nc.sync}
        self.sem = {k: nc.alloc_semaphore("sem_" + k) for k in self.eng}
        self.cnt = {k: 0 for k in self.eng}
        self.seen = {k: {} for k in self.eng}
        self.nsem = 0
        self.ninst = 0
        self.dma_ev = {}
        self.log = {k: [] for k in self.eng}
        self.semname = {id(v): "sem_" + k for k, v in self.sem.items()}

    def newsem(self, name):
        self.nsem += 1
        sm = self.nc.alloc_semaphore(f"{name}_{self.nsem}")
        self.semname[id(sm)] = f"{name}_{self.nsem}"
        self._keep = getattr(self, "_keep", []) + [sm]
        return sm

    def _wait(self, e, ev):
        sem, val = ev
        key = id(sem)
        if e == "pe" and sem is self.sem["pe"]:
            return
        if self.seen[e].get(key, 0) >= val:
            return
        self.eng[e].wait_ge(sem, val)
        self.log[e].append(("W", self.semname[key], val))
        self.seen[e][key] = val
        self.ninst += 1

    def _deps(self, e, r, w):
        for b in r:
            if b.w is not None:
                self._wait(e, b.w)
            if b.psum:
                for k, ev in b.r.items():
                    if k != e:
                        self._wait(e, ev)
        for b in w:
            if b.w is not None:
                self._wait(e, b.w)
            for ev in b.r.values():
                self._wait(e, ev)

    def op(self, e, fn, r=(), w=()):
        self._deps(e, r, w)
        ins = fn(self.eng[e])
        self.cnt[e] += 1
        ins.then_inc(self.sem[e], 1)
        self.log[e].append(("I", "sem_" + e, 1))
        ev = (self.sem[e], self.cnt[e])
        for b in r:
            b.r[e] = ev
        for b in w:
            b.w = ev
            b.r = {}
        self.ninst += 1
        return ins

    def mm(self, fns, r=(), w=()):
        self._deps("pe", r, w)
        ins = None
        for fn in fns:
            ins = fn(self.eng["pe"])
            self.ninst += 1
        self.cnt["pe"] += 1
        ins.then_inc(self.sem["pe"], 1)
        self.log["pe"].append(("I", "sem_pe", 1))
        ev = (self.sem["pe"], self.cnt["pe"])
        for b in r:
            b.r["pe"] = ev
        for b in w:
            b.w = ev
            b.r = {}

    def dma_in(self, q, out_ap, in_ap, w, cont=False):
        if not cont:
            self._deps(q, (), (w,))
        if w.dsem is None:
            w.dsem = self.newsem("d")
        self.eng[q].dma_start(out=out_ap, in_=in_ap).then_inc(w.dsem, 16)
        self.log[q].append(("I", self.semname[id(w.dsem)], 16))
        w.dcnt += 16
        self.dma_ev[id(w.dsem)] = (w.dsem, w.dcnt)
        w.w = (w.dsem, w.dcnt)
        w.r = {}
        self.ninst += 1

    def dma_out(self, q, out_ap, in_ap, r):
        self._deps(q, (r,), ())
        if r.osem is None:
            r.osem = self.newsem("o")
        self.eng[q].dma_start(out=out_ap, in_=in_ap).then_inc(r.osem, 16)
        self.log[q].append(("I", self.semname[id(r.osem)], 16))
        r.ocnt += 16
        self.dma_ev[id(r.osem)] = (r.osem, r.ocnt)
        r.r["dmaout"] = (r.osem, r.ocnt)
        self.ninst += 1

    def barrier(self):
        evs = [(self.sem[k], self.cnt[k]) for k in self.eng if self.cnt[k] > 0]
        evs += list(self.dma_ev.values())
        for e in self.eng:
            for ev in evs:
                self._wait(e, ev)

    def flush_out(self, q, bufs):
        for b in bufs:
            if b.osem is not None and b.ocnt > 0:
                self._wait(q, (b.osem, b.ocnt))


class T:
    def __init__(self, nc, name, shape, dt, psum=False, stack=None):
        if psum:
            self.t = nc.alloc_psum_tensor(name, list(shape), dt)
        elif stack is not None:
            self.t = stack.enter_context(nc.sbuf_tensor(name, list(shape), dt))
        else:
            self.t = nc.alloc_sbuf_tensor(name, list(shape), dt)
        self.b = Buf(name)
        self.b.psum = psum
        self.shape = shape

    def __getitem__(self, idx):
        return self.t[idx]


def _cw_consts():
    two_pi = 2.0 * np.pi
    c1 = 6.28125
    r = two_pi - c1
    c2 = float(np.float32(np.round(r * 2 ** 19) / 2 ** 19))
    c3 = float(np.float32(two_pi - c1 - c2))
    return c1, c2, c3


def _consts():
    cst = {}
    cst["ident"] = np.eye(128, dtype=np.float32)
    half = 128
    inv = (np.float32(10000.0) ** (-(np.arange(half, dtype=np.float32) / np.float32(half)))).astype(np.float32)
    cst["inv"] = inv.reshape(128, 1)
    lg = np.log1p(-np.exp2(-5.0 - np.arange(4, dtype=np.float64)))
    t = np.arange(128)
    m_ = t[:, None]
    c_ = t[None, :]
    same = (m_ // 64) == (c_ // 64)
    maskT = np.zeros((4, 128, 128), np.float64)
    for h in range(4):
        d1 = np.exp(lg[h] * np.abs(c_ - m_))
        d2 = np.exp(lg[h] * (c_ - m_))
        maskT[h] = np.where(same, d1, np.where(c_ > m_, d2, 0.0)) * (256.0 ** -0.5)
    cst["ret_maskT"] = np.ascontiguousarray(maskT.transpose(1, 0, 2)).astype(np.float32)
    gq = np.stack([np.exp(lg[h] * (t + 1.0)) * (256.0 ** -0.5) for h in range(4)], 0)
    cst["ret_gq"] = np.broadcast_to(gq[None], (128, 4, 128)).astype(np.float32).copy()
    gk = np.stack([np.exp(lg[h] * (127.0 - t)) for h in range(4)], 1)
    cst["ret_gk"] = gk.astype(np.float32)
    cst["ret_sdec"] = np.exp(lg * 128.0).astype(np.float64)
    cst["gla_tri"] = (np.where(m_ <= c_, 1.0, 0.0) * (-1.0 / 16.0)).astype(np.float32)
    gm = np.zeros((128, 2, 128), np.float32)
    gm[:, 0, :] = np.where(c_ >= m_, 1.0, 0.0)
    gm[:, 1, :] = np.where((m_ > c_) & same, 1.0, 0.0)
    cst["gla_mask"] = gm
    same32 = (m_ // 32) == (c_ // 32)
    cst["fox_sfx"] = np.where((m_ > c_) & same32, 1.0, 0.0).astype(np.float32)
    ci = np.zeros((128, 8, 70), np.float32)
    for c in range(4):
        ci[:, c, :] = (t < (c + 1) * 32)[:, None]
        ci[:, 4 + c, :] = (t < c * 32)[:, None]
    cst["fox_cumind"] = ci.reshape(128, 560)
    cst["fox_negmask"] = np.where(m_ > c_, -30000.0, 0.0).astype(np.float32)
    sel = np.zeros((70, 8), np.float32)
    sel[64, 0] = 1; sel[65, 1] = 1; sel[66, 2] = 1; sel[67:70, 3] = 1
    sel[67, 4] = -1; sel[68, 5] = -1; sel[69, 6] = -1; sel[64:67, 7] = 1
    cst["fox_sel"] = sel
    return cst


class Prog:
    def __init__(self, S, layers, final=True):
        self.S = S
        self.NB = S // 128
        self.layers = layers
        self.final = final
        self.cst = _consts()
        import os
        self.dbg_stage = int(os.environ.get('DBG_STAGE', '99'))

    def build(self):
        S, NB = self.S, self.NB
        nc = bass.Bass("TRN2", target_bir_lowering=False)
        self.nc = nc
        cx = Ctx(nc)
        self.cx = cx

        def din(name, shape, dt=F32):
            return nc.dram_tensor(name, list(shape), dt, kind="ExternalInput").ap()

        self.x_in = din("x", [S, D])
        self.c_col = din("c_col", [128, 8])
        self.posf = din("posf", [1, S])
        self.mod_w = din("mod_w", [4, D, 3 * D])
        self.mod_b_col = din("mod_b_col", [128, 4, 24])
        self.norm_g_col = din("norm_g_col", [128, 4, 8])
        self.final_g_col = din("final_g_col", [128, 8])
        self.ret_w_in = din("ret_w_in", [2, D, 6144])
        self.ret_w_out = din("ret_w_out", [2, 2048, D])
        self.fox_w_in = din("fox_w_in", [1, D, 4112])
        self.fox_b_f = din("fox_b_f", [1, 16])
        self.fox_qg_col = din("fox_qg_col", [64, 1])
        self.fox_kg_col = din("fox_kg_col", [64, 1])
        self.fox_w_out = din("fox_w_out", [1, D, D])
        self.gla_w_in = din("gla_w_in", [1, D, 3088])
        self.gla_w_gate2 = din("gla_w_gate2", [1, 16, 512])
        self.gla_b_gate = din("gla_b_gate", [1, 512])
        self.gla_w_out = din("gla_w_out", [1, D, D])
        self.dconst = {k: din("k_" + k, v.shape) for k, v in self.cst.items()
                       if isinstance(v, np.ndarray) and v.dtype == np.float32}
        self.out = nc.dram_tensor("out", [S, D], F32, kind="ExternalOutput").ap()
        import os
        self.dbg_on = os.environ.get("DBG_DUMP", "") == "1"
        self.dbg_map = {}
        self.dbg_off = 0
        if self.dbg_on:
            self.dbg = nc.dram_tensor("dbg", [128, 8192], F32, kind="ExternalOutput").ap()
        self.xs = [nc.dram_tensor(f"xs{i}", [S, D], F32, kind="Internal").ap() for i in range(3)]
        self.rope_d = nc.dram_tensor("rope_d", [2, 128, S], F32, kind="Internal").ap()

        self._alloc_common()
        with contextlib.ExitStack() as stk:
            self.pstack = stk
            self._startup()
            cx.barrier()
        src = self.x_in
        prev = None
        NP = {0: 2, 1: 4, 2: 2}
        total = sum(NP[l % 3] for l in self.layers)
        pi = 0
        for l in self.layers:
            kind = l % 3
            self._layer_mod(l)
            for p in range(NP[kind]):
                pi += 1
                is_final = (pi == total) and self.final
                if is_final:
                    dst = self.out
                elif pi == total:
                    dst = self.out
                else:
                    dst = [b for b in self.xs if b is not src and b is not prev][0]
                with contextlib.ExitStack() as stk:
                    self.pstack = stk
                    cx.flush_out("sp", self.all_out_bufs)
                    cx.flush_out("pool", self.all_out_bufs)
                    fn = (self._ret_pass, self._fox_pass, self._gla_pass)[kind]
                    fn(l // 3, p, src, prev, dst, is_final)
                    cx.barrier()
                prev = dst
            src = prev
            prev = None
        cx.flush_out("sp", self.all_out_bufs)
        return nc

    def dump(self, name, ap, buf, P, n):
        if not self.dbg_on or name in self.dbg_map:
            return
        cx = self.cx
        st = self.dbg_st[len(self.dbg_map)]
        cx.op("dve", lambda e: e.tensor_copy(st[0:P, 0:n], ap), r=[buf], w=[st.b])
        cx.dma_out("sp", self.dbg[0:P, self.dbg_off:self.dbg_off + n], st[0:P, 0:n], st.b)
        self.all_out_bufs.append(st.b)
        self.dbg_map[name] = (self.dbg_off, P, n)
        self.dbg_off += n

    def mk(self, name, shape, dt=F32):
        self._uid = getattr(self, "_uid", 0) + 1
        return T(self.nc, f"{name}_{self._uid}", shape, dt, stack=self.pstack)

    def _alloc_common(self):
        nc, cx = self.nc, self.cx
        S = self.S
        mk = lambda name, shape, dt=F32: T(nc, name, shape, dt)
        self.ident_f = mk("ident_f", [128, 128])
        self.ident_b = mk("ident_b", [128, 128], BF16)
        self.ones_f = mk("ones_f", [128, 128])
        self.ones_b = mk("ones_b", [128, 128], BF16)
        self.inv = mk("inv", [128, 1])
        self.modcol = mk("modcol", [128, 4, 24])
        self.ngcol = mk("ngcol", [128, 4, 8])
        self.fgcol = mk("fgcol", [128, 8])
        self.G_bc = mk("G_bc", [128, D])
        self.shift_bc = mk("shift_bc", [128, D])
        self.gate_bc = mk("gate_bc", [128, D])
        self.fg_bc = mk("fg_bc", [128, D])
        self.xt = [mk(f"xt{i}", [128, D]) for i in range(2)]
        self.xp = [mk(f"xp{i}", [128, D]) for i in range(2)]
        self.xo = [mk(f"xo{i}", [128, D]) for i in range(2)]
        self.tmpA = mk("tmpA", [128, D])
        self.hb = mk("hb", [128, D], BF16)
        self.hT = mk("hT", [128, 8, 128], BF16)
        self.st = mk("stat", [128, 8])
        self.rs_i = mk("rs_i", [128, 16], I32)
        self.rs_u = mk("rs_u", [128, 16])
        self.halfpi = mk("halfpi", [128, 1])
        self.junk = mk("junk", [128, D], BF16)
        self.all_out_bufs = [t.b for t in self.xo]
        if self.dbg_on:
            self.dbg_st = [mk(f"dbgst{i}", [128, 512]) for i in range(8)]
        self.pT = [T(nc, f"pT{i}", [128, 1024], BF16, psum=True) for i in range(2)]
        self.pF = [T(nc, f"pF{i}", [128, 512], F32, psum=True) for i in range(4)]
        self.pacc = [T(nc, f"pacc{i}", [128, 512], F32, psum=True) for i in range(2)]
        self._pti = 0
        self.one_c = mk("one_c", [128, 1])
        self.eps_c = mk("eps_c", [128, 1])
        self._pf_i = 0
        self._pt_i = 0

    def pf(self):
        t = self.pF[self._pf_i % len(self.pF)]
        self._pf_i += 1
        return t

    def pt(self):
        t = self.pT[self._pt_i % 2]
        self._pt_i += 1
        return t

    def _startup(self):
        nc, cx = self.nc, self.cx
        S = self.S
        dc = self.dconst
        cx.dma_in("sp", self.ident_f[:], dc["ident"][:, :], self.ident_f.b)
        cx.dma_in("sp", self.inv[:], dc["inv"][:, :], self.inv.b)
        cx.dma_in("sp", self.modcol[:], self.mod_b_col[:, :, :], self.modcol.b)
        cx.dma_in("sp", self.ngcol[:], self.norm_g_col[:, :, :], self.ngcol.b)
        cx.dma_in("sp", self.fgcol[:], self.final_g_col[:, :], self.fgcol.b)
        cx.op("dve", lambda e: e.tensor_copy(self.ident_b[:], self.ident_f[:]),
              r=[self.ident_f.b], w=[self.ident_b.b])
        cx.op("pool", lambda e: e.memset(self.ones_f[:], 1.0), w=[self.ones_f.b])
        cx.op("pool", lambda e: e.memset(self.ones_b[:], 1.0), w=[self.ones_b.b])
        cx.op("pool", lambda e: e.memset(self.halfpi[:], float(np.pi / 2)), w=[self.halfpi.b])
        cx.op("pool", lambda e: e.memset(self.one_c[:], 1.0), w=[self.one_c.b])
        cx.op("pool", lambda e: e.memset(self.eps_c[:], EPS), w=[self.eps_c.b])
        ccol = self.mk("ccol", [128, 8])
        cacol = self.mk("cacol", [128, 8])
        cx.dma_in("sp", ccol[:], self.c_col[:, :], ccol.b)
        cx.op("act", lambda e: e.activation(out=cacol[:], in_=ccol[:], func=AF.Silu),
              r=[ccol.b], w=[cacol.b])
        mw = [self.mk(f"mw{i}", [128, 8, 512]) for i in range(2)]
        k = 0
        for l in self.layers:
            pm = self.pf()
            first = True
            for p in range(6):
                buf = mw[k % 2]
                k += 1
                src = self.mod_w[l].rearrange("(kc p) n -> p kc n", p=128)[:, :, p * 512:(p + 1) * 512]
                cx.dma_in("sp", buf[:], src, buf.b)
                fns = []
                for j in range(4):
                    col = p * 4 + j
                    for kc in range(8):
                        fns.append(lambda e, buf=buf, j=j, kc=kc, col=col: e.matmul(
                            pm[:, col:col + 1], lhsT=buf[:, kc, j * 128:(j + 1) * 128],
                            rhs=cacol[:, kc:kc + 1], start=(kc == 0), stop=(kc == 7)))
                cx.mm(fns, r=[buf.b, cacol.b], w=[pm.b])
            cx.op("dve", lambda e, pm=pm, l=l: e.tensor_tensor(
                out=self.modcol[:, l, :], in0=pm[:, 0:24], in1=self.modcol[:, l, :], op=ALU.add),
                r=[pm.b, self.modcol.b], w=[self.modcol.b])
            cx.op("dve", lambda e, l=l: e.scalar_tensor_tensor(
                out=self.modcol[:, l, 8:16], in0=self.modcol[:, l, 8:16], scalar=1.0,
                in1=self.ngcol[:, l, :], op0=ALU.add, op1=ALU.mult),
                r=[self.modcol.b, self.ngcol.b], w=[self.modcol.b])
        if any(l % 3 == 0 for l in self.layers):
            self._rope_tables()
        if self.final:
            self._bcast_cols(self.fgcol, None, [(self.fg_bc, 0)])

    def _rope_tables(self):
        nc, cx = self.nc, self.cx
        S = self.S
        c1, c2, c3 = _cw_consts()
        ang = self.mk("ang", [128, S])
        kf = self.mk("kf", [128, S])
        ki = self.mk("ki", [128, S], I32)
        sinT = self.mk("sinT", [128, S])
        cosT = self.mk("cosT", [128, S])
        cx.dma_in("sp", ang[:], self.posf[0:1, :].partition_broadcast(128), ang.b)
        cx.op("dve", lambda e: e.tensor_scalar(ang[:], ang[:], self.inv[:, 0:1], None, ALU.mult),
              r=[ang.b, self.inv.b], w=[ang.b])
        cx.op("dve", lambda e: e.tensor_scalar(kf[:], ang[:], float(1.0 / (2 * np.pi)), None, ALU.mult),
              r=[ang.b], w=[kf.b])
        cx.op("dve", lambda e: e.tensor_copy(ki[:], kf[:]), r=[kf.b], w=[ki.b])
        cx.op("dve", lambda e: e.tensor_copy(kf[:], ki[:]), r=[ki.b], w=[kf.b])
        for cc in (c1, c2, c3):
            cx.op("dve", lambda e, cc=cc: e.scalar_tensor_tensor(
                out=ang[:], in0=kf[:], scalar=-float(cc), in1=ang[:], op0=ALU.mult, op1=ALU.add),
                r=[kf.b, ang.b], w=[ang.b])
        pi = float(np.pi)
        cx.op("dve", lambda e: e.tensor_scalar(ang[:], ang[:], pi, -pi, ALU.min, ALU.max),
              r=[ang.b], w=[ang.b])
        cx.op("act", lambda e: e.activation(out=sinT[:], in_=ang[:], func=AF.Sin),
              r=[ang.b], w=[sinT.b])
        cx.op("act", lambda e: e.activation(out=kf[:], in_=ang[:], func=AF.Abs),
              r=[ang.b], w=[kf.b])
        cx.op("act", lambda e: e.activation(out=cosT[:], in_=kf[:], func=AF.Sin, scale=-1.0,
                                            bias=self.halfpi[:, 0:1]),
              r=[kf.b, self.halfpi.b], w=[cosT.b])
        cx.dma_out("sp", self.rope_d[0], cosT[:], cosT.b)
        cx.dma_out("sp", self.rope_d[1], sinT[:], sinT.b)
        cx.flush_out("sp", [cosT.b, sinT.b])

    def _wrap(self, a, tmp):
        cx = self.cx
        pi = float(np.pi)
        cx.op("dve", lambda e: e.tensor_scalar(tmp[:], a[:], pi, -2.0 * pi, ALU.is_gt, ALU.mult),
              r=[a.b], w=[tmp.b])
        cx.op("dve", lambda e: e.tensor_tensor(out=a[:], in0=a[:], in1=tmp[:], op=ALU.add),
              r=[a.b, tmp.b], w=[a.b])
        cx.op("dve", lambda e: e.tensor_scalar(tmp[:], a[:], -pi, 2.0 * pi, ALU.is_lt, ALU.mult),
              r=[a.b], w=[tmp.b])
        cx.op("dve", lambda e: e.tensor_tensor(out=a[:], in0=a[:], in1=tmp[:], op=ALU.add),
              r=[a.b, tmp.b], w=[a.b])
        cx.op("dve", lambda e: e.tensor_scalar(a[:], a[:], pi, -pi, ALU.min, ALU.max),
              r=[a.b], w=[a.b])

    def _bcast_cols(self, colT, l, outs):
        cx = self.cx
        for dst, off in outs:
            rep = self.tmpA
            src = colT[:, off:off + 8] if l is None else colT[:, l, off:off + 8]
            cx.op("dve", lambda e, src=src: e.tensor_copy(
                rep[:].rearrange("p (c m) -> p c m", c=8), src.unsqueeze(2).to_broadcast([128, 8, 128])),
                r=[colT.b], w=[rep.b])
            for hf in range(2):
                pm = self.pf()
                fns = []
                for c in range(4):
                    cc = hf * 4 + c
                    fns.append(lambda e, cc=cc, c=c, pm=pm: e.matmul(
                        pm[:, c * 128:(c + 1) * 128], lhsT=rep[:, cc * 128:(cc + 1) * 128],
                        rhs=self.ident_f[:], start=True, stop=True))
                cx.mm(fns, r=[rep.b, self.ident_f.b], w=[pm.b])
                cx.op("act", lambda e, pm=pm, hf=hf, dst=dst: e.copy(dst[:, hf * 512:(hf + 1) * 512], pm[:]),
                      r=[pm.b], w=[dst.b])

    def _layer_mod(self, l):
        self._bcast_cols(self.modcol, l, [(self.shift_bc, 0), (self.G_bc, 8), (self.gate_bc, 16)])

    def _load_x(self, t, src, prev, half):
        cx = self.cx
        xt = self.xt[t % 2]
        cx.dma_in("sp", xt[:], src[t * 128:(t + 1) * 128, :], xt.b)
        if prev is not None:
            xp = self.xp[t % 2]
            cx.dma_in("sp", xp[:], prev[t * 128:(t + 1) * 128, :], xp.b)

    def _split(self, src, sb, hi, hib, lo, lob):
        cx = self.cx
        cx.op("dve", lambda e: e.tensor_copy(hi, src), r=[sb], w=[hib])
        cx.op("dve", lambda e: e.tensor_tensor(out=lo, in0=src, in1=hi, op=ALU.subtract), r=[sb, hib], w=[lob])

    def _rsqrt(self, dst, a, n, rbufs, wbufs):
        cx = self.cx
        yi = self.rs_i[:, 0:n]
        y = yi.bitcast(F32)
        u = self.rs_u[:, 0:n]
        sb = self.rs_i.b
        cx.op("dve", lambda e: e.tensor_scalar(yi, a.bitcast(I32), 1, None, ALU.arith_shift_right),
              r=rbufs, w=[sb])
        cx.op("dve", lambda e: e.tensor_scalar(yi, yi, -1.0, float(0x5f3759df), ALU.mult, ALU.add),
              r=[sb], w=[sb])
        NIT = 2
        for it in range(NIT):
            cx.op("dve", lambda e: e.scalar_tensor_tensor(out=u, in0=y, scalar=-0.5, in1=y,
                                                          op0=ALU.mult, op1=ALU.mult), r=[sb], w=[self.rs_u.b])
            cx.op("dve", lambda e: e.tensor_tensor(out=u, in0=u, in1=a, op=ALU.mult),
                  r=[self.rs_u.b] + list(rbufs), w=[self.rs_u.b])
            if it < NIT - 1:
                cx.op("dve", lambda e: e.scalar_tensor_tensor(out=y, in0=u, scalar=1.5, in1=y,
                                                              op0=ALU.add, op1=ALU.mult),
                      r=[self.rs_u.b, sb], w=[sb])
            else:
                cx.op("dve", lambda e: e.scalar_tensor_tensor(out=dst, in0=u, scalar=1.5, in1=y,
                                                              op0=ALU.add, op1=ALU.mult),
                      r=[self.rs_u.b, sb], w=wbufs)

    def _rstd(self, xt, col):
        cx = self.cx
        st = self.st
        cx.op("act", lambda e: e.activation(out=self.junk[:], in_=xt[:], func=AF.Square,
                                            accum_out=st[:, col:col + 1]),
              r=[xt.b], w=[self.junk.b, st.b])
        cx.op("dve", lambda e: e.tensor_scalar(st[:, col:col + 1], st[:, col:col + 1], 1.0 / D, EPS,
                                               ALU.mult, ALU.add), r=[st.b], w=[st.b])
        self._rsqrt(st[:, col + 2:col + 3], st[:, col:col + 1], 1, [st.b], [st.b])

    def _prologue(self, t):
        cx = self.cx
        xt = self.xt[t % 2]
        self._rstd(xt, 0)
        cx.op("dve", lambda e: e.scalar_tensor_tensor(
            out=self.tmpA[:], in0=xt[:], scalar=self.st[:, 2:3], in1=self.G_bc[:],
            op0=ALU.mult, op1=ALU.mult), r=[xt.b, self.st.b, self.G_bc.b], w=[self.tmpA.b])
        cx.op("pool", lambda e: e.tensor_tensor(out=self.hb[:], in0=self.tmpA[:], in1=self.shift_bc[:],
                                                op=ALU.add),
              r=[self.tmpA.b, self.shift_bc.b], w=[self.hb.b])
        pt = self.pt()
        cx.mm([lambda e, c=c: e.transpose(pt[:, c * 128:(c + 1) * 128], self.hb[:, c * 128:(c + 1) * 128],
                                          self.ident_b[:]) for c in range(8)],
              r=[self.hb.b, self.ident_b.b], w=[pt.b])
        cx.op("act", lambda e: e.copy(self.hT[:].rearrange("p c t -> p (c t)"), pt[:]),
              r=[pt.b], w=[self.hT.b])

    def _epilogue(self, t, half, dst, is_final, ogT, nchunk, wout, kparts=128, tsl=None, xprev=None):
        cx = self.cx
        if xprev is None:
            xprev = self.xt[t % 2] if half is None else self.xp[t % 2]
        if tsl is None:
            tsl = slice(0, 128)
        xo = self.xo[t % 2]
        for hf in range(2):
            py = self.pf()
            cx.mm([lambda e, c=c, py=py, hf=hf: e.matmul(
                py[:], lhsT=ogT[0:kparts, c, tsl], rhs=wout[0:kparts, c, hf * 512:(hf + 1) * 512],
                start=(c == 0), stop=(c == nchunk - 1)) for c in range(nchunk)],
                r=[ogT.b, wout.b], w=[py.b])
            cx.op("dve", lambda e, py=py, hf=hf: e.tensor_tensor(
                out=self.tmpA[:, hf * 512:(hf + 1) * 512], in0=py[:],
                in1=self.gate_bc[:, hf * 512:(hf + 1) * 512], op=ALU.mult),
                r=[py.b, self.gate_bc.b], w=[self.tmpA.b])
        cx.op("pool", lambda e: e.tensor_tensor(out=xo[:], in0=self.tmpA[:], in1=xprev[:], op=ALU.add),
              r=[self.tmpA.b, xprev.b], w=[xo.b])
        if is_final:
            self._rstd(xo, 1)
            cx.op("dve", lambda e: e.scalar_tensor_tensor(
                out=xo[:], in0=xo[:], scalar=self.st[:, 3:4], in1=self.fg_bc[:],
                op0=ALU.mult, op1=ALU.mult), r=[xo.b, self.st.b, self.fg_bc.b], w=[xo.b])
        cx.dma_out("sp", dst[t * 128:(t + 1) * 128, :], xo[:], xo.b)

    def _ret_alloc(self):
        nc = self.nc
        mk = self.mk
        self.r_cs = [mk(f"r_cs{i}", [128, 2, 128]) for i in range(2)]
        self.rw_in = mk("rw_in", [128, 8, 3072], BF16)
        self.rw_out = mk("rw_out", [128, 8, 1024], BF16)
        self.r_maskT = mk("r_maskT", [128, 4, 128])
        self.r_gq = mk("r_gq", [128, 4, 128])
        self.r_gk = mk("r_gk", [128, 4])
        self.r_S32 = mk("r_S32", [128, 4, 512])
        self.r_Sb = mk("r_Sb", [128, 4, 512], BF16)
        self.r_qkT = mk("r_qkT", [128, 8, 128], BF16)
        self.r_qin = mk("r_qin", [128, 4, 128], BF16)
        self.r_t1 = mk("r_t1", [128, 4, 128])
        self.r_t2 = mk("r_t2", [128, 4, 128])
        self.r_kin = mk("r_kin", [128, 512], BF16)
        self.r_v = mk("r_v", [128, 1024], BF16)
        self.r_sg = mk("r_sg", [128, 1024], BF16)
        self.r_P = mk("r_P", [128, 2, 128], BF16)
        self.r_on = mk("r_on", [128, 1024], BF16)
        self.r_og = mk("r_og", [128, 1024], BF16)
        self.r_ogT = mk("r_ogT", [128, 8, 128], BF16)
        self.r_bn = mk("r_bn", [128, 2, 6])
        self.r_mv = mk("r_mv", [128, 2, 2])
        self.r_nb = mk("r_nb", [128, 2])
        self.r_rs = mk("r_rs", [128, 2])
        cx = self.cx
        dc = self.dconst
        cx.dma_in("sp", self.r_maskT[:], dc["ret_maskT"][:, :, :], self.r_maskT.b)
        cx.dma_in("sp", self.r_gq[:], dc["ret_gq"][:, :, :], self.r_gq.b)
        cx.dma_in("sp", self.r_gk[:], dc["ret_gk"][:, :], self.r_gk.b)

    def _ret_pass(self, j, half, src, prev, dst, is_final):
        nc, cx = self.nc, self.cx
        self._ret_alloc()
        NB = self.NB
        h0 = 2 * half
        win = self.ret_w_in[j].rearrange("(kc p) n -> p kc n", p=128)
        first = True
        for (dst0, src0, n) in ((0, h0 * 256, 512), (512, 1024 + h0 * 256, 512),
                                (1024, 2048 + h0 * 512, 1024), (2048, 4096 + h0 * 512, 1024)):
            for kc in range(8):
                cx.dma_in("pool", self.rw_in[:, kc, dst0:dst0 + n], win[:, kc, src0:src0 + n],
                          self.rw_in.b, cont=not first)
                first = False
        wout = self.ret_w_out[j].rearrange("(ec p) n -> p ec n", p=128)
        for ec in range(8):
            cx.dma_in("pool", self.rw_out[:, ec, :], wout[:, h0 * 4 + ec, :], self.rw_out.b, cont=(ec > 0))
        cx.op("pool", lambda e: e.memset(self.r_S32[:], 0.0), w=[self.r_S32.b])
        cx.op("pool", lambda e: e.memset(self.r_Sb[:], 0.0), w=[self.r_Sb.b])
        sdec = [float(self.cst["ret_sdec"][h0 + i]) for i in range(2)]

        self._load_x(0, src, prev, half)
        self._prologue(0)
        for t in range(NB):
            if t + 1 < NB:
                self._load_x(t + 1, src, prev, half)
            hT = self.hT
            W = self.rw_in
            pq = [self.pf(), self.pf()]
            for qk in range(2):
                fns = []
                for c in range(4):
                    col = qk * 512 + c * 128
                    for kc in range(8):
                        fns.append(lambda e, qk=qk, c=c, col=col, kc=kc: e.matmul(
                            pq[qk][:, c * 128:(c + 1) * 128], lhsT=W[:, kc, col:col + 128],
                            rhs=hT[:, kc, :], start=(kc == 0), stop=(kc == 7)))
                cx.mm(fns, r=[W.b, hT.b], w=[pq[qk].b])
            rcs = self.r_cs[t % 2]
            cx.dma_in("sp", rcs[:], self.rope_d[:, :, t * 128:(t + 1) * 128].rearrange("c p t -> p c t"), rcs.b)
            cs = rcs[:, 0:1, :].to_broadcast([128, 2, 128])
            sn = rcs[:, 1:2, :].to_broadcast([128, 2, 128])
            for qk in range(2):
                pv = pq[qk][:].rearrange("p (h d t) -> p h d t", h=2, d=2)
                x1 = pv[:, :, 0, :]
                x2 = pv[:, :, 1, :]
                t1 = self.r_t1[:, qk * 2:(qk + 1) * 2, :]
                t2 = self.r_t2[:, qk * 2:(qk + 1) * 2, :]
                ov = self.r_qkT[:, qk * 4:(qk + 1) * 4, :].rearrange("p (h d) t -> p h d t", d=2)
                cx.op("dve", lambda e, x1=x1, t1=t1: e.tensor_tensor(out=t1, in0=x1, in1=cs, op=ALU.mult),
                      r=[pq[qk].b, rcs.b], w=[self.r_t1.b])
                cx.op("dve", lambda e, x2=x2, t2=t2: e.tensor_tensor(out=t2, in0=x2, in1=sn, op=ALU.mult),
                      r=[pq[qk].b, rcs.b], w=[self.r_t2.b])
                cx.op("pool", lambda e, t1=t1, t2=t2, ov=ov: e.tensor_tensor(
                    out=ov[:, :, 0, :], in0=t1, in1=t2, op=ALU.subtract),
                    r=[self.r_t1.b, self.r_t2.b], w=[self.r_qkT.b])
                cx.op("dve", lambda e, x1=x1, t1=t1: e.tensor_tensor(out=t1, in0=x1, in1=sn, op=ALU.mult),
                      r=[pq[qk].b, rcs.b], w=[self.r_t1.b])
                cx.op("dve", lambda e, x2=x2, t2=t2: e.tensor_tensor(out=t2, in0=x2, in1=cs, op=ALU.mult),
                      r=[pq[qk].b, rcs.b], w=[self.r_t2.b])
                cx.op("pool", lambda e, t1=t1, t2=t2, ov=ov: e.tensor_tensor(
                    out=ov[:, :, 1, :], in0=t1, in1=t2, op=ALU.add),
                    r=[self.r_t1.b, self.r_t2.b], w=[self.r_qkT.b])
            qkT = self.r_qkT
            gq = self.r_gq[:, h0:h0 + 2, :].unsqueeze(2).to_broadcast([128, 2, 2, 128])
            cx.op("dve", lambda e: e.tensor_tensor(
                out=self.r_qin[:].rearrange("p (h d) t -> p h d t", d=2),
                in0=qkT[:, 0:4, :].rearrange("p (h d) t -> p h d t", d=2), in1=gq, op=ALU.mult),
                r=[qkT.b, self.r_gq.b], w=[self.r_qin.b])
            ps = self.pf()
            fns = []
            for h in range(2):
                for d in range(2):
                    fns.append(lambda e, h=h, d=d: e.matmul(
                        ps[:, h * 128:(h + 1) * 128], lhsT=qkT[:, 4 + h * 2 + d, :], rhs=qkT[:, h * 2 + d, :],
                        start=(d == 0), stop=(d == 1)))
            cx.mm(fns, r=[qkT.b], w=[ps.b])
            cx.op("dve", lambda e: e.tensor_tensor(
                out=self.r_P[:], in0=ps[:, 0:256].rearrange("p (h c) -> p h c", h=2),
                in1=self.r_maskT[:, h0:h0 + 2, :], op=ALU.mult),
                r=[ps.b, self.r_maskT.b], w=[self.r_P.b])
            pk = self.pt()
            cx.mm([lambda e, c=c: e.transpose(pk[:, c * 128:(c + 1) * 128], qkT[:, 4 + c, :], self.ident_b[:])
                   for c in range(4)], r=[qkT.b, self.ident_b.b], w=[pk.b])
            for h in range(2):
                cx.op("act", lambda e, h=h: e.activation(
                    out=self.r_kin[:, h * 256:(h + 1) * 256], in_=pk[:, h * 256:(h + 1) * 256],
                    func=AF.Copy, scale=self.r_gk[:, h0 + h:h0 + h + 1]),
                    r=[pk.b, self.r_gk.b], w=[self.r_kin.b])
            for vi in range(2):
                pvv = self.pf()
                cx.mm([lambda e, kc=kc, vi=vi, pvv=pvv: e.matmul(
                    pvv[:], lhsT=hT[:, kc, :], rhs=W[:, kc, 1024 + vi * 512:1024 + (vi + 1) * 512],
                    start=(kc == 0), stop=(kc == 7)) for kc in range(8)], r=[W.b, hT.b], w=[pvv.b])
                cx.op("act", lambda e, vi=vi, pvv=pvv: e.copy(self.r_v[:, vi * 512:(vi + 1) * 512], pvv[:]),
                      r=[pvv.b], w=[self.r_v.b])
            for gi in range(2):
                pg = self.pf()
                cx.mm([lambda e, kc=kc, gi=gi, pg=pg: e.matmul(
                    pg[:], lhsT=hT[:, kc, :], rhs=W[:, kc, 2048 + gi * 512:2048 + (gi + 1) * 512],
                    start=(kc == 0), stop=(kc == 7)) for kc in range(8)], r=[W.b, hT.b], w=[pg.b])
                cx.op("act", lambda e, gi=gi, pg=pg: e.activation(
                    out=self.r_sg[:, gi * 512:(gi + 1) * 512], in_=pg[:], func=AF.Silu),
                    r=[pg.b], w=[self.r_sg.b])
            if t + 1 < NB:
                self._prologue(t + 1)
            pos = []
            for h in range(2):
                po = self.pf()
                pos.append(po)
                fns = [lambda e, h=h, po=po: e.matmul(po[:], lhsT=self.r_P[:, h, :],
                                                      rhs=self.r_v[:, h * 512:(h + 1) * 512],
                                                      start=True, stop=False)]
                for d in range(2):
                    fns.append(lambda e, h=h, d=d, po=po: e.matmul(
                        po[:], lhsT=self.r_qin[:, h * 2 + d, :], rhs=self.r_Sb[:, h * 2 + d, :],
                        start=False, stop=(d == 1)))
                cx.mm(fns, r=[self.r_P.b, self.r_v.b, self.r_qin.b, self.r_Sb.b], w=[po.b])
                cx.op("dve", lambda e, h=h, po=po: e.bn_stats(self.r_bn[:, h, :], po[:]),
                      r=[po.b], w=[self.r_bn.b])
                cx.op("dve", lambda e, h=h: e.bn_aggr(self.r_mv[:, h, :], self.r_bn[:, h, :]),
                      r=[self.r_bn.b], w=[self.r_mv.b])
            cx.op("dve", lambda e: e.tensor_scalar(self.r_mv[:, :, 1], self.r_mv[:, :, 1], EPS, None, ALU.add),
                  r=[self.r_mv.b], w=[self.r_mv.b])
            self._rsqrt(self.r_rs[:, 0:2], self.r_mv[:, :, 1], 2, [self.r_mv.b], [self.r_rs.b])
            cx.op("dve", lambda e: e.scalar_tensor_tensor(
                out=self.r_nb[:, 0:2], in0=self.r_mv[:, :, 0], scalar=-1.0,
                in1=self.r_rs[:, 0:2], op0=ALU.mult, op1=ALU.mult),
                r=[self.r_mv.b, self.r_rs.b], w=[self.r_nb.b])
            for h in range(2):
                po = pos[h]
                cx.op("act", lambda e, h=h, po=po: e.activation(
                    out=self.r_on[:, h * 512:(h + 1) * 512], in_=po[:], func=AF.Identity,
                    bias=self.r_nb[:, h:h + 1], scale=self.r_rs[:, h:h + 1]),
                    r=[po.b, self.r_nb.b, self.r_rs.b], w=[self.r_on.b])
            cx.op("pool", lambda e: e.tensor_tensor(out=self.r_og[:], in0=self.r_on[:], in1=self.r_sg[:],
                                                    op=ALU.mult),
                  r=[self.r_on.b, self.r_sg.b], w=[self.r_og.b])
            for h in range(2):
                for d in range(2):
                    pd = self.pf()
                    cx.mm([lambda e, h=h, d=d, pd=pd: e.matmul(
                        pd[:], lhsT=self.r_kin[:, h * 256 + d * 128:h * 256 + (d + 1) * 128],
                        rhs=self.r_v[:, h * 512:(h + 1) * 512], start=True, stop=True)],
                        r=[self.r_kin.b, self.r_v.b], w=[pd.b])
                    cx.op("dve", lambda e, h=h, d=d, pd=pd: e.scalar_tensor_tensor(
                        out=self.r_S32[:, h * 2 + d, :], in0=self.r_S32[:, h * 2 + d, :], scalar=sdec[h],
                        in1=pd[:], op0=ALU.mult, op1=ALU.add),
                        r=[pd.b, self.r_S32.b], w=[self.r_S32.b])
            cx.op("pool", lambda e: e.tensor_copy(self.r_Sb[:], self.r_S32[:]),
                  r=[self.r_S32.b], w=[self.r_Sb.b])
            pt = self.pt()
            cx.mm([lambda e, c=c: e.transpose(pt[:, c * 128:(c + 1) * 128], self.r_og[:, c * 128:(c + 1) * 128],
                                              self.ident_b[:]) for c in range(8)],
                  r=[self.r_og.b, self.ident_b.b], w=[pt.b])
            cx.op("act", lambda e: e.copy(self.r_ogT[:].rearrange("p c t -> p (c t)"), pt[:]),
                  r=[pt.b], w=[self.r_ogT.b])
            self._epilogue(t, prev, dst, is_final, self.r_ogT, 8, self.rw_out)

    def _fox_pass(self, j, p, src, prev, dst, is_final):
        nc, cx = self.nc, self.cx
        mk = self.mk
        NB, S = self.NB, self.S
        HP = 4
        hb = HP * p
        dc = self.dconst
        NSB = NB // 4
        w_in = mk("fw_in", [128, 8, 4 * HP * 64 + HP], BF16)
        w_out = mk("fw_out", [64, HP, 1024], BF16)
        bfb = mk("fbf", [128, HP])
        qg = mk("fqg", [64, 1])
        kg = mk("fkg", [64, 1])
        sfxm = mk("fsfx", [128, 128])
        negm = mk("fnegm", [128, 128], BF16)
        negm_f = mk("fnegmf", [128, 128])
        sel = mk("fsel", [70, 8])
        kTa = mk("fkTa", [70, HP, S], BF16)
        qTa = mk("fqTa", [70, HP, 512], BF16)
        Vp = mk("fVp", [128, NB, HP, 65], BF16)
        sgT = mk("fsgT", [64, HP, 512], BF16)
        ogT = mk("fogT", [64, HP, 512], BF16)
        sq = mk("fsq", [64, 2, HP * 128])
        sqh = mk("fsqh", [64, 2, HP * 128], BF16)
        sql = mk("fsql", [64, 2, HP * 128], BF16)
        lafh = mk("flafh", [128, HP], BF16)
        lafl = mk("flafl", [128, HP], BF16)
        sfxb = mk("fsfxb", [128, 128], BF16)
        rdh = mk("frdh", [65, 512], BF16)
        rdl = mk("frdl", [65, 512], BF16)
        rs = mk("frs", [64, 2, HP * 128])
        zf = mk("fzf", [128, HP])
        laf = mk("flaf", [128, HP])
        wk = mk("fwk", [128, HP])
        Rp = mk("fRp", [70, HP])
        R2 = mk("fR2", [70, 2, 4, HP])
        pcs = [mk(f"fpc{i}", [70, 2, 4, HP]) for i in range(3)]
        pcb = mk("fpcb", [70, 2, 4, HP], BF16)
        dd = mk("fdd", [70, 2, 4, HP])
        selk = mk("fselk", [70, 4, HP])
        selq = mk("fselq", [70, 4, HP])
        cumind = mk("fcumind", [128, 8, 70], BF16)
        Pt = [mk(f"fPt{i}", [128, 512], BF16) for i in range(3)]
        osb = mk("fosb", [65, 512])
        onr = mk("fonr", [64, 512])
        nq = HP * 64

        win = self.fox_w_in[j].rearrange("(kc p) n -> p kc n", p=128)
        first = True
        for gi in range(4):
            for kc in range(8):
                cx.dma_in("pool", w_in[:, kc, gi * nq:(gi + 1) * nq],
                          win[:, kc, gi * 1024 + hb * 64:gi * 1024 + hb * 64 + nq], w_in.b, cont=not first)
                first = False
        for kc in range(8):
            cx.dma_in("pool", w_in[:, kc, 4 * nq:4 * nq + HP], win[:, kc, 4096 + hb:4096 + hb + HP], w_in.b, cont=True)
        wout = self.fox_w_out[j].rearrange("(h d) n -> d h n", d=64)
        for h in range(HP):
            cx.dma_in("pool", w_out[:, h, :], wout[:, hb + h, :], w_out.b, cont=(h > 0))
        cx.dma_in("sp", bfb[:], self.fox_b_f[j:j + 1, hb:hb + HP].partition_broadcast(128), bfb.b)
        cx.dma_in("sp", qg[:], self.fox_qg_col[:, :], qg.b)
        cx.dma_in("sp", kg[:], self.fox_kg_col[:, :], kg.b)
        cx.dma_in("sp", sfxm[:], dc["fox_sfx"][:, :], sfxm.b)
        cx.dma_in("sp", negm_f[:], dc["fox_negmask"][:, :], negm_f.b)
        cx.dma_in("sp", sel[:], dc["fox_sel"][:, :], sel.b)
        cx.dma_in("pool", cumind[:].rearrange("p c m -> p (c m)"), dc["fox_cumind"][:, :], cumind.b)
        cx.op("dve", lambda e: e.tensor_copy(negm[:], negm_f[:]), r=[negm_f.b], w=[negm.b])
        cx.op("dve", lambda e: e.tensor_copy(sfxb[:], sfxm[:]), r=[sfxm.b], w=[sfxb.b])
        cx.op("dve", lambda e: e.tensor_scalar(qg[:], qg[:], 0.125, None, ALU.mult), r=[qg.b], w=[qg.b])
        cx.op("pool", lambda e: e.memset(Rp[:], 0.0), w=[Rp.b])

        self._load_x(0, src, None, None)
        self._prologue(0)
        for J in range(NSB):
            for tb in range(4):
                t = 4 * J + tb
                if t + 1 < NB:
                    self._load_x(t + 1, src, None, None)
                hT = self.hT
                for qi, (off, gcol, dstT, tok0) in enumerate(((0, qg, qTa, tb * 128), (nq, kg, kTa, t * 128))):
                    pp = self.pf()
                    fns = []
                    for h in range(HP):
                        for kc in range(8):
                            fns.append(lambda e, pp=pp, off=off, h=h, kc=kc: e.matmul(
                                pp[0:64, h * 128:(h + 1) * 128], lhsT=w_in[:, kc, off + h * 64:off + (h + 1) * 64],
                                rhs=hT[:, kc, :], start=(kc == 0), stop=(kc == 7)))
                    cx.mm(fns, r=[w_in.b, hT.b], w=[pp.b])
                    cx.op("act", lambda e, qi=qi, pp=pp: e.activation(out=sq[:, qi, :], in_=pp[0:64, :],
                                                                      func=AF.Square), r=[pp.b], w=[sq.b])
                    self._split(sq[:, qi, :], sq.b, sqh[:, qi, :], sqh.b, sql[:, qi, :], sql.b)
                    ps_ = self.pf()
                    cx.mm([lambda e, qi=qi, ps_=ps_: e.matmul(ps_[0:64, :], lhsT=self.ones_b[0:64, 0:64],
                                                            rhs=sqh[:, qi, :], start=True, stop=False),
                           lambda e, qi=qi, ps_=ps_: e.matmul(ps_[0:64, :], lhsT=self.ones_b[0:64, 0:64],
                                                            rhs=sql[:, qi, :], start=False, stop=True)],
                          r=[sqh.b, sql.b, self.ones_b.b], w=[ps_.b])
                    cx.op("dve", lambda e, qi=qi, ps_=ps_: e.tensor_scalar(
                        rs[:, qi, :], ps_[0:64, :], 1.0 / 64.0, EPS, ALU.mult, ALU.add), r=[ps_.b], w=[rs.b])
                    cx.op("act", lambda e, qi=qi: e.activation(out=rs[:, qi, :], in_=rs[:, qi, :], func=AF.Ln),
                          r=[rs.b], w=[rs.b])
                    cx.op("act", lambda e, qi=qi: e.activation(out=rs[:, qi, :], in_=rs[:, qi, :], func=AF.Exp,
                                                               scale=-0.5), r=[rs.b], w=[rs.b])
                    cx.op("dve", lambda e, qi=qi, pp=pp, gcol=gcol, dstT=dstT, tok0=tok0: e.scalar_tensor_tensor(
                        out=dstT[0:64, :, tok0:tok0 + 128], in0=pp[0:64, :].rearrange("p (h t) -> p h t", h=HP),
                        scalar=gcol[:, 0:1], in1=rs[:, qi, :].rearrange("p (h t) -> p h t", h=HP),
                        op0=ALU.mult, op1=ALU.mult), r=[pp.b, gcol.b, rs.b], w=[dstT.b])
                pg = self.pf()
                fns = []
                for h in range(HP):
                    for kc in range(8):
                        fns.append(lambda e, h=h, kc=kc: e.matmul(
                            pg[0:64, h * 128:(h + 1) * 128], lhsT=w_in[:, kc, 3 * nq + h * 64:3 * nq + (h + 1) * 64],
                            rhs=hT[:, kc, :], start=(kc == 0), stop=(kc == 7)))
                cx.mm(fns, r=[w_in.b, hT.b], w=[pg.b])
                cx.op("act", lambda e: e.activation(out=sgT[:, :, tb * 128:(tb + 1) * 128],
                                                    in_=pg[0:64, :].rearrange("p (h t) -> p h t", h=HP),
                                                    func=AF.Silu), r=[pg.b], w=[sgT.b])
                pv = self.pf()
                cx.mm([lambda e, kc=kc: e.matmul(pv[:, 0:nq], lhsT=hT[:, kc, :], rhs=w_in[:, kc, 2 * nq:3 * nq],
                                                 start=(kc == 0), stop=(kc == 7)) for kc in range(8)],
                      r=[w_in.b, hT.b], w=[pv.b])
                pff = self.pf()
                cx.mm([lambda e, kc=kc: e.matmul(pff[:, 0:HP], lhsT=hT[:, kc, :], rhs=w_in[:, kc, 4 * nq:4 * nq + HP],
                                                 start=(kc == 0), stop=(kc == 7)) for kc in range(8)],
                      r=[w_in.b, hT.b], w=[pff.b])
                if t + 1 < NB:
                    self._prologue(t + 1)
                cx.op("dve", lambda e: e.tensor_tensor(out=zf[:], in0=pff[:, 0:HP], in1=bfb[:], op=ALU.add),
                      r=[pff.b, bfb.b], w=[zf.b])
                cx.op("act", lambda e: e.activation(out=zf[:], in_=zf[:], func=AF.Exp, scale=-1.0),
                      r=[zf.b], w=[zf.b])
                cx.op("dve", lambda e: e.tensor_scalar(zf[:], zf[:], 1.0, None, ALU.add), r=[zf.b], w=[zf.b])
                cx.op("act", lambda e: e.activation(out=laf[:], in_=zf[:], func=AF.Ln), r=[zf.b], w=[laf.b])
                psx = self.pf()
                self._split(laf[:], laf.b, lafh[:], lafh.b, lafl[:], lafl.b)
                fns = [lambda e: e.matmul(psx[:, 0:HP], lhsT=sfxb[:], rhs=lafh[:], start=True, stop=False),
                       lambda e: e.matmul(psx[:, 0:HP], lhsT=sfxb[:], rhs=lafl[:], start=False, stop=True)]
                for c8 in range(8):
                    o0 = 8 + c8 * HP
                    fns.append(lambda e, c8=c8, o0=o0: e.matmul(psx[0:70, o0:o0 + HP], lhsT=cumind[:, c8, :],
                                                                rhs=lafh[:], start=True, stop=False))
                    fns.append(lambda e, c8=c8, o0=o0: e.matmul(psx[0:70, o0:o0 + HP], lhsT=cumind[:, c8, :],
                                                                rhs=lafl[:], start=False, stop=True))
                cx.mm(fns, r=[sfxb.b, lafh.b, lafl.b, cumind.b], w=[psx.b])
                cx.op("act", lambda e: e.activation(out=wk[:], in_=psx[:, 0:HP], func=AF.Exp, scale=-1.0),
                      r=[psx.b], w=[wk.b])
                cx.op("dve", lambda e: e.tensor_tensor(
                    out=Vp[:, t, :, 0:64], in0=pv[:, 0:nq].rearrange("p (h d) -> p h d", h=HP),
                    in1=wk[:].unsqueeze(2).to_broadcast([128, HP, 64]), op=ALU.mult),
                    r=[pv.b, wk.b], w=[Vp.b])
                cx.op("dve", lambda e: e.tensor_copy(Vp[:, t, :, 64], wk[:]), r=[wk.b], w=[Vp.b])
                cx.op("dve", lambda e: e.tensor_tensor(
                    out=R2[:].rearrange("p s c h -> p (s c) h"),
                    in0=psx[0:70, 8:8 + 8 * HP].rearrange("p (c h) -> p c h", h=HP),
                    in1=Rp[:].unsqueeze(1).to_broadcast([70, 8, HP]), op=ALU.add),
                    r=[psx.b, Rp.b], w=[R2.b])
                cx.op("dve", lambda e: e.tensor_copy(Rp[:], R2[:, 0, 3, :]), r=[R2.b], w=[Rp.b])
                cur = R2
                for i in range(3):
                    cx.op("dve", lambda e, cur=cur: e.tensor_copy(pcb[:], cur[:]), r=[cur.b], w=[pcb.b])
                    cx.op("dve", lambda e, i=i: e.tensor_copy(pcs[i][:], pcb[:]), r=[pcb.b], w=[pcs[i].b])
                    if i < 2:
                        cx.op("dve", lambda e, cur=cur, i=i: e.tensor_tensor(out=dd[:], in0=cur[:], in1=pcs[i][:],
                                                                             op=ALU.subtract),
                              r=[cur.b, pcs[i].b], w=[dd.b])
                        cur = dd
                self._fox_sel(selk, pcs, 0, sel, 0)
                self._fox_sel(selq, pcs, 1, sel, 4)
                cx.op("dve", lambda e: e.tensor_copy(
                    kTa[64:70, :, t * 128:(t + 1) * 128].rearrange("p h (c u) -> p h c u", u=32),
                    selk[64:70, :, :].rearrange("p c h -> p h c").unsqueeze(3).to_broadcast([6, HP, 4, 32])),
                    r=[selk.b], w=[kTa.b])
                cx.op("dve", lambda e: e.tensor_copy(
                    qTa[64:70, :, tb * 128:(tb + 1) * 128].rearrange("p h (c u) -> p h c u", u=32),
                    selq[64:70, :, :].rearrange("p c h -> p h c").unsqueeze(3).to_broadcast([6, HP, 4, 32])),
                    r=[selq.b], w=[qTa.b])
            if J == 0 and p == 0:
                self.dump("qTa", qTa[0:70, 0, :], qTa.b, 70, 512)
                self.dump("kTa", kTa[0:70, 0, 0:512], kTa.b, 70, 512)
                self.dump("Vp", Vp[:, 0:4, 0, :], Vp.b, 128, 260)
                self.dump("sgT", sgT[:, 0, :], sgT.b, 64, 512)
            nkb = 4 * J + 4
            for h in range(HP):
                acc = self.pacc[h % 2]

                def emit_qk(i, h=h):
                    a = i - 4 * J
                    c0 = max(a, 0) * 128
                    ncol = 512 - c0
                    ps = self.pf()
                    fns = [lambda e, ps=ps, i=i, c0=c0, ncol=ncol, a=a: e.matmul(
                        ps[:, 0:ncol], lhsT=kTa[0:70, h, i * 128:(i + 1) * 128], rhs=qTa[0:70, h, c0:512],
                        start=True, stop=(a < 0))]
                    if a >= 0:
                        fns.append(lambda e, ps=ps: e.matmul(ps[:, 0:128], lhsT=self.ident_b[:], rhs=negm[:],
                                                             start=False, stop=True))
                    cx.mm(fns, r=[kTa.b, qTa.b, self.ident_b.b, negm.b], w=[ps.b])
                    return ps, c0, ncol

                nxt = emit_qk(0)
                for i in range(nkb):
                    ps, c0, ncol = nxt
                    if i + 1 < nkb:
                        nxt = emit_qk(i + 1)
                    pt_ = Pt[self._pti % 3]
                    self._pti += 1
                    cx.op("act", lambda e, ps=ps, pt_=pt_, ncol=ncol: e.activation(
                        out=pt_[:, 0:ncol], in_=ps[:, 0:ncol], func=AF.Exp), r=[ps.b], w=[pt_.b])
                    cx.mm([lambda e, pt_=pt_, i=i, c0=c0, ncol=ncol: e.matmul(
                        acc[0:65, c0:512], lhsT=Vp[:, i, h, :], rhs=pt_[:, 0:ncol],
                        start=(i == 0), stop=(i == nkb - 1))], r=[Vp.b, pt_.b], w=[acc.b])
                cx.op("act", lambda e, acc=acc: e.copy(osb[:], acc[0:65, :]), r=[acc.b], w=[osb.b])
                if J == 0 and p == 0 and h == 0:
                    self.dump("osb", osb[:, :], osb.b, 65, 512)
                cx.op("dve", lambda e: e.reciprocal(osb[64:65, :], osb[64:65, :]), r=[osb.b], w=[osb.b])
                pbc = self.pf()
                self._split(osb[64:65, :], osb.b, rdh[64:65, :], rdh.b, rdl[64:65, :], rdl.b)
                cx.mm([lambda e: e.matmul(pbc[0:64, :], lhsT=self.ones_b[64:65, 0:64], rhs=rdh[64:65, :],
                                          start=True, stop=False),
                       lambda e: e.matmul(pbc[0:64, :], lhsT=self.ones_b[64:65, 0:64], rhs=rdl[64:65, :],
                                          start=False, stop=True)], r=[rdh.b, rdl.b, self.ones_b.b], w=[pbc.b])
                cx.op("dve", lambda e: e.tensor_tensor(out=onr[:], in0=osb[0:64, :], in1=pbc[0:64, :], op=ALU.mult),
                      r=[osb.b, pbc.b], w=[onr.b])
                if J == 0 and p == 0 and h == 0:
                    self.dump("onr", onr[:, :], onr.b, 64, 512)
                cx.op("pool", lambda e, h=h: e.tensor_tensor(out=ogT[:, h, :], in0=onr[:], in1=sgT[:, h, :],
                                                             op=ALU.mult), r=[onr.b, sgT.b], w=[ogT.b])
            for tb in range(4):
                t = 4 * J + tb
                xq = self.xp[t % 2]
                cx.dma_in("sp", xq[:], (src if prev is None else prev)[t * 128:(t + 1) * 128, :], xq.b)
                self._epilogue(t, prev, dst, is_final, ogT, HP, w_out, kparts=64,
                               tsl=slice(tb * 128, (tb + 1) * 128), xprev=xq)

    def _fox_sel(self, out, pcs, side, sel, c0):
        cx = self.cx
        cx.op("dve", lambda e: e.tensor_scalar(out[:], pcs[0][:, side, :, :], sel[:, c0:c0 + 1], sel[:, c0 + 3:c0 + 4],
                                               ALU.mult, ALU.add), r=[pcs[0].b, sel.b], w=[out.b])
        for i in (1, 2):
            cx.op("dve", lambda e, i=i: e.scalar_tensor_tensor(
                out=out[:], in0=pcs[i][:, side, :, :], scalar=sel[:, c0 + i:c0 + i + 1],
                in1=out[:], op0=ALU.mult, op1=ALU.add),
                r=[pcs[i].b, sel.b, out.b], w=[out.b])

    def _gla_pass(self, j, half, src, prev, dst, is_final):
        nc, cx = self.nc, self.cx
        mk = self.mk
        NB = self.NB
        h0 = 2 * half
        dc = self.dconst
        w_in = mk("gw_in", [128, 8, 1552], BF16)
        w_out = mk("gw_out", [128, 4, 1024], BF16)
        w2 = mk("gw2", [128, 256], BF16)
        bg = mk("gbg", [128, 256])
        tri = mk("gtri", [128, 128])
        trib = mk("gtrib", [128, 128], BF16)
        lah = mk("glah", [128, 256], BF16)
        lal = mk("glal", [128, 256], BF16)
        gmask = mk("gmask", [128, 2, 128])
        S32 = mk("gS32", [128, 2, 256])
        Sb = mk("gSb", [128, 2, 256], BF16)
        rT = mk("grT", [128, 128], BF16)
        e1 = mk("ge1", [128, 256])
        la = mk("gla", [128, 256])
        ebT = mk("gebT", [128, 2, 128])
        enbT = mk("genbT", [128, 2, 128])
        kuf = mk("gkuf", [128, 2, 128])
        cbl = mk("gcbl", [128, 2])
        dec = mk("gdec", [128, 2])
        qeT = mk("gqeT", [128, 2, 128], BF16)
        qnT = mk("gqnT", [128, 2, 128], BF16)
        keT = mk("gkeT", [128, 2, 128], BF16)
        kpT = mk("gkpT", [128, 2, 128], BF16)
        kuT = mk("gkuT", [128, 2, 128], BF16)
        ku = mk("gku", [128, 256], BF16)
        pm = mk("gpm", [128, 2, 2, 128])
        P = mk("gP", [128, 2, 128], BF16)
        v = mk("gv", [128, 512], BF16)
        sg = mk("gsg", [128, 512], BF16)
        ssq = mk("gssq", [128, 4])
        on = mk("gon", [128, 512], BF16)
        og = mk("gog", [128, 512], BF16)
        ogT = mk("gogT", [128, 4, 128], BF16)
        scale = 128.0 ** -0.5

        win = self.gla_w_in[j].rearrange("(kc p) n -> p kc n", p=128)
        first = True
        for (d0, s0, n) in ((0, h0 * 128, 256), (256, 512 + h0 * 128, 256), (512, 1024 + h0 * 256, 512),
                            (1024, 2048 + h0 * 256, 512), (1536, 3072, 16)):
            for kc in range(8):
                cx.dma_in("pool", w_in[:, kc, d0:d0 + n], win[:, kc, s0:s0 + n], w_in.b, cont=not first)
                first = False
        wout = self.gla_w_out[j].rearrange("(ec p) n -> p ec n", p=128)
        for ec in range(4):
            cx.dma_in("pool", w_out[:, ec, :], wout[:, h0 * 2 + ec, :], w_out.b, cont=(ec > 0))
        cx.op("pool", lambda e: e.memset(w2[:], 0.0), w=[w2.b])
        cx.op("pool", lambda e: e.memset(rT[:], 0.0), w=[rT.b])
        cx.dma_in("pool", w2[0:16, :], self.gla_w_gate2[j][:, h0 * 128:h0 * 128 + 256], w2.b)
        cx.dma_in("sp", bg[:], self.gla_b_gate[j:j + 1, h0 * 128:h0 * 128 + 256].partition_broadcast(128), bg.b)
        cx.dma_in("sp", tri[:], dc["gla_tri"][:, :], tri.b)
        cx.op("dve", lambda e: e.tensor_copy(trib[:], tri[:]), r=[tri.b], w=[trib.b])
        cx.dma_in("sp", gmask[:], dc["gla_mask"][:, :, :], gmask.b)
        cx.op("pool", lambda e: e.memset(S32[:], 0.0), w=[S32.b])
        cx.op("pool", lambda e: e.memset(Sb[:], 0.0), w=[Sb.b])

        self._load_x(0, src, prev, half)
        self._prologue(0)
        for t in range(NB):
            if t + 1 < NB:
                self._load_x(t + 1, src, prev, half)
            hT = self.hT
            for _once in (0,):
                pqk = self.pf()
                fns = []
                for c in range(4):
                    for kc in range(8):
                        fns.append(lambda e, c=c, kc=kc: e.matmul(
                            pqk[:, c * 128:(c + 1) * 128], lhsT=w_in[:, kc, c * 128:(c + 1) * 128],
                            rhs=hT[:, kc, :], start=(kc == 0), stop=(kc == 7)))
                cx.mm(fns, r=[w_in.b, hT.b], w=[pqk.b])
                if self.dbg_stage < 1:
                    break
                pr = self.pf()
                cx.mm([lambda e, kc=kc: e.matmul(pr[0:16, 0:128], lhsT=w_in[:, kc, 1536:1552], rhs=hT[:, kc, :],
                                                 start=(kc == 0), stop=(kc == 7)) for kc in range(8)],
                      r=[w_in.b, hT.b], w=[pr.b])
                cx.op("dve", lambda e: e.tensor_copy(rT[0:16, :], pr[0:16, 0:128]), r=[pr.b], w=[rT.b])
                if self.dbg_stage < 2:
                    break
                pz = self.pf()
                import os as _os
                if _os.environ.get("DBG_VAR", "") == "A":
                    cx.mm([lambda e: e.matmul(pz[:, 0:256], lhsT=hT[:, 0, :], rhs=w_in[:, 0, 0:256], start=True, stop=True)],
                          r=[hT.b, w_in.b], w=[pz.b])
                else:
                    cx.mm([lambda e: e.matmul(pz[:, 0:256], lhsT=rT[:], rhs=w2[:], start=True, stop=True)],
                          r=[rT.b, w2.b], w=[pz.b])
                if _os.environ.get("DBG_VAR", "") == "B":
                    cx.op("dve", lambda e: e.tensor_tensor(out=e1[:], in0=pz[:, 0:256], in1=self.G_bc[:, 0:256], op=ALU.add),
                          r=[pz.b, self.G_bc.b], w=[e1.b])
                else:
                    cx.op("dve", lambda e: e.tensor_tensor(out=e1[:], in0=pz[:, 0:256], in1=bg[:], op=ALU.add),
                          r=[pz.b, bg.b], w=[e1.b])
                cx.op("act", lambda e: e.activation(out=e1[:], in_=e1[:], func=AF.Exp, scale=-1.0),
                      r=[e1.b], w=[e1.b])
                cx.op("dve", lambda e: e.tensor_scalar(e1[:], e1[:], 1.0, None, ALU.add), r=[e1.b], w=[e1.b])
                cx.op("act", lambda e: e.activation(out=la[:], in_=e1[:], func=AF.Ln), r=[e1.b], w=[la.b])
                if self.dbg_stage < 3:
                    break
                pcb = self.pf()
                self._split(la[:], la.b, lah[:], lah.b, lal[:], lal.b)
                fns = []
                for h in range(2):
                    fns.append(lambda e, h=h: e.matmul(pcb[:, h * 128:(h + 1) * 128], lhsT=lah[:, h * 128:(h + 1) * 128],
                                                       rhs=trib[:], start=True, stop=False))
                    fns.append(lambda e, h=h: e.matmul(pcb[:, h * 128:(h + 1) * 128], lhsT=lal[:, h * 128:(h + 1) * 128],
                                                       rhs=trib[:], start=False, stop=True))
                cx.mm(fns, r=[lah.b, lal.b, trib.b], w=[pcb.b])
                pcv = pcb[:, 0:256].rearrange("p (h t) -> p h t", h=2)
                cx.op("act", lambda e: e.activation(out=ebT[:], in_=pcv, func=AF.Exp), r=[pcb.b], w=[ebT.b])
                cx.op("act", lambda e: e.activation(out=enbT[:], in_=pcv, func=AF.Exp, scale=-1.0),
                      r=[pcb.b], w=[enbT.b])
                cx.op("dve", lambda e: e.tensor_copy(cbl[:], pcv[:, :, 127]), r=[pcb.b], w=[cbl.b])
                cx.op("act", lambda e: e.activation(out=dec[:], in_=cbl[:], func=AF.Exp), r=[cbl.b], w=[dec.b])
                for h in range(2):
                    cx.op("act", lambda e, h=h: e.activation(out=kuf[:, h, :], in_=pcv[:, h, :], func=AF.Exp,
                                                             scale=-1.0, bias=cbl[:, h:h + 1]),
                          r=[pcb.b, cbl.b], w=[kuf.b])
                if self.dbg_stage < 4:
                    break
                qv = pqk[:, 0:256].rearrange("p (h t) -> p h t", h=2)
                kv = pqk[:, 256:512].rearrange("p (h t) -> p h t", h=2)
                cx.op("dve", lambda e: e.scalar_tensor_tensor(out=qeT[:], in0=qv, scalar=scale, in1=ebT[:],
                                                              op0=ALU.mult, op1=ALU.mult),
                      r=[pqk.b, ebT.b], w=[qeT.b])
                cx.op("dve", lambda e: e.scalar_tensor_tensor(out=qnT[:], in0=qv, scalar=scale, in1=enbT[:],
                                                              op0=ALU.mult, op1=ALU.mult),
                      r=[pqk.b, enbT.b], w=[qnT.b])
                cx.op("dve", lambda e: e.tensor_tensor(out=keT[:], in0=kv, in1=enbT[:], op=ALU.mult),
                      r=[pqk.b, enbT.b], w=[keT.b])
                cx.op("dve", lambda e: e.tensor_tensor(out=kpT[:], in0=kv, in1=ebT[:], op=ALU.mult),
                      r=[pqk.b, ebT.b], w=[kpT.b])
                cx.op("dve", lambda e: e.tensor_tensor(out=kuT[:], in0=kv, in1=kuf[:], op=ALU.mult),
                      r=[pqk.b, kuf.b], w=[kuT.b])
                if self.dbg_stage < 5:
                    break
                pa = self.pf()
                fns = []
                for h in range(2):
                    fns.append(lambda e, h=h: e.matmul(pa[:, h * 128:(h + 1) * 128], lhsT=keT[:, h, :],
                                                       rhs=qeT[:, h, :], start=True, stop=True))
                    fns.append(lambda e, h=h: e.matmul(pa[:, 256 + h * 128:256 + (h + 1) * 128], lhsT=kpT[:, h, :],
                                                       rhs=qnT[:, h, :], start=True, stop=True))
                cx.mm(fns, r=[keT.b, qeT.b, kpT.b, qnT.b], w=[pa.b])
                cx.op("dve", lambda e: e.tensor_tensor(
                    out=pm[:], in0=pa[:].rearrange("p (y h c) -> p y h c", y=2, h=2),
                    in1=gmask[:].unsqueeze(2).to_broadcast([128, 2, 2, 128]), op=ALU.mult),
                    r=[pa.b, gmask.b], w=[pm.b])
                cx.op("pool", lambda e: e.tensor_tensor(out=P[:], in0=pm[:, 0, :, :], in1=pm[:, 1, :, :], op=ALU.add),
                      r=[pm.b], w=[P.b])
                if self.dbg_stage < 6:
                    break
                pv = self.pf()
                cx.mm([lambda e, kc=kc: e.matmul(pv[:], lhsT=hT[:, kc, :], rhs=w_in[:, kc, 512:1024],
                                                 start=(kc == 0), stop=(kc == 7)) for kc in range(8)],
                      r=[w_in.b, hT.b], w=[pv.b])
                cx.op("act", lambda e: e.copy(v[:], pv[:]), r=[pv.b], w=[v.b])
                pg = self.pf()
                cx.mm([lambda e, kc=kc: e.matmul(pg[:], lhsT=hT[:, kc, :], rhs=w_in[:, kc, 1024:1536],
                                                 start=(kc == 0), stop=(kc == 7)) for kc in range(8)],
                      r=[w_in.b, hT.b], w=[pg.b])
                cx.op("act", lambda e: e.activation(out=sg[:], in_=pg[:], func=AF.Silu), r=[pg.b], w=[sg.b])
                if t + 1 < NB:
                    self._prologue(t + 1)
                if self.dbg_stage < 7:
                    break
                po = self.pf()
                fns = []
                for h in range(2):
                    fns.append(lambda e, h=h: e.matmul(po[:, h * 256:(h + 1) * 256], lhsT=P[:, h, :],
                                                       rhs=v[:, h * 256:(h + 1) * 256], start=True, stop=False))
                    fns.append(lambda e, h=h: e.matmul(po[:, h * 256:(h + 1) * 256], lhsT=qeT[:, h, :],
                                                       rhs=Sb[:, h, :], start=False, stop=True))
                cx.mm(fns, r=[P.b, v.b, qeT.b, Sb.b], w=[po.b])
                if self.dbg_stage < 8:
                    break
                for h in range(2):
                    cx.op("act", lambda e, h=h: e.activation(out=self.junk[:, 0:256], in_=po[:, h * 256:(h + 1) * 256],
                                                             func=AF.Square, accum_out=ssq[:, h:h + 1]),
                          r=[po.b], w=[self.junk.b, ssq.b])
                cx.op("dve", lambda e: e.tensor_scalar(ssq[:, 0:2], ssq[:, 0:2], 1.0 / 256.0, EPS, ALU.mult, ALU.add),
                      r=[ssq.b], w=[ssq.b])
                self._rsqrt(ssq[:, 2:4], ssq[:, 0:2], 2, [ssq.b], [ssq.b])
                for h in range(2):
                    cx.op("act", lambda e, h=h: e.activation(out=on[:, h * 256:(h + 1) * 256],
                                                             in_=po[:, h * 256:(h + 1) * 256], func=AF.Copy,
                                                             scale=ssq[:, 2 + h:3 + h]),
                          r=[po.b, ssq.b], w=[on.b])
                cx.op("pool", lambda e: e.tensor_tensor(out=og[:], in0=on[:], in1=sg[:], op=ALU.mult),
                      r=[on.b, sg.b], w=[og.b])
                if self.dbg_stage < 9:
                    break
                pk = self.pt()
                cx.mm([lambda e, h=h: e.transpose(pk[:, h * 128:(h + 1) * 128], kuT[:, h, :], self.ident_b[:])
                       for h in range(2)], r=[kuT.b, self.ident_b.b], w=[pk.b])
                cx.op("act", lambda e: e.copy(ku[:], pk[:, 0:256]), r=[pk.b], w=[ku.b])
                pd = self.pf()
                cx.mm([lambda e, h=h: e.matmul(pd[:, h * 256:(h + 1) * 256], lhsT=ku[:, h * 128:(h + 1) * 128],
                                               rhs=v[:, h * 256:(h + 1) * 256], start=True, stop=True)
                       for h in range(2)], r=[ku.b, v.b], w=[pd.b])
                for h in range(2):
                    cx.op("dve", lambda e, h=h: e.scalar_tensor_tensor(
                        out=S32[:, h, :], in0=S32[:, h, :], scalar=dec[:, h:h + 1], in1=pd[:, h * 256:(h + 1) * 256],
                        op0=ALU.mult, op1=ALU.add), r=[pd.b, S32.b, dec.b], w=[S32.b])
                cx.op("pool", lambda e: e.tensor_copy(Sb[:], S32[:]), r=[S32.b], w=[Sb.b])
                if self.dbg_stage < 10:
                    break
                pt = self.pt()
                cx.mm([lambda e, c=c: e.transpose(pt[:, c * 128:(c + 1) * 128], og[:, c * 128:(c + 1) * 128],
                                                  self.ident_b[:]) for c in range(4)],
                      r=[og.b, self.ident_b.b], w=[pt.b])
                cx.op("act", lambda e: e.copy(ogT[:].rearrange("p c t -> p (c t)"), pt[:, 0:512]),
                      r=[pt.b], w=[ogT.b])
            self._epilogue(t, prev, dst, is_final, ogT, 4, w_out)


def _core_inputs(b, inp, cst, S):
    f = lambda a: np.ascontiguousarray(np.asarray(a, dtype=np.float32))
    m = {}
    m["x"] = f(inp["x"][b, :S])
    m["c_col"] = f(np.asarray(inp["c"])[b].reshape(8, 128).T)
    m["posf"] = f(np.asarray(inp["positions"])[b, :S].astype(np.float32).reshape(1, S))
    m["mod_w"] = f(inp["mod_w"])
    m["mod_b_col"] = f(np.asarray(inp["mod_b"]).reshape(4, 24, 128).transpose(2, 0, 1))
    m["norm_g_col"] = f(np.asarray(inp["norm_g"]).reshape(4, 8, 128).transpose(2, 0, 1))
    m["final_g_col"] = f(np.asarray(inp["final_g"]).reshape(8, 128).T)
    for k in ("ret_w_in", "ret_w_out", "fox_w_in", "fox_b_f", "fox_w_out", "gla_w_in",
              "gla_w_gate2", "gla_b_gate", "gla_w_out"):
        m[k] = f(inp[k])
    m["fox_qg_col"] = f(np.asarray(inp["fox_q_gain"]).reshape(64, 1))
    m["fox_kg_col"] = f(np.asarray(inp["fox_k_gain"]).reshape(64, 1))
    for k, v in cst.items():
        if isinstance(v, np.ndarray) and v.dtype == np.float32:
            m["k_" + k] = v
    return m


def run(inputs, S=4096, layers=(0, 1, 2, 3), final=True, trace=False):
    prog = Prog(S, list(layers), final)
    nc = prog.build()
    B = np.asarray(inputs["x"]).shape[0]
    in_maps = [_core_inputs(b % B, inputs, prog.cst, S) for b in range(N_CORES)]
    res = run_bass_kernel_spmd(nc, in_maps, core_ids=list(range(N_CORES)), trace=trace)
    out = np.stack([np.asarray(res.results[b]["out"]) for b in range(B)], 0)
    if prog.dbg_on:
        d = np.asarray(res.results[0]["dbg"])
        prog.dbg_vals = {k: d[0:P, off:off + n] for k, (off, P, n) in prog.dbg_map.items()}
    return out.astype(np.float32), res, prog


def kernel(**inputs):
    out, _, _ = run(inputs)
    return out
```

```python
import contextlib
import numpy as np
import concourse.bass as bass
import concourse.mybir as mybir
from concourse.bass_utils import run_bass_kernel_spmd

F32 = mybir.dt.float32
BF16 = mybir.dt.bfloat16
I32 = mybir.dt.int32
AF = mybir.ActivationFunctionType
ALU = mybir.AluOpType
AX = mybir.AxisListType

D = 1024
EPS = 1e-6
N_CORES = 8


class Buf:
    __slots__ = ("name", "w", "r", "dsem", "dcnt", "osem", "ocnt", "psum")

    def __init__(self, name):
        self.name = name
        self.psum = False
        self.w = None
        self.r = {}
        self.dsem = None
        self.dcnt = 0
        self.osem = None
        self.ocnt = 0


class Ctx:
    def __init__(self, nc):
        self.nc = nc
        self.eng = {"pe": nc.tensor, "dve": nc.vector, "act": nc.scalar,
                    "pool": nc.gpsimd, "sp": nc.sync}
        self.sem = {k: nc.alloc_semaphore("sem_" + k) for k in self.eng}
        self.cnt = {k: 0 for k in self.eng}
        self.seen = {k: {} for k in self.eng}
        self.nsem = 0
        self.ninst = 0
        self.dma_ev = {}
        self.log = {k: [] for k in self.eng}
        self.semname = {id(v): "sem_" + k for k, v in self.sem.items()}

    def newsem(self, name):
        self.nsem += 1
        sm = self.nc.alloc_semaphore(f"{name}_{self.nsem}")
        self.semname[id(sm)] = f"{name}_{self.nsem}"
        self._keep = getattr(self, "_keep", []) + [sm]
        return sm

    def _wait(self, e, ev):
        sem, val = ev
        key = id(sem)
        if e == "pe" and sem is self.sem["pe"]:
            return
        if self.seen[e].get(key, 0) >= val:
            return
        self.eng[e].wait_ge(sem, val)
        self.log[e].append(("W", self.semname[key], val))
        self.seen[e][key] = val
        self.ninst += 1

    def _deps(self, e, r, w):
        for b in r:
            if b.w is not None:
                self._wait(e, b.w)
            if b.psum:
                for k, ev in b.r.items():
                    if k != e:
                        self._wait(e, ev)
        for b in w:
            if b.w is not None:
                self._wait(e, b.w)
            for ev in b.r.values():
                self._wait(e, ev)

    def op(self, e, fn, r=(), w=()):
        self._deps(e, r, w)
        ins = fn(self.eng[e])
        self.cnt[e] += 1
        ins.then_inc(self.sem[e], 1)
        self.log[e].append(("I", "sem_" + e, 1))
        ev = (self.sem[e], self.cnt[e])
        for b in r:
            b.r[e] = ev
        for b in w:
            b.w = ev
            b.r = {}
        self.ninst += 1
        return ins

    def mm(self, fns, r=(), w=()):
        self._deps("pe", r, w)
        ins = None
        for fn in fns:
            ins = fn(self.eng["pe"])
            self.ninst += 1
        self.cnt["pe"] += 1
        ins.then_inc(self.sem["pe"], 1)
        self.log["pe"].append(("I", "sem_pe", 1))
        ev = (self.sem["pe"], self.cnt["pe"])
        for b in r:
            b.r["pe"] = ev
        for b in w:
            b.w = ev
            b.r = {}

    def dma_in(self, q, out_ap, in_ap, w, cont=False):
        if not cont:
            self._deps(q, (), (w,))
        if w.dsem is None:
            w.dsem = self.newsem("d")
        self.eng[q].dma_start(out=out_ap, in_=in_ap).then_inc(w.dsem, 16)
        self.log[q].append(("I", self.semname[id(w.dsem)], 16))
        w.dcnt += 16
        self.dma_ev[id(w.dsem)] = (w.dsem, w.dcnt)
        w.w = (w.dsem, w.dcnt)
        w.r = {}
        self.ninst += 1

    def dma_out(self, q, out_ap, in_ap, r):
        self._deps(q, (r,), ())
        if r.osem is None:
            r.osem = self.newsem("o")
        self.eng[q].dma_start(out=out_ap, in_=in_ap).then_inc(r.osem, 16)
        self.log[q].append(("I", self.semname[id(r.osem)], 16))
        r.ocnt += 16
        self.dma_ev[id(r.osem)] = (r.osem, r.ocnt)
        r.r["dmaout"] = (r.osem, r.ocnt)
        self.ninst += 1

    def barrier(self):
        evs = [(self.sem[k], self.cnt[k]) for k in self.eng if self.cnt[k] > 0]
        evs += list(self.dma_ev.values())
        for e in self.eng:
            for ev in evs:
                self._wait(e, ev)

    def flush_out(self, q, bufs):
        for b in bufs:
            if b.osem is not None and b.ocnt > 0:
                self._wait(q, (b.osem, b.ocnt))


class T:
    def __init__(self, nc, name, shape, dt, psum=False, stack=None):
        if psum:
            self.t = nc.alloc_psum_tensor(name, list(shape), dt)
        elif stack is not None:
            self.t = stack.enter_context(nc.sbuf_tensor(name, list(shape), dt))
        else:
            self.t = nc.alloc_sbuf_tensor(name, list(shape), dt)
        self.b = Buf(name)
        self.b.psum = psum
        self.shape = shape

    def __getitem__(self, idx):
        return self.t[idx]


def _cw_consts():
    two_pi = 2.0 * np.pi
    c1 = 6.28125
    r = two_pi - c1
    c2 = float(np.float32(np.round(r * 2 ** 19) / 2 ** 19))
    c3 = float(np.float32(two_pi - c1 - c2))
    return c1, c2, c3


def _consts():
    cst = {}
    cst["ident"] = np.eye(128, dtype=np.float32)
    half = 128
    inv = (np.float32(10000.0) ** (-(np.arange(half, dtype=np.float32) / np.float32(half)))).astype(np.float32)
    cst["inv"] = inv.reshape(128, 1)
    lg = np.log1p(-np.exp2(-5.0 - np.arange(4, dtype=np.float64)))
    t = np.arange(128)
    m_ = t[:, None]
    c_ = t[None, :]
    same = (m_ // 64) == (c_ // 64)
    maskT = np.zeros((4, 128, 128), np.float64)
    for h in range(4):
        d1 = np.exp(lg[h] * np.abs(c_ - m_))
        d2 = np.exp(lg[h] * (c_ - m_))
        maskT[h] = np.where(same, d1, np.where(c_ > m_, d2, 0.0)) * (256.0 ** -0.5)
    cst["ret_maskT"] = np.ascontiguousarray(maskT.transpose(1, 0, 2)).astype(np.float32)
    gq = np.stack([np.exp(lg[h] * (t + 1.0)) * (256.0 ** -0.5) for h in range(4)], 0)
    cst["ret_gq"] = np.broadcast_to(gq[None], (128, 4, 128)).astype(np.float32).copy()
    gk = np.stack([np.exp(lg[h] * (127.0 - t)) for h in range(4)], 1)
    cst["ret_gk"] = gk.astype(np.float32)
    cst["ret_sdec"] = np.exp(lg * 128.0).astype(np.float64)
    cst["gla_tri"] = (np.where(m_ <= c_, 1.0, 0.0) * (-1.0 / 16.0)).astype(np.float32)
    gm = np.zeros((128, 2, 128), np.float32)
    gm[:, 0, :] = np.where(c_ >= m_, 1.0, 0.0)
    gm[:, 1, :] = np.where((m_ > c_) & same, 1.0, 0.0)
    cst["gla_mask"] = gm
    same32 = (m_ // 32) == (c_ // 32)
    cst["fox_sfx"] = np.where((m_ > c_) & same32, 1.0, 0.0).astype(np.float32)
    ci = np.zeros((128, 8, 70), np.float32)
    for c in range(4):
        ci[:, c, :] = (t < (c + 1) * 32)[:, None]
        ci[:, 4 + c, :] = (t < c * 32)[:, None]
    cst["fox_cumind"] = ci.reshape(128, 560)
    cst["fox_negmask"] = np.where(m_ > c_, -30000.0, 0.0).astype(np.float32)
    sel = np.zeros((70, 8), np.float32)
    sel[64, 0] = 1; sel[65, 1] = 1; sel[66, 2] = 1; sel[67:70, 3] = 1
    sel[67, 4] = -1; sel[68, 5] = -1; sel[69, 6] = -1; sel[64:67, 7] = 1
    cst["fox_sel"] = sel
    return cst


class Prog:
    def __init__(self, S, layers, final=True):
        self.S = S
        self.NB = S // 128
        self.layers = layers
        self.final = final
        self.cst = _consts()
        import os
        self.dbg_stage = int(os.environ.get('DBG_STAGE', '99'))

    def build(self):
        S, NB = self.S, self.NB
        nc = bass.Bass("TRN2", target_bir_lowering=False)
        self.nc = nc
        cx = Ctx(nc)
        self.cx = cx

        def din(name, shape, dt=F32):
            return nc.dram_tensor(name, list(shape), dt, kind="ExternalInput").ap()

        self.x_in = din("x", [S, D])
        self.c_col = din("c_col", [128, 8])
        self.posf = din("posf", [1, S])
        self.mod_w = din("mod_w", [4, D, 3 * D])
        self.mod_b_col = din("mod_b_col", [128, 4, 24])
        self.norm_g_col = din("norm_g_col", [128, 4, 8])
        self.final_g_col = din("final_g_col", [128, 8])
        self.ret_w_in = din("ret_w_in", [2, D, 6144])
        self.ret_w_out = din("ret_w_out", [2, 2048, D])
        self.fox_w_in = din("fox_w_in", [1, D, 4112])
        self.fox_b_f = din("fox_b_f", [1, 16])
        self.fox_qg_col = din("fox_qg_col", [64, 1])
        self.fox_kg_col = din("fox_kg_col", [64, 1])
        self.fox_w_out = din("fox_w_out", [1, D, D])
        self.gla_w_in = din("gla_w_in", [1, D, 3088])
        self.gla_w_gate2 = din("gla_w_gate2", [1, 16, 512])
        self.gla_b_gate = din("gla_b_gate", [1, 512])
        self.gla_w_out = din("gla_w_out", [1, D, D])
        self.dconst = {k: din("k_" + k, v.shape) for k, v in self.cst.items()
                       if isinstance(v, np.ndarray) and v.dtype == np.float32}
        self.out = nc.dram_tensor("out", [S, D], F32, kind="ExternalOutput").ap()
        import os
        self.dbg_on = os.environ.get("DBG_DUMP", "") == "1"
        self.dbg_map = {}
        self.dbg_off = 0
        if self.dbg_on:
            self.dbg = nc.dram_tensor("dbg", [128, 8192], F32, kind="ExternalOutput").ap()
        self.xs = [nc.dram_tensor(f"xs{i}", [S, D], F32, kind="Internal").ap() for i in range(3)]
        self.rope_d = nc.dram_tensor("rope_d", [2, 128, S], F32, kind="Internal").ap()

        self._alloc_common()
        with contextlib.ExitStack() as stk:
            self.pstack = stk
            self._startup()
            cx.barrier()
        src = self.x_in
        prev = None
        NP = {0: 2, 1: 4, 2: 2}
        total = sum(NP[l % 3] for l in self.layers)
        pi = 0
        for l in self.layers:
            kind = l % 3
            self._layer_mod(l)
            for p in range(NP[kind]):
                pi += 1
                is_final = (pi == total) and self.final
                if is_final:
                    dst = self.out
                elif pi == total:
                    dst = self.out
                else:
                    dst = [b for b in self.xs if b is not src and b is not prev][0]
                with contextlib.ExitStack() as stk:
                    self.pstack = stk
                    cx.flush_out("sp", self.all_out_bufs)
                    cx.flush_out("pool", self.all_out_bufs)
                    fn = (self._ret_pass, self._fox_pass, self._gla_pass)[kind]
                    fn(l // 3, p, src, prev, dst, is_final)
                    cx.barrier()
                prev = dst
            src = prev
            prev = None
        cx.flush_out("sp", self.all_out_bufs)
        return nc

    def dump(self, name, ap, buf, P, n):
        if not self.dbg_on or name in self.dbg_map:
            return
        cx = self.cx
        st = self.dbg_st[len(self.dbg_map)]
        cx.op("dve", lambda e: e.tensor_copy(st[0:P, 0:n], ap), r=[buf], w=[st.b])
        cx.dma_out("sp", self.dbg[0:P, self.dbg_off:self.dbg_off + n], st[0:P, 0:n], st.b)
        self.all_out_bufs.append(st.b)
        self.dbg_map[name] = (self.dbg_off, P, n)
        self.dbg_off += n

    def mk(self, name, shape, dt=F32):
        self._uid = getattr(self, "_uid", 0) + 1
        return T(self.nc, f"{name}_{self._uid}", shape, dt, stack=self.pstack)

    def _alloc_common(self):
        nc, cx = self.nc, self.cx
        S = self.S
        mk = lambda name, shape, dt=F32: T(nc, name, shape, dt)
        self.ident_f = mk("ident_f", [128, 128])
        self.ident_b = mk("ident_b", [128, 128], BF16)
        self.ones_f = mk("ones_f", [128, 128])
        self.ones_b = mk("ones_b", [128, 128], BF16)
        self.inv = mk("inv", [128, 1])
        self.modcol = mk("modcol", [128, 4, 24])
        self.ngcol = mk("ngcol", [128, 4, 8])
        self.fgcol = mk("fgcol", [128, 8])
        self.G_bc = mk("G_bc", [128, D])
        self.shift_bc = mk("shift_bc", [128, D])
        self.gate_bc = mk("gate_bc", [128, D])
        self.fg_bc = mk("fg_bc", [128, D])
        self.xt = [mk(f"xt{i}", [128, D]) for i in range(3)]
        self.xp = [mk(f"xp{i}", [128, D]) for i in range(3)]
        self.xo = [mk(f"xo{i}", [128, D]) for i in range(2)]
        self.tmpA = mk("tmpA", [128, D])
        self.hb = mk("hb", [128, D], BF16)
        self.hT = mk("hT", [128, 8, 128], BF16)
        self.st = mk("stat", [128, 8])
        self.rs_i = mk("rs_i", [128, 16], I32)
        self.rs_u = mk("rs_u", [128, 16])
        self.halfpi = mk("halfpi", [128, 1])
        self.junk = mk("junk", [128, D], BF16)
        self.all_out_bufs = [t.b for t in self.xo]
        if self.dbg_on:
            self.dbg_st = [mk(f"dbgst{i}", [128, 512]) for i in range(8)]
        self.pT = [T(nc, f"pT{i}", [128, 1024], BF16, psum=True) for i in range(2)]
        self.pF = [T(nc, f"pF{i}", [128, 512], F32, psum=True) for i in range(4)]
        self.pacc = [T(nc, f"pacc{i}", [128, 512], F32, psum=True) for i in range(2)]
        self._pti = 0
        self.one_c = mk("one_c", [128, 1])
        self.eps_c = mk("eps_c", [128, 1])
        self._pf_i = 0
        self._pt_i = 0

    def pf(self):
        t = self.pF[self._pf_i % len(self.pF)]
        self._pf_i += 1
        return t

    def pt(self):
        t = self.pT[self._pt_i % 2]
        self._pt_i += 1
        return t

    def _startup(self):
        nc, cx = self.nc, self.cx
        S = self.S
        dc = self.dconst
        cx.dma_in("sp", self.ident_f[:], dc["ident"][:, :], self.ident_f.b)
        cx.dma_in("sp", self.inv[:], dc["inv"][:, :], self.inv.b)
        cx.dma_in("sp", self.modcol[:], self.mod_b_col[:, :, :], self.modcol.b)
        cx.dma_in("sp", self.ngcol[:], self.norm_g_col[:, :, :], self.ngcol.b)
        cx.dma_in("sp", self.fgcol[:], self.final_g_col[:, :], self.fgcol.b)
        cx.op("dve", lambda e: e.tensor_copy(self.ident_b[:], self.ident_f[:]),
              r=[self.ident_f.b], w=[self.ident_b.b])
        cx.op("pool", lambda e: e.memset(self.ones_f[:], 1.0), w=[self.ones_f.b])
        cx.op("pool", lambda e: e.memset(self.ones_b[:], 1.0), w=[self.ones_b.b])
        cx.op("pool", lambda e: e.memset(self.halfpi[:], float(np.pi / 2)), w=[self.halfpi.b])
        cx.op("pool", lambda e: e.memset(self.one_c[:], 1.0), w=[self.one_c.b])
        cx.op("pool", lambda e: e.memset(self.eps_c[:], EPS), w=[self.eps_c.b])
        ccol = self.mk("ccol", [128, 8])
        cacol = self.mk("cacol", [128, 8])
        cx.dma_in("sp", ccol[:], self.c_col[:, :], ccol.b)
        cx.op("act", lambda e: e.activation(out=cacol[:], in_=ccol[:], func=AF.Silu),
              r=[ccol.b], w=[cacol.b])
        mw = [self.mk(f"mw{i}", [128, 8, 512]) for i in range(2)]
        k = 0
        for l in self.layers:
            pm = self.pf()
            first = True
            for p in range(6):
                buf = mw[k % 2]
                k += 1
                src = self.mod_w[l].rearrange("(kc p) n -> p kc n", p=128)[:, :, p * 512:(p + 1) * 512]
                cx.dma_in("sp", buf[:], src, buf.b)
                fns = []
                for j in range(4):
                    col = p * 4 + j
                    for kc in range(8):
                        fns.append(lambda e, buf=buf, j=j, kc=kc, col=col: e.matmul(
                            pm[:, col:col + 1], lhsT=buf[:, kc, j * 128:(j + 1) * 128],
                            rhs=cacol[:, kc:kc + 1], start=(kc == 0), stop=(kc == 7)))
                cx.mm(fns, r=[buf.b, cacol.b], w=[pm.b])
            cx.op("dve", lambda e, pm=pm, l=l: e.tensor_tensor(
                out=self.modcol[:, l, :], in0=pm[:, 0:24], in1=self.modcol[:, l, :], op=ALU.add),
                r=[pm.b, self.modcol.b], w=[self.modcol.b])
            cx.op("dve", lambda e, l=l: e.scalar_tensor_tensor(
                out=self.modcol[:, l, 8:16], in0=self.modcol[:, l, 8:16], scalar=1.0,
                in1=self.ngcol[:, l, :], op0=ALU.add, op1=ALU.mult),
                r=[self.modcol.b, self.ngcol.b], w=[self.modcol.b])
        if any(l % 3 == 0 for l in self.layers):
            self._rope_tables()
        if self.final:
            self._bcast_cols(self.fgcol, None, [(self.fg_bc, 0)])

    def _rope_tables(self):
        nc, cx = self.nc, self.cx
        S = self.S
        c1, c2, c3 = _cw_consts()
        ang = self.mk("ang", [128, S])
        kf = self.mk("kf", [128, S])
        ki = self.mk("ki", [128, S], I32)
        sinT = self.mk("sinT", [128, S])
        cosT = self.mk("cosT", [128, S])
        cx.dma_in("sp", ang[:], self.posf[0:1, :].partition_broadcast(128), ang.b)
        cx.op("dve", lambda e: e.tensor_scalar(ang[:], ang[:], self.inv[:, 0:1], None, ALU.mult),
              r=[ang.b, self.inv.b], w=[ang.b])
        cx.op("dve", lambda e: e.tensor_scalar(kf[:], ang[:], float(1.0 / (2 * np.pi)), None, ALU.mult),
              r=[ang.b], w=[kf.b])
        cx.op("dve", lambda e: e.tensor_copy(ki[:], kf[:]), r=[kf.b], w=[ki.b])
        cx.op("dve", lambda e: e.tensor_copy(kf[:], ki[:]), r=[ki.b], w=[kf.b])
        for cc in (c1, c2, c3):
            cx.op("dve", lambda e, cc=cc: e.scalar_tensor_tensor(
                out=ang[:], in0=kf[:], scalar=-float(cc), in1=ang[:], op0=ALU.mult, op1=ALU.add),
                r=[kf.b, ang.b], w=[ang.b])
        pi = float(np.pi)
        cx.op("dve", lambda e: e.tensor_scalar(ang[:], ang[:], pi, -pi, ALU.min, ALU.max),
              r=[ang.b], w=[ang.b])
        cx.op("act", lambda e: e.activation(out=sinT[:], in_=ang[:], func=AF.Sin),
              r=[ang.b], w=[sinT.b])
        cx.op("act", lambda e: e.activation(out=kf[:], in_=ang[:], func=AF.Abs),
              r=[ang.b], w=[kf.b])
        cx.op("act", lambda e: e.activation(out=cosT[:], in_=kf[:], func=AF.Sin, scale=-1.0,
                                            bias=self.halfpi[:, 0:1]),
              r=[kf.b, self.halfpi.b], w=[cosT.b])
        cx.dma_out("sp", self.rope_d[0], cosT[:], cosT.b)
        cx.dma_out("sp", self.rope_d[1], sinT[:], sinT.b)
        cx.flush_out("sp", [cosT.b, sinT.b])

    def _wrap(self, a, tmp):
        cx = self.cx
        pi = float(np.pi)
        cx.op("dve", lambda e: e.tensor_scalar(tmp[:], a[:], pi, -2.0 * pi, ALU.is_gt, ALU.mult),
              r=[a.b], w=[tmp.b])
        cx.op("dve", lambda e: e.tensor_tensor(out=a[:], in0=a[:], in1=tmp[:], op=ALU.add),
              r=[a.b, tmp.b], w=[a.b])
        cx.op("dve", lambda e: e.tensor_scalar(tmp[:], a[:], -pi, 2.0 * pi, ALU.is_lt, ALU.mult),
              r=[a.b], w=[tmp.b])
        cx.op("dve", lambda e: e.tensor_tensor(out=a[:], in0=a[:], in1=tmp[:], op=ALU.add),
              r=[a.b, tmp.b], w=[a.b])
        cx.op("dve", lambda e: e.tensor_scalar(a[:], a[:], pi, -pi, ALU.min, ALU.max),
              r=[a.b], w=[a.b])

    def _bcast_cols(self, colT, l, outs):
        cx = self.cx
        for dst, off in outs:
            rep = self.tmpA
            src = colT[:, off:off + 8] if l is None else colT[:, l, off:off + 8]
            cx.op("dve", lambda e, src=src: e.tensor_copy(
                rep[:].rearrange("p (c m) -> p c m", c=8), src.unsqueeze(2).to_broadcast([128, 8, 128])),
                r=[colT.b], w=[rep.b])
            for hf in range(2):
                pm = self.pf()
                fns = []
                for c in range(4):
                    cc = hf * 4 + c
                    fns.append(lambda e, cc=cc, c=c, pm=pm: e.matmul(
                        pm[:, c * 128:(c + 1) * 128], lhsT=rep[:, cc * 128:(cc + 1) * 128],
                        rhs=self.ident_f[:], start=True, stop=True))
                cx.mm(fns, r=[rep.b, self.ident_f.b], w=[pm.b])
                cx.op("act", lambda e, pm=pm, hf=hf, dst=dst: e.copy(dst[:, hf * 512:(hf + 1) * 512], pm[:]),
                      r=[pm.b], w=[dst.b])

    def _layer_mod(self, l):
        self._bcast_cols(self.modcol, l, [(self.shift_bc, 0), (self.G_bc, 8), (self.gate_bc, 16)])

    def _load_x(self, t, src, prev, half):
        cx = self.cx
        xt = self.xt[t % 3]
        cx.dma_in("sp", xt[:], src[t * 128:(t + 1) * 128, :], xt.b)
        if prev is not None:
            xp = self.xp[t % 3]
            cx.dma_in("sp", xp[:], prev[t * 128:(t + 1) * 128, :], xp.b)

    def _split(self, src, sb, hi, hib, lo, lob):
        cx = self.cx
        cx.op("dve", lambda e: e.tensor_copy(hi, src), r=[sb], w=[hib])
        cx.op("dve", lambda e: e.tensor_tensor(out=lo, in0=src, in1=hi, op=ALU.subtract), r=[sb, hib], w=[lob])

    def _rsqrt(self, dst, a, n, rbufs, wbufs):
        cx = self.cx
        yi = self.rs_i[:, 0:n]
        y = yi.bitcast(F32)
        u = self.rs_u[:, 0:n]
        sb = self.rs_i.b
        cx.op("dve", lambda e: e.tensor_scalar(yi, a.bitcast(I32), 1, None, ALU.arith_shift_right),
              r=rbufs, w=[sb])
        cx.op("dve", lambda e: e.tensor_scalar(yi, yi, -1.0, float(0x5f3759df), ALU.mult, ALU.add),
              r=[sb], w=[sb])
        NIT = 2
        for it in range(NIT):
            cx.op("dve", lambda e: e.scalar_tensor_tensor(out=u, in0=y, scalar=-0.5, in1=y,
                                                          op0=ALU.mult, op1=ALU.mult), r=[sb], w=[self.rs_u.b])
            cx.op("dve", lambda e: e.tensor_tensor(out=u, in0=u, in1=a, op=ALU.mult),
                  r=[self.rs_u.b] + list(rbufs), w=[self.rs_u.b])
            if it < NIT - 1:
                cx.op("dve", lambda e: e.scalar_tensor_tensor(out=y, in0=u, scalar=1.5, in1=y,
                                                              op0=ALU.add, op1=ALU.mult),
                      r=[self.rs_u.b, sb], w=[sb])
            else:
                cx.op("dve", lambda e: e.scalar_tensor_tensor(out=dst, in0=u, scalar=1.5, in1=y,
                                                              op0=ALU.add, op1=ALU.mult),
                      r=[self.rs_u.b, sb], w=wbufs)

    def _rstd(self, xt, col):
        cx = self.cx
        st = self.st
        cx.op("act", lambda e: e.activation(out=self.junk[:], in_=xt[:], func=AF.Square,
                                            accum_out=st[:, col:col + 1]),
              r=[xt.b], w=[self.junk.b, st.b])
        cx.op("dve", lambda e: e.tensor_scalar(st[:, col:col + 1], st[:, col:col + 1], 1.0 / D, EPS,
                                               ALU.mult, ALU.add), r=[st.b], w=[st.b])
        self._rsqrt(st[:, col + 2:col + 3], st[:, col:col + 1], 1, [st.b], [st.b])

    def _prologue(self, t):
        cx = self.cx
        xt = self.xt[t % 3]
        self._rstd(xt, 0)
        cx.op("dve", lambda e: e.scalar_tensor_tensor(
            out=self.tmpA[:], in0=xt[:], scalar=self.st[:, 2:3], in1=self.G_bc[:],
            op0=ALU.mult, op1=ALU.mult), r=[xt.b, self.st.b, self.G_bc.b], w=[self.tmpA.b])
        cx.op("pool", lambda e: e.tensor_tensor(out=self.hb[:], in0=self.tmpA[:], in1=self.shift_bc[:],
                                                op=ALU.add),
              r=[self.tmpA.b, self.shift_bc.b], w=[self.hb.b])
        pt = self.pt()
        cx.mm([lambda e, c=c: e.transpose(pt[:, c * 128:(c + 1) * 128], self.hb[:, c * 128:(c + 1) * 128],
                                          self.ident_b[:]) for c in range(8)],
              r=[self.hb.b, self.ident_b.b], w=[pt.b])
        cx.op("act", lambda e: e.copy(self.hT[:].rearrange("p c t -> p (c t)"), pt[:]),
              r=[pt.b], w=[self.hT.b])

    def _epilogue(self, t, half, dst, is_final, ogT, nchunk, wout, kparts=128, tsl=None, xprev=None):
        cx = self.cx
        if xprev is None:
            xprev = self.xt[t % 3] if half is None else self.xp[t % 3]
        if tsl is None:
            tsl = slice(0, 128)
        xo = self.xo[t % 2]
        for hf in range(2):
            py = self.pf()
            cx.mm([lambda e, c=c, py=py, hf=hf: e.matmul(
                py[:], lhsT=ogT[0:kparts, c, tsl], rhs=wout[0:kparts, c, hf * 512:(hf + 1) * 512],
                start=(c == 0), stop=(c == nchunk - 1)) for c in range(nchunk)],
                r=[ogT.b, wout.b], w=[py.b])
            cx.op("dve", lambda e, py=py, hf=hf: e.tensor_tensor(
                out=self.tmpA[:, hf * 512:(hf + 1) * 512], in0=py[:],
                in1=self.gate_bc[:, hf * 512:(hf + 1) * 512], op=ALU.mult),
                r=[py.b, self.gate_bc.b], w=[self.tmpA.b])
        cx.op("pool", lambda e: e.tensor_tensor(out=xo[:], in0=self.tmpA[:], in1=xprev[:], op=ALU.add),
              r=[self.tmpA.b, xprev.b], w=[xo.b])
        if is_final:
            self._rstd(xo, 1)
            cx.op("dve", lambda e: e.scalar_tensor_tensor(
                out=xo[:], in0=xo[:], scalar=self.st[:, 3:4], in1=self.fg_bc[:],
                op0=ALU.mult, op1=ALU.mult), r=[xo.b, self.st.b, self.fg_bc.b], w=[xo.b])
        cx.dma_out("sp", dst[t * 128:(t + 1) * 128, :], xo[:], xo.b)

    def _ret_alloc(self):
        nc = self.nc
        mk = self.mk
        self.r_cs = [mk(f"r_cs{i}", [128, 2, 128]) for i in range(2)]
        self.rw_in = mk("rw_in", [128, 8, 3072], BF16)
        self.rw_out = mk("rw_out", [128, 8, 1024], BF16)
        self.r_maskT = mk("r_maskT", [128, 4, 128])
        self.r_gq = mk("r_gq", [128, 4, 128])
        self.r_gk = mk("r_gk", [128, 4])
        self.r_S32 = mk("r_S32", [128, 4, 512])
        self.r_Sb = mk("r_Sb", [128, 4, 512], BF16)
        self.r_qkT = mk("r_qkT", [128, 8, 128], BF16)
        self.r_qin = mk("r_qin", [128, 4, 128], BF16)
        self.r_t1 = mk("r_t1", [128, 4, 128])
        self.r_t2 = mk("r_t2", [128, 4, 128])
        self.r_kin = mk("r_kin", [128, 512], BF16)
        self.r_v = mk("r_v", [128, 1024], BF16)
        self.r_sg = mk("r_sg", [128, 1024], BF16)
        self.r_P = mk("r_P", [128, 2, 128], BF16)
        self.r_on = mk("r_on", [128, 1024], BF16)
        self.r_og = mk("r_og", [128, 1024], BF16)
        self.r_ogT = mk("r_ogT", [128, 8, 128], BF16)
        self.r_bn = mk("r_bn", [128, 2, 6])
        self.r_mv = mk("r_mv", [128, 2, 2])
        self.r_nb = mk("r_nb", [128, 2])
        self.r_rs = mk("r_rs", [128, 2])
        cx = self.cx
        dc = self.dconst
        cx.dma_in("sp", self.r_maskT[:], dc["ret_maskT"][:, :, :], self.r_maskT.b)
        cx.dma_in("sp", self.r_gq[:], dc["ret_gq"][:, :, :], self.r_gq.b)
        cx.dma_in("sp", self.r_gk[:], dc["ret_gk"][:, :], self.r_gk.b)

    def _ret_pass(self, j, half, src, prev, dst, is_final):
        nc, cx = self.nc, self.cx
        self._ret_alloc()
        NB = self.NB
        h0 = 2 * half
        win = self.ret_w_in[j].rearrange("(kc p) n -> p kc n", p=128)
        first = True
        for (dst0, src0, n) in ((0, h0 * 256, 512), (512, 1024 + h0 * 256, 512),
                                (1024, 2048 + h0 * 512, 1024), (2048, 4096 + h0 * 512, 1024)):
            for kc in range(8):
                cx.dma_in("pool", self.rw_in[:, kc, dst0:dst0 + n], win[:, kc, src0:src0 + n],
                          self.rw_in.b, cont=not first)
                first = False
        wout = self.ret_w_out[j].rearrange("(ec p) n -> p ec n", p=128)
        for ec in range(8):
            cx.dma_in("pool", self.rw_out[:, ec, :], wout[:, h0 * 4 + ec, :], self.rw_out.b, cont=(ec > 0))
        cx.op("pool", lambda e: e.memset(self.r_S32[:], 0.0), w=[self.r_S32.b])
        cx.op("pool", lambda e: e.memset(self.r_Sb[:], 0.0), w=[self.r_Sb.b])
        sdec = [float(self.cst["ret_sdec"][h0 + i]) for i in range(2)]

        self._load_x(0, src, prev, half)
        self._prologue(0)
        pending = None
        for t in range(NB):
            if t + 1 < NB:
                self._load_x(t + 1, src, prev, half)
            hT = self.hT
            W = self.rw_in
            pq = [self.pf(), self.pf()]
            for qk in range(2):
                fns = []
                for c in range(4):
                    col = qk * 512 + c * 128
                    for kc in range(8):
                        fns.append(lambda e, qk=qk, c=c, col=col, kc=kc: e.matmul(
                            pq[qk][:, c * 128:(c + 1) * 128], lhsT=W[:, kc, col:col + 128],
                            rhs=hT[:, kc, :], start=(kc == 0), stop=(kc == 7)))
                cx.mm(fns, r=[W.b, hT.b], w=[pq[qk].b])
            rcs = self.r_cs[t % 2]
            cx.dma_in("sp", rcs[:], self.rope_d[:, :, t * 128:(t + 1) * 128].rearrange("c p t -> p c t"), rcs.b)
            cs = rcs[:, 0:1, :].to_broadcast([128, 2, 128])
            sn = rcs[:, 1:2, :].to_broadcast([128, 2, 128])
            for qk in range(2):
                pv = pq[qk][:].rearrange("p (h d t) -> p h d t", h=2, d=2)
                x1 = pv[:, :, 0, :]
                x2 = pv[:, :, 1, :]
                t1 = self.r_t1[:, qk * 2:(qk + 1) * 2, :]
                t2 = self.r_t2[:, qk * 2:(qk + 1) * 2, :]
                ov = self.r_qkT[:, qk * 4:(qk + 1) * 4, :].rearrange("p (h d) t -> p h d t", d=2)
                cx.op("dve", lambda e, x1=x1, t1=t1: e.tensor_tensor(out=t1, in0=x1, in1=cs, op=ALU.mult),
                      r=[pq[qk].b, rcs.b], w=[self.r_t1.b])
                cx.op("dve", lambda e, x2=x2, t2=t2: e.tensor_tensor(out=t2, in0=x2, in1=sn, op=ALU.mult),
                      r=[pq[qk].b, rcs.b], w=[self.r_t2.b])
                cx.op("pool", lambda e, t1=t1, t2=t2, ov=ov: e.tensor_tensor(
                    out=ov[:, :, 0, :], in0=t1, in1=t2, op=ALU.subtract),
                    r=[self.r_t1.b, self.r_t2.b], w=[self.r_qkT.b])
                cx.op("dve", lambda e, x1=x1, t1=t1: e.tensor_tensor(out=t1, in0=x1, in1=sn, op=ALU.mult),
                      r=[pq[qk].b, rcs.b], w=[self.r_t1.b])
                cx.op("dve", lambda e, x2=x2, t2=t2: e.tensor_tensor(out=t2, in0=x2, in1=cs, op=ALU.mult),
                      r=[pq[qk].b, rcs.b], w=[self.r_t2.b])
                cx.op("pool", lambda e, t1=t1, t2=t2, ov=ov: e.tensor_tensor(
                    out=ov[:, :, 1, :], in0=t1, in1=t2, op=ALU.add),
                    r=[self.r_t1.b, self.r_t2.b], w=[self.r_qkT.b])
            qkT = self.r_qkT
            gq = self.r_gq[:, h0:h0 + 2, :].unsqueeze(2).to_broadcast([128, 2, 2, 128])
            cx.op("dve", lambda e: e.tensor_tensor(
                out=self.r_qin[:].rearrange("p (h d) t -> p h d t", d=2),
                in0=qkT[:, 0:4, :].rearrange("p (h d) t -> p h d t", d=2), in1=gq, op=ALU.mult),
                r=[qkT.b, self.r_gq.b], w=[self.r_qin.b])
            ps = self.pf()
            fns = []
            for h in range(2):
                for d in range(2):
                    fns.append(lambda e, h=h, d=d: e.matmul(
                        ps[:, h * 128:(h + 1) * 128], lhsT=qkT[:, 4 + h * 2 + d, :], rhs=qkT[:, h * 2 + d, :],
                        start=(d == 0), stop=(d == 1)))
            cx.mm(fns, r=[qkT.b], w=[ps.b])
            cx.op("dve", lambda e: e.tensor_tensor(
                out=self.r_P[:], in0=ps[:, 0:256].rearrange("p (h c) -> p h c", h=2),
                in1=self.r_maskT[:, h0:h0 + 2, :], op=ALU.mult),
                r=[ps.b, self.r_maskT.b], w=[self.r_P.b])
            pk = self.pt()
            cx.mm([lambda e, c=c: e.transpose(pk[:, c * 128:(c + 1) * 128], qkT[:, 4 + c, :], self.ident_b[:])
                   for c in range(4)], r=[qkT.b, self.ident_b.b], w=[pk.b])
            for h in range(2):
                cx.op("act", lambda e, h=h: e.activation(
                    out=self.r_kin[:, h * 256:(h + 1) * 256], in_=pk[:, h * 256:(h + 1) * 256],
                    func=AF.Copy, scale=self.r_gk[:, h0 + h:h0 + h + 1]),
                    r=[pk.b, self.r_gk.b], w=[self.r_kin.b])
            for vi in range(2):
                pvv = self.pf()
                cx.mm([lambda e, kc=kc, vi=vi, pvv=pvv: e.matmul(
                    pvv[:], lhsT=hT[:, kc, :], rhs=W[:, kc, 1024 + vi * 512:1024 + (vi + 1) * 512],
                    start=(kc == 0), stop=(kc == 7)) for kc in range(8)], r=[W.b, hT.b], w=[pvv.b])
                cx.op("act", lambda e, vi=vi, pvv=pvv: e.copy(self.r_v[:, vi * 512:(vi + 1) * 512], pvv[:]),
                      r=[pvv.b], w=[self.r_v.b])
            for gi in range(2):
                pg = self.pf()
                cx.mm([lambda e, kc=kc, gi=gi, pg=pg: e.matmul(
                    pg[:], lhsT=hT[:, kc, :], rhs=W[:, kc, 2048 + gi * 512:2048 + (gi + 1) * 512],
                    start=(kc == 0), stop=(kc == 7)) for kc in range(8)], r=[W.b, hT.b], w=[pg.b])
                cx.op("act", lambda e, gi=gi, pg=pg: e.activation(
                    out=self.r_sg[:, gi * 512:(gi + 1) * 512], in_=pg[:], func=AF.Silu),
                    r=[pg.b], w=[self.r_sg.b])
            if pending is not None:
                pending()
                pending = None
            if t + 1 < NB:
                self._prologue(t + 1)
            pos = []
            for h in range(2):
                po = self.pf()
                pos.append(po)
                fns = [lambda e, h=h, po=po: e.matmul(po[:], lhsT=self.r_P[:, h, :],
                                                      rhs=self.r_v[:, h * 512:(h + 1) * 512],
                                                      start=True, stop=False)]
                for d in range(2):
                    fns.append(lambda e, h=h, d=d, po=po: e.matmul(
                        po[:], lhsT=self.r_qin[:, h * 2 + d, :], rhs=self.r_Sb[:, h * 2 + d, :],
                        start=False, stop=(d == 1)))
                cx.mm(fns, r=[self.r_P.b, self.r_v.b, self.r_qin.b, self.r_Sb.b], w=[po.b])
                cx.op("dve", lambda e, h=h, po=po: e.bn_stats(self.r_bn[:, h, :], po[:]),
                      r=[po.b], w=[self.r_bn.b])
                cx.op("dve", lambda e, h=h: e.bn_aggr(self.r_mv[:, h, :], self.r_bn[:, h, :]),
                      r=[self.r_bn.b], w=[self.r_mv.b])
            cx.op("dve", lambda e: e.tensor_scalar(self.r_mv[:, :, 1], self.r_mv[:, :, 1], EPS, None, ALU.add),
                  r=[self.r_mv.b], w=[self.r_mv.b])
            self._rsqrt(self.r_rs[:, 0:2], self.r_mv[:, :, 1], 2, [self.r_mv.b], [self.r_rs.b])
            cx.op("dve", lambda e: e.scalar_tensor_tensor(
                out=self.r_nb[:, 0:2], in0=self.r_mv[:, :, 0], scalar=-1.0,
                in1=self.r_rs[:, 0:2], op0=ALU.mult, op1=ALU.mult),
                r=[self.r_mv.b, self.r_rs.b], w=[self.r_nb.b])
            for h in range(2):
                po = pos[h]
                cx.op("act", lambda e, h=h, po=po: e.activation(
                    out=self.r_on[:, h * 512:(h + 1) * 512], in_=po[:], func=AF.Identity,
                    bias=self.r_nb[:, h:h + 1], scale=self.r_rs[:, h:h + 1]),
                    r=[po.b, self.r_nb.b, self.r_rs.b], w=[self.r_on.b])
            cx.op("pool", lambda e: e.tensor_tensor(out=self.r_og[:], in0=self.r_on[:], in1=self.r_sg[:],
                                                    op=ALU.mult),
                  r=[self.r_on.b, self.r_sg.b], w=[self.r_og.b])
            for h in range(2):
                for d in range(2):
                    pd = self.pf()
                    cx.mm([lambda e, h=h, d=d, pd=pd: e.matmul(
                        pd[:], lhsT=self.r_kin[:, h * 256 + d * 128:h * 256 + (d + 1) * 128],
                        rhs=self.r_v[:, h * 512:(h + 1) * 512], start=True, stop=True)],
                        r=[self.r_kin.b, self.r_v.b], w=[pd.b])
                    cx.op("dve", lambda e, h=h, d=d, pd=pd: e.scalar_tensor_tensor(
                        out=self.r_S32[:, h * 2 + d, :], in0=self.r_S32[:, h * 2 + d, :], scalar=sdec[h],
                        in1=pd[:], op0=ALU.mult, op1=ALU.add),
                        r=[pd.b, self.r_S32.b], w=[self.r_S32.b])
            cx.op("pool", lambda e: e.tensor_copy(self.r_Sb[:], self.r_S32[:]),
                  r=[self.r_S32.b], w=[self.r_Sb.b])
            def tail(t=t):
                pt = self.pt()
                cx.mm([lambda e, c=c: e.transpose(pt[:, c * 128:(c + 1) * 128], self.r_og[:, c * 128:(c + 1) * 128],
                                                  self.ident_b[:]) for c in range(8)],
                      r=[self.r_og.b, self.ident_b.b], w=[pt.b])
                cx.op("act", lambda e: e.copy(self.r_ogT[:].rearrange("p c t -> p (c t)"), pt[:]),
                      r=[pt.b], w=[self.r_ogT.b])
                self._epilogue(t, prev, dst, is_final, self.r_ogT, 8, self.rw_out)
            pending = tail
        if pending is not None:
            pending()

    def _fox_pass(self, j, p, src, prev, dst, is_final):
        nc, cx = self.nc, self.cx
        mk = self.mk
        NB, S = self.NB, self.S
        HP = 4
        hb = HP * p
        dc = self.dconst
        NSB = NB // 4
        w_in = mk("fw_in", [128, 8, 4 * HP * 64 + HP], BF16)
        w_out = mk("fw_out", [64, HP, 1024], BF16)
        bfb = mk("fbf", [128, HP])
        qg = mk("fqg", [64, 1])
        kg = mk("fkg", [64, 1])
        sfxm = mk("fsfx", [128, 128])
        negm = mk("fnegm", [128, 128], BF16)
        negm_f = mk("fnegmf", [128, 128])
        sel = mk("fsel", [70, 8])
        kTa = mk("fkTa", [70, HP, S], BF16)
        qTa = mk("fqTa", [70, HP, 512], BF16)
        Vp = mk("fVp", [128, NB, HP, 65], BF16)
        sgT = mk("fsgT", [64, HP, 512], BF16)
        ogT = mk("fogT", [64, HP, 512], BF16)
        sq = mk("fsq", [64, 2, HP * 128])
        sqh = mk("fsqh", [64, 2, HP * 128], BF16)
        sql = mk("fsql", [64, 2, HP * 128], BF16)
        lafh = mk("flafh", [128, HP], BF16)
        lafl = mk("flafl", [128, HP], BF16)
        sfxb = mk("fsfxb", [128, 128], BF16)
        rdh = mk("frdh", [65, 512], BF16)
        rdl = mk("frdl", [65, 512], BF16)
        rs = mk("frs", [64, 2, HP * 128])
        zf = mk("fzf", [128, HP])
        laf = mk("flaf", [128, HP])
        wk = mk("fwk", [128, HP])
        Rp = mk("fRp", [70, HP])
        R2 = mk("fR2", [70, 2, 4, HP])
        pcs = [mk(f"fpc{i}", [70, 2, 4, HP]) for i in range(3)]
        pcb = mk("fpcb", [70, 2, 4, HP], BF16)
        dd = mk("fdd", [70, 2, 4, HP])
        selk = mk("fselk", [70, 4, HP])
        selq = mk("fselq", [70, 4, HP])
        cumind = mk("fcumind", [128, 8, 70], BF16)
        Pt = [mk(f"fPt{i}", [128, 512], BF16) for i in range(3)]
        osb = mk("fosb", [65, 512])
        onr = mk("fonr", [64, 512])
        nq = HP * 64

        win = self.fox_w_in[j].rearrange("(kc p) n -> p kc n", p=128)
        first = True
        for gi in range(4):
            for kc in range(8):
                cx.dma_in("pool", w_in[:, kc, gi * nq:(gi + 1) * nq],
                          win[:, kc, gi * 1024 + hb * 64:gi * 1024 + hb * 64 + nq], w_in.b, cont=not first)
                first = False
        for kc in range(8):
            cx.dma_in("pool", w_in[:, kc, 4 * nq:4 * nq + HP], win[:, kc, 4096 + hb:4096 + hb + HP], w_in.b, cont=True)
        wout = self.fox_w_out[j].rearrange("(h d) n -> d h n", d=64)
        for h in range(HP):
            cx.dma_in("pool", w_out[:, h, :], wout[:, hb + h, :], w_out.b, cont=(h > 0))
        cx.dma_in("sp", bfb[:], self.fox_b_f[j:j + 1, hb:hb + HP].partition_broadcast(128), bfb.b)
        cx.dma_in("sp", qg[:], self.fox_qg_col[:, :], qg.b)
        cx.dma_in("sp", kg[:], self.fox_kg_col[:, :], kg.b)
        cx.dma_in("sp", sfxm[:], dc["fox_sfx"][:, :], sfxm.b)
        cx.dma_in("sp", negm_f[:], dc["fox_negmask"][:, :], negm_f.b)
        cx.dma_in("sp", sel[:], dc["fox_sel"][:, :], sel.b)
        cx.dma_in("pool", cumind[:].rearrange("p c m -> p (c m)"), dc["fox_cumind"][:, :], cumind.b)
        cx.op("dve", lambda e: e.tensor_copy(negm[:], negm_f[:]), r=[negm_f.b], w=[negm.b])
        cx.op("dve", lambda e: e.tensor_copy(sfxb[:], sfxm[:]), r=[sfxm.b], w=[sfxb.b])
        cx.op("dve", lambda e: e.tensor_scalar(qg[:], qg[:], 0.125, None, ALU.mult), r=[qg.b], w=[qg.b])
        cx.op("pool", lambda e: e.memset(Rp[:], 0.0), w=[Rp.b])

        self._load_x(0, src, None, None)
        self._prologue(0)
        for J in range(NSB):
            for tb in range(4):
                t = 4 * J + tb
                if t + 1 < NB:
                    self._load_x(t + 1, src, None, None)
                hT = self.hT
                for qi, (off, gcol, dstT, tok0) in enumerate(((0, qg, qTa, tb * 128), (nq, kg, kTa, t * 128))):
                    pp = self.pf()
                    fns = []
                    for h in range(HP):
                        for kc in range(8):
                            fns.append(lambda e, pp=pp, off=off, h=h, kc=kc: e.matmul(
                                pp[0:64, h * 128:(h + 1) * 128], lhsT=w_in[:, kc, off + h * 64:off + (h + 1) * 64],
                                rhs=hT[:, kc, :], start=(kc == 0), stop=(kc == 7)))
                    cx.mm(fns, r=[w_in.b, hT.b], w=[pp.b])
                    cx.op("act", lambda e, qi=qi, pp=pp: e.activation(out=sq[:, qi, :], in_=pp[0:64, :],
                                                                      func=AF.Square), r=[pp.b], w=[sq.b])
                    self._split(sq[:, qi, :], sq.b, sqh[:, qi, :], sqh.b, sql[:, qi, :], sql.b)
                    ps_ = self.pf()
                    cx.mm([lambda e, qi=qi, ps_=ps_: e.matmul(ps_[0:64, :], lhsT=self.ones_b[0:64, 0:64],
                                                            rhs=sqh[:, qi, :], start=True, stop=False),
                           lambda e, qi=qi, ps_=ps_: e.matmul(ps_[0:64, :], lhsT=self.ones_b[0:64, 0:64],
                                                            rhs=sql[:, qi, :], start=False, stop=True)],
                          r=[sqh.b, sql.b, self.ones_b.b], w=[ps_.b])
                    cx.op("dve", lambda e, qi=qi, ps_=ps_: e.tensor_scalar(
                        rs[:, qi, :], ps_[0:64, :], 1.0 / 64.0, EPS, ALU.mult, ALU.add), r=[ps_.b], w=[rs.b])
                    cx.op("act", lambda e, qi=qi: e.activation(out=rs[:, qi, :], in_=rs[:, qi, :], func=AF.Ln),
                          r=[rs.b], w=[rs.b])
                    cx.op("act", lambda e, qi=qi: e.activation(out=rs[:, qi, :], in_=rs[:, qi, :], func=AF.Exp,
                                                               scale=-0.5), r=[rs.b], w=[rs.b])
                    cx.op("dve", lambda e, qi=qi, pp=pp, gcol=gcol, dstT=dstT, tok0=tok0: e.scalar_tensor_tensor(
                        out=dstT[0:64, :, tok0:tok0 + 128], in0=pp[0:64, :].rearrange("p (h t) -> p h t", h=HP),
                        scalar=gcol[:, 0:1], in1=rs[:, qi, :].rearrange("p (h t) -> p h t", h=HP),
                        op0=ALU.mult, op1=ALU.mult), r=[pp.b, gcol.b, rs.b], w=[dstT.b])
                pg = self.pf()
                fns = []
                for h in range(HP):
                    for kc in range(8):
                        fns.append(lambda e, h=h, kc=kc: e.matmul(
                            pg[0:64, h * 128:(h + 1) * 128], lhsT=w_in[:, kc, 3 * nq + h * 64:3 * nq + (h + 1) * 64],
                            rhs=hT[:, kc, :], start=(kc == 0), stop=(kc == 7)))
                cx.mm(fns, r=[w_in.b, hT.b], w=[pg.b])
                cx.op("act", lambda e: e.activation(out=sgT[:, :, tb * 128:(tb + 1) * 128],
                                                    in_=pg[0:64, :].rearrange("p (h t) -> p h t", h=HP),
                                                    func=AF.Silu), r=[pg.b], w=[sgT.b])
                pv = self.pf()
                cx.mm([lambda e, kc=kc: e.matmul(pv[:, 0:nq], lhsT=hT[:, kc, :], rhs=w_in[:, kc, 2 * nq:3 * nq],
                                                 start=(kc == 0), stop=(kc == 7)) for kc in range(8)],
                      r=[w_in.b, hT.b], w=[pv.b])
                pff = self.pf()
                cx.mm([lambda e, kc=kc: e.matmul(pff[:, 0:HP], lhsT=hT[:, kc, :], rhs=w_in[:, kc, 4 * nq:4 * nq + HP],
                                                 start=(kc == 0), stop=(kc == 7)) for kc in range(8)],
                      r=[w_in.b, hT.b], w=[pff.b])
                if t + 1 < NB:
                    self._prologue(t + 1)
                cx.op("dve", lambda e: e.tensor_tensor(out=zf[:], in0=pff[:, 0:HP], in1=bfb[:], op=ALU.add),
                      r=[pff.b, bfb.b], w=[zf.b])
                cx.op("act", lambda e: e.activation(out=zf[:], in_=zf[:], func=AF.Exp, scale=-1.0),
                      r=[zf.b], w=[zf.b])
                cx.op("dve", lambda e: e.tensor_scalar(zf[:], zf[:], 1.0, None, ALU.add), r=[zf.b], w=[zf.b])
                cx.op("act", lambda e: e.activation(out=laf[:], in_=zf[:], func=AF.Ln), r=[zf.b], w=[laf.b])
                psx = self.pf()
                self._split(laf[:], laf.b, lafh[:], lafh.b, lafl[:], lafl.b)
                fns = [lambda e: e.matmul(psx[:, 0:HP], lhsT=sfxb[:], rhs=lafh[:], start=True, stop=False),
                       lambda e: e.matmul(psx[:, 0:HP], lhsT=sfxb[:], rhs=lafl[:], start=False, stop=True)]
                for c8 in range(8):
                    o0 = 8 + c8 * HP
                    fns.append(lambda e, c8=c8, o0=o0: e.matmul(psx[0:70, o0:o0 + HP], lhsT=cumind[:, c8, :],
                                                                rhs=lafh[:], start=True, stop=False))
                    fns.append(lambda e, c8=c8, o0=o0: e.matmul(psx[0:70, o0:o0 + HP], lhsT=cumind[:, c8, :],
                                                                rhs=lafl[:], start=False, stop=True))
                cx.mm(fns, r=[sfxb.b, lafh.b, lafl.b, cumind.b], w=[psx.b])
                cx.op("act", lambda e: e.activation(out=wk[:], in_=psx[:, 0:HP], func=AF.Exp, scale=-1.0),
                      r=[psx.b], w=[wk.b])
                cx.op("dve", lambda e: e.tensor_tensor(
                    out=Vp[:, t, :, 0:64], in0=pv[:, 0:nq].rearrange("p (h d) -> p h d", h=HP),
                    in1=wk[:].unsqueeze(2).to_broadcast([128, HP, 64]), op=ALU.mult),
                    r=[pv.b, wk.b], w=[Vp.b])
                cx.op("dve", lambda e: e.tensor_copy(Vp[:, t, :, 64], wk[:]), r=[wk.b], w=[Vp.b])
                cx.op("dve", lambda e: e.tensor_tensor(
                    out=R2[:].rearrange("p s c h -> p (s c) h"),
                    in0=psx[0:70, 8:8 + 8 * HP].rearrange("p (c h) -> p c h", h=HP),
                    in1=Rp[:].unsqueeze(1).to_broadcast([70, 8, HP]), op=ALU.add),
                    r=[psx.b, Rp.b], w=[R2.b])
                cx.op("dve", lambda e: e.tensor_copy(Rp[:], R2[:, 0, 3, :]), r=[R2.b], w=[Rp.b])
                cur = R2
                for i in range(3):
                    cx.op("dve", lambda e, cur=cur: e.tensor_copy(pcb[:], cur[:]), r=[cur.b], w=[pcb.b])
                    cx.op("dve", lambda e, i=i: e.tensor_copy(pcs[i][:], pcb[:]), r=[pcb.b], w=[pcs[i].b])
                    if i < 2:
                        cx.op("dve", lambda e, cur=cur, i=i: e.tensor_tensor(out=dd[:], in0=cur[:], in1=pcs[i][:],
                                                                             op=ALU.subtract),
                              r=[cur.b, pcs[i].b], w=[dd.b])
                        cur = dd
                self._fox_sel(selk, pcs, 0, sel, 0)
                self._fox_sel(selq, pcs, 1, sel, 4)
                cx.op("dve", lambda e: e.tensor_copy(
                    kTa[64:70, :, t * 128:(t + 1) * 128].rearrange("p h (c u) -> p h c u", u=32),
                    selk[64:70, :, :].rearrange("p c h -> p h c").unsqueeze(3).to_broadcast([6, HP, 4, 32])),
                    r=[selk.b], w=[kTa.b])
                cx.op("dve", lambda e: e.tensor_copy(
                    qTa[64:70, :, tb * 128:(tb + 1) * 128].rearrange("p h (c u) -> p h c u", u=32),
                    selq[64:70, :, :].rearrange("p c h -> p h c").unsqueeze(3).to_broadcast([6, HP, 4, 32])),
                    r=[selq.b], w=[qTa.b])
            if J == 0 and p == 0:
                self.dump("qTa", qTa[0:70, 0, :], qTa.b, 70, 512)
                self.dump("kTa", kTa[0:70, 0, 0:512], kTa.b, 70, 512)
                self.dump("Vp", Vp[:, 0:4, 0, :], Vp.b, 128, 260)
                self.dump("sgT", sgT[:, 0, :], sgT.b, 64, 512)
            nkb = 4 * J + 4
            for h in range(HP):
                acc = self.pacc[h % 2]

                def emit_qk(i, h=h):
                    a = i - 4 * J
                    c0 = max(a, 0) * 128
                    ncol = 512 - c0
                    ps = self.pf()
                    fns = [lambda e, ps=ps, i=i, c0=c0, ncol=ncol, a=a: e.matmul(
                        ps[:, 0:ncol], lhsT=kTa[0:70, h, i * 128:(i + 1) * 128], rhs=qTa[0:70, h, c0:512],
                        start=True, stop=(a < 0))]
                    if a >= 0:
                        fns.append(lambda e, ps=ps: e.matmul(ps[:, 0:128], lhsT=self.ident_b[:], rhs=negm[:],
                                                             start=False, stop=True))
                    cx.mm(fns, r=[kTa.b, qTa.b, self.ident_b.b, negm.b], w=[ps.b])
                    return ps, c0, ncol

                nxt = emit_qk(0)
                for i in range(nkb):
                    ps, c0, ncol = nxt
                    if i + 1 < nkb:
                        nxt = emit_qk(i + 1)
                    pt_ = Pt[self._pti % 3]
                    self._pti += 1
                    cx.op("act", lambda e, ps=ps, pt_=pt_, ncol=ncol: e.activation(
                        out=pt_[:, 0:ncol], in_=ps[:, 0:ncol], func=AF.Exp), r=[ps.b], w=[pt_.b])
                    cx.mm([lambda e, pt_=pt_, i=i, c0=c0, ncol=ncol: e.matmul(
                        acc[0:65, c0:512], lhsT=Vp[:, i, h, :], rhs=pt_[:, 0:ncol],
                        start=(i == 0), stop=(i == nkb - 1))], r=[Vp.b, pt_.b], w=[acc.b])
                cx.op("act", lambda e, acc=acc: e.copy(osb[:], acc[0:65, :]), r=[acc.b], w=[osb.b])
                if J == 0 and p == 0 and h == 0:
                    self.dump("osb", osb[:, :], osb.b, 65, 512)
                cx.op("dve", lambda e: e.reciprocal(osb[64:65, :], osb[64:65, :]), r=[osb.b], w=[osb.b])
                pbc = self.pf()
                self._split(osb[64:65, :], osb.b, rdh[64:65, :], rdh.b, rdl[64:65, :], rdl.b)
                cx.mm([lambda e: e.matmul(pbc[0:64, :], lhsT=self.ones_b[64:65, 0:64], rhs=rdh[64:65, :],
                                          start=True, stop=False),
                       lambda e: e.matmul(pbc[0:64, :], lhsT=self.ones_b[64:65, 0:64], rhs=rdl[64:65, :],
                                          start=False, stop=True)], r=[rdh.b, rdl.b, self.ones_b.b], w=[pbc.b])
                cx.op("dve", lambda e: e.tensor_tensor(out=onr[:], in0=osb[0:64, :], in1=pbc[0:64, :], op=ALU.mult),
                      r=[osb.b, pbc.b], w=[onr.b])
                if J == 0 and p == 0 and h == 0:
                    self.dump("onr", onr[:, :], onr.b, 64, 512)
                cx.op("pool", lambda e, h=h: e.tensor_tensor(out=ogT[:, h, :], in0=onr[:], in1=sgT[:, h, :],
                                                             op=ALU.mult), r=[onr.b, sgT.b], w=[ogT.b])
            for tb in range(4):
                t = 4 * J + tb
                xq = self.xp[t % 3]
                cx.dma_in("sp", xq[:], (src if prev is None else prev)[t * 128:(t + 1) * 128, :], xq.b)
                self._epilogue(t, prev, dst, is_final, ogT, HP, w_out, kparts=64,
                               tsl=slice(tb * 128, (tb + 1) * 128), xprev=xq)

    def _fox_sel(self, out, pcs, side, sel, c0):
        cx = self.cx
        cx.op("dve", lambda e: e.tensor_scalar(out[:], pcs[0][:, side, :, :], sel[:, c0:c0 + 1], sel[:, c0 + 3:c0 + 4],
                                               ALU.mult, ALU.add), r=[pcs[0].b, sel.b], w=[out.b])
        for i in (1, 2):
            cx.op("dve", lambda e, i=i: e.scalar_tensor_tensor(
                out=out[:], in0=pcs[i][:, side, :, :], scalar=sel[:, c0 + i:c0 + i + 1],
                in1=out[:], op0=ALU.mult, op1=ALU.add),
                r=[pcs[i].b, sel.b, out.b], w=[out.b])

    def _gla_pass(self, j, half, src, prev, dst, is_final):
        nc, cx = self.nc, self.cx
        mk = self.mk
        NB = self.NB
        h0 = 2 * half
        dc = self.dconst
        w_in = mk("gw_in", [128, 8, 1552], BF16)
        w_out = mk("gw_out", [128, 4, 1024], BF16)
        w2 = mk("gw2", [128, 256], BF16)
        bg = mk("gbg", [128, 256])
        tri = mk("gtri", [128, 128])
        trib = mk("gtrib", [128, 128], BF16)
        lah = mk("glah", [128, 256], BF16)
        lal = mk("glal", [128, 256], BF16)
        gmask = mk("gmask", [128, 2, 128])
        S32 = mk("gS32", [128, 2, 256])
        Sb = mk("gSb", [128, 2, 256], BF16)
        rT = mk("grT", [128, 128], BF16)
        e1 = mk("ge1", [128, 256])
        la = mk("gla", [128, 256])
        ebT = mk("gebT", [128, 2, 128])
        enbT = mk("genbT", [128, 2, 128])
        kuf = mk("gkuf", [128, 2, 128])
        cbl = mk("gcbl", [128, 2])
        dec = mk("gdec", [128, 2])
        qeT = mk("gqeT", [128, 2, 128], BF16)
        qnT = mk("gqnT", [128, 2, 128], BF16)
        keT = mk("gkeT", [128, 2, 128], BF16)
        kpT = mk("gkpT", [128, 2, 128], BF16)
        kuT = mk("gkuT", [128, 2, 128], BF16)
        ku = mk("gku", [128, 256], BF16)
        pm = mk("gpm", [128, 2, 2, 128])
        P = mk("gP", [128, 2, 128], BF16)
        v = mk("gv", [128, 512], BF16)
        sg = mk("gsg", [128, 512], BF16)
        ssq = mk("gssq", [128, 4])
        on = mk("gon", [128, 512], BF16)
        og = mk("gog", [128, 512], BF16)
        ogT = mk("gogT", [128, 4, 128], BF16)
        scale = 128.0 ** -0.5

        win = self.gla_w_in[j].rearrange("(kc p) n -> p kc n", p=128)
        first = True
        for (d0, s0, n) in ((0, h0 * 128, 256), (256, 512 + h0 * 128, 256), (512, 1024 + h0 * 256, 512),
                            (1024, 2048 + h0 * 256, 512), (1536, 3072, 16)):
            for kc in range(8):
                cx.dma_in("pool", w_in[:, kc, d0:d0 + n], win[:, kc, s0:s0 + n], w_in.b, cont=not first)
                first = False
        wout = self.gla_w_out[j].rearrange("(ec p) n -> p ec n", p=128)
        for ec in range(4):
            cx.dma_in("pool", w_out[:, ec, :], wout[:, h0 * 2 + ec, :], w_out.b, cont=(ec > 0))
        cx.op("pool", lambda e: e.memset(w2[:], 0.0), w=[w2.b])
        cx.op("pool", lambda e: e.memset(rT[:], 0.0), w=[rT.b])
        cx.dma_in("pool", w2[0:16, :], self.gla_w_gate2[j][:, h0 * 128:h0 * 128 + 256], w2.b)
        cx.dma_in("sp", bg[:], self.gla_b_gate[j:j + 1, h0 * 128:h0 * 128 + 256].partition_broadcast(128), bg.b)
        cx.dma_in("sp", tri[:], dc["gla_tri"][:, :], tri.b)
        cx.op("dve", lambda e: e.tensor_copy(trib[:], tri[:]), r=[tri.b], w=[trib.b])
        cx.dma_in("sp", gmask[:], dc["gla_mask"][:, :, :], gmask.b)
        cx.op("pool", lambda e: e.memset(S32[:], 0.0), w=[S32.b])
        cx.op("pool", lambda e: e.memset(Sb[:], 0.0), w=[Sb.b])

        self._load_x(0, src, prev, half)
        self._prologue(0)
        pending = None
        for t in range(NB):
            if t + 1 < NB:
                self._load_x(t + 1, src, prev, half)
            hT = self.hT
            for _once in (0,):
                pqk = self.pf()
                fns = []
                for c in range(4):
                    for kc in range(8):
                        fns.append(lambda e, c=c, kc=kc: e.matmul(
                            pqk[:, c * 128:(c + 1) * 128], lhsT=w_in[:, kc, c * 128:(c + 1) * 128],
                            rhs=hT[:, kc, :], start=(kc == 0), stop=(kc == 7)))
                cx.mm(fns, r=[w_in.b, hT.b], w=[pqk.b])
                if self.dbg_stage < 1:
                    break
                pr = self.pf()
                cx.mm([lambda e, kc=kc: e.matmul(pr[0:16, 0:128], lhsT=w_in[:, kc, 1536:1552], rhs=hT[:, kc, :],
                                                 start=(kc == 0), stop=(kc == 7)) for kc in range(8)],
                      r=[w_in.b, hT.b], w=[pr.b])
                cx.op("dve", lambda e: e.tensor_copy(rT[0:16, :], pr[0:16, 0:128]), r=[pr.b], w=[rT.b])
                if self.dbg_stage < 2:
                    break
                pz = self.pf()
                import os as _os
                if _os.environ.get("DBG_VAR", "") == "A":
                    cx.mm([lambda e: e.matmul(pz[:, 0:256], lhsT=hT[:, 0, :], rhs=w_in[:, 0, 0:256], start=True, stop=True)],
                          r=[hT.b, w_in.b], w=[pz.b])
                else:
                    cx.mm([lambda e: e.matmul(pz[:, 0:256], lhsT=rT[:], rhs=w2[:], start=True, stop=True)],
                          r=[rT.b, w2.b], w=[pz.b])
                if _os.environ.get("DBG_VAR", "") == "B":
                    cx.op("dve", lambda e: e.tensor_tensor(out=e1[:], in0=pz[:, 0:256], in1=self.G_bc[:, 0:256], op=ALU.add),
                          r=[pz.b, self.G_bc.b], w=[e1.b])
                else:
                    cx.op("dve", lambda e: e.tensor_tensor(out=e1[:], in0=pz[:, 0:256], in1=bg[:], op=ALU.add),
                          r=[pz.b, bg.b], w=[e1.b])
                cx.op("act", lambda e: e.activation(out=e1[:], in_=e1[:], func=AF.Exp, scale=-1.0),
                      r=[e1.b], w=[e1.b])
                cx.op("dve", lambda e: e.tensor_scalar(e1[:], e1[:], 1.0, None, ALU.add), r=[e1.b], w=[e1.b])
                cx.op("act", lambda e: e.activation(out=la[:], in_=e1[:], func=AF.Ln), r=[e1.b], w=[la.b])
                if self.dbg_stage < 3:
                    break
                pcb = self.pf()
                self._split(la[:], la.b, lah[:], lah.b, lal[:], lal.b)
                fns = []
                for h in range(2):
                    fns.append(lambda e, h=h: e.matmul(pcb[:, h * 128:(h + 1) * 128], lhsT=lah[:, h * 128:(h + 1) * 128],
                                                       rhs=trib[:], start=True, stop=False))
                    fns.append(lambda e, h=h: e.matmul(pcb[:, h * 128:(h + 1) * 128], lhsT=lal[:, h * 128:(h + 1) * 128],
                                                       rhs=trib[:], start=False, stop=True))
                cx.mm(fns, r=[lah.b, lal.b, trib.b], w=[pcb.b])
                pcv = pcb[:, 0:256].rearrange("p (h t) -> p h t", h=2)
                cx.op("act", lambda e: e.activation(out=ebT[:], in_=pcv, func=AF.Exp), r=[pcb.b], w=[ebT.b])
                cx.op("act", lambda e: e.activation(out=enbT[:], in_=pcv, func=AF.Exp, scale=-1.0),
                      r=[pcb.b], w=[enbT.b])
                cx.op("dve", lambda e: e.tensor_copy(cbl[:], pcv[:, :, 127]), r=[pcb.b], w=[cbl.b])
                cx.op("act", lambda e: e.activation(out=dec[:], in_=cbl[:], func=AF.Exp), r=[cbl.b], w=[dec.b])
                for h in range(2):
                    cx.op("act", lambda e, h=h: e.activation(out=kuf[:, h, :], in_=pcv[:, h, :], func=AF.Exp,
                                                             scale=-1.0, bias=cbl[:, h:h + 1]),
                          r=[pcb.b, cbl.b], w=[kuf.b])
                if self.dbg_stage < 4:
                    break
                qv = pqk[:, 0:256].rearrange("p (h t) -> p h t", h=2)
                kv = pqk[:, 256:512].rearrange("p (h t) -> p h t", h=2)
                cx.op("dve", lambda e: e.scalar_tensor_tensor(out=qeT[:], in0=qv, scalar=scale, in1=ebT[:],
                                                              op0=ALU.mult, op1=ALU.mult),
                      r=[pqk.b, ebT.b], w=[qeT.b])
                cx.op("dve", lambda e: e.scalar_tensor_tensor(out=qnT[:], in0=qv, scalar=scale, in1=enbT[:],
                                                              op0=ALU.mult, op1=ALU.mult),
                      r=[pqk.b, enbT.b], w=[qnT.b])
                cx.op("dve", lambda e: e.tensor_tensor(out=keT[:], in0=kv, in1=enbT[:], op=ALU.mult),
                      r=[pqk.b, enbT.b], w=[keT.b])
                cx.op("dve", lambda e: e.tensor_tensor(out=kpT[:], in0=kv, in1=ebT[:], op=ALU.mult),
                      r=[pqk.b, ebT.b], w=[kpT.b])
                cx.op("dve", lambda e: e.tensor_tensor(out=kuT[:], in0=kv, in1=kuf[:], op=ALU.mult),
                      r=[pqk.b, kuf.b], w=[kuT.b])
                if self.dbg_stage < 5:
                    break
                pa = self.pf()
                fns = []
                for h in range(2):
                    fns.append(lambda e, h=h: e.matmul(pa[:, h * 128:(h + 1) * 128], lhsT=keT[:, h, :],
                                                       rhs=qeT[:, h, :], start=True, stop=True))
                    fns.append(lambda e, h=h: e.matmul(pa[:, 256 + h * 128:256 + (h + 1) * 128], lhsT=kpT[:, h, :],
                                                       rhs=qnT[:, h, :], start=True, stop=True))
                cx.mm(fns, r=[keT.b, qeT.b, kpT.b, qnT.b], w=[pa.b])
                cx.op("dve", lambda e: e.tensor_tensor(
                    out=pm[:], in0=pa[:].rearrange("p (y h c) -> p y h c", y=2, h=2),
                    in1=gmask[:].unsqueeze(2).to_broadcast([128, 2, 2, 128]), op=ALU.mult),
                    r=[pa.b, gmask.b], w=[pm.b])
                cx.op("pool", lambda e: e.tensor_tensor(out=P[:], in0=pm[:, 0, :, :], in1=pm[:, 1, :, :], op=ALU.add),
                      r=[pm.b], w=[P.b])
                if self.dbg_stage < 6:
                    break
                pv = self.pf()
                cx.mm([lambda e, kc=kc: e.matmul(pv[:], lhsT=hT[:, kc, :], rhs=w_in[:, kc, 512:1024],
                                                 start=(kc == 0), stop=(kc == 7)) for kc in range(8)],
                      r=[w_in.b, hT.b], w=[pv.b])
                cx.op("act", lambda e: e.copy(v[:], pv[:]), r=[pv.b], w=[v.b])
                pg = self.pf()
                cx.mm([lambda e, kc=kc: e.matmul(pg[:], lhsT=hT[:, kc, :], rhs=w_in[:, kc, 1024:1536],
                                                 start=(kc == 0), stop=(kc == 7)) for kc in range(8)],
                      r=[w_in.b, hT.b], w=[pg.b])
                cx.op("act", lambda e: e.activation(out=sg[:], in_=pg[:], func=AF.Silu), r=[pg.b], w=[sg.b])
                if pending is not None:
                    pending()
                    pending = None
                if t + 1 < NB:
                    self._prologue(t + 1)
                if self.dbg_stage < 7:
                    break
                po = self.pf()
                fns = []
                for h in range(2):
                    fns.append(lambda e, h=h: e.matmul(po[:, h * 256:(h + 1) * 256], lhsT=P[:, h, :],
                                                       rhs=v[:, h * 256:(h + 1) * 256], start=True, stop=False))
                    fns.append(lambda e, h=h: e.matmul(po[:, h * 256:(h + 1) * 256], lhsT=qeT[:, h, :],
                                                       rhs=Sb[:, h, :], start=False, stop=True))
                cx.mm(fns, r=[P.b, v.b, qeT.b, Sb.b], w=[po.b])
                if self.dbg_stage < 8:
                    break
                for h in range(2):
                    cx.op("act", lambda e, h=h: e.activation(out=self.junk[:, 0:256], in_=po[:, h * 256:(h + 1) * 256],
                                                             func=AF.Square, accum_out=ssq[:, h:h + 1]),
                          r=[po.b], w=[self.junk.b, ssq.b])
                cx.op("dve", lambda e: e.tensor_scalar(ssq[:, 0:2], ssq[:, 0:2], 1.0 / 256.0, EPS, ALU.mult, ALU.add),
                      r=[ssq.b], w=[ssq.b])
                self._rsqrt(ssq[:, 2:4], ssq[:, 0:2], 2, [ssq.b], [ssq.b])
                for h in range(2):
                    cx.op("act", lambda e, h=h: e.activation(out=on[:, h * 256:(h + 1) * 256],
                                                             in_=po[:, h * 256:(h + 1) * 256], func=AF.Copy,
                                                             scale=ssq[:, 2 + h:3 + h]),
                          r=[po.b, ssq.b], w=[on.b])
                cx.op("pool", lambda e: e.tensor_tensor(out=og[:], in0=on[:], in1=sg[:], op=ALU.mult),
                      r=[on.b, sg.b], w=[og.b])
                if self.dbg_stage < 9:
                    break
                pk = self.pt()
                cx.mm([lambda e, h=h: e.transpose(pk[:, h * 128:(h + 1) * 128], kuT[:, h, :], self.ident_b[:])
                       for h in range(2)], r=[kuT.b, self.ident_b.b], w=[pk.b])
                cx.op("act", lambda e: e.copy(ku[:], pk[:, 0:256]), r=[pk.b], w=[ku.b])
                pd = self.pf()
                cx.mm([lambda e, h=h: e.matmul(pd[:, h * 256:(h + 1) * 256], lhsT=ku[:, h * 128:(h + 1) * 128],
                                               rhs=v[:, h * 256:(h + 1) * 256], start=True, stop=True)
                       for h in range(2)], r=[ku.b, v.b], w=[pd.b])
                for h in range(2):
                    cx.op("dve", lambda e, h=h: e.scalar_tensor_tensor(
                        out=S32[:, h, :], in0=S32[:, h, :], scalar=dec[:, h:h + 1], in1=pd[:, h * 256:(h + 1) * 256],
                        op0=ALU.mult, op1=ALU.add), r=[pd.b, S32.b, dec.b], w=[S32.b])
                cx.op("pool", lambda e: e.tensor_copy(Sb[:], S32[:]), r=[S32.b], w=[Sb.b])
                if self.dbg_stage < 10:
                    break
                pass
            def tail(t=t):
                pt = self.pt()
                cx.mm([lambda e, c=c: e.transpose(pt[:, c * 128:(c + 1) * 128], og[:, c * 128:(c + 1) * 128],
                                                  self.ident_b[:]) for c in range(4)],
                      r=[og.b, self.ident_b.b], w=[pt.b])
                cx.op("act", lambda e: e.copy(ogT[:].rearrange("p c t -> p (c t)"), pt[:, 0:512]),
                      r=[pt.b], w=[ogT.b])
                self._epilogue(t, prev, dst, is_final, ogT, 4, w_out)
            pending = tail
        if pending is not None:
            pending()


def _core_inputs(b, inp, cst, S):
    f = lambda a: np.ascontiguousarray(np.asarray(a, dtype=np.float32))
    m = {}
    m["x"] = f(inp["x"][b, :S])
    m["c_col"] = f(np.asarray(inp["c"])[b].reshape(8, 128).T)
    m["posf"] = f(np.asarray(inp["positions"])[b, :S].astype(np.float32).reshape(1, S))
    m["mod_w"] = f(inp["mod_w"])
    m["mod_b_col"] = f(np.asarray(inp["mod_b"]).reshape(4, 24, 128).transpose(2, 0, 1))
    m["norm_g_col"] = f(np.asarray(inp["norm_g"]).reshape(4, 8, 128).transpose(2, 0, 1))
    m["final_g_col"] = f(np.asarray(inp["final_g"]).reshape(8, 128).T)
    for k in ("ret_w_in", "ret_w_out", "fox_w_in", "fox_b_f", "fox_w_out", "gla_w_in",
              "gla_w_gate2", "gla_b_gate", "gla_w_out"):
        m[k] = f(inp[k])
    m["fox_qg_col"] = f(np.asarray(inp["fox_q_gain"]).reshape(64, 1))
    m["fox_kg_col"] = f(np.asarray(inp["fox_k_gain"]).reshape(64, 1))
    for k, v in cst.items():
        if isinstance(v, np.ndarray) and v.dtype == np.float32:
            m["k_" + k] = v
    return m


def run(inputs, S=4096, layers=(0, 1, 2, 3), final=True, trace=False):
    prog = Prog(S, list(layers), final)
    nc = prog.build()
    B = np.asarray(inputs["x"]).shape[0]
    in_maps = [_core_inputs(b % B, inputs, prog.cst, S) for b in range(N_CORES)]
    res = run_bass_kernel_spmd(nc, in_maps, core_ids=list(range(N_CORES)), trace=trace)
    out = np.stack([np.asarray(res.results[b]["out"]) for b in range(B)], 0)
    if prog.dbg_on:
        d = np.asarray(res.results[0]["dbg"])
        prog.dbg_vals = {k: d[0:P, off:off + n] for k, (off, P, n) in prog.dbg_map.items()}
    return out.astype(np.float32), res, prog


def kernel(**inputs):
    out, _, _ = run(inputs)
    return out
```
